# Optimizing a Trainium2 kernel written in Bass

```python
import math
import jax, jax.numpy as jnp
from jax import lax
import numpy as np

D_MODEL = 2048
BATCH = 2
SEQ = 16384
DEPTH = 2

HEAD_DIM = 128
SGU_GROUPS = 8
SGU_CHUNK = 128
SGU_WIDTH = SGU_GROUPS * HEAD_DIM
NSA_HEADS = 8
NSA_KV_GROUPS = 2
NSA_REP = NSA_HEADS // NSA_KV_GROUPS
NSA_WIDTH = NSA_HEADS * HEAD_DIM
KV_WIDTH = NSA_KV_GROUPS * HEAD_DIM
CMP_LEN = 32
CMP_STRIDE = 16
SLC_LEN = 64
SLC_TOP_N = 16
WINDOW = 512
Q_BLOCK = 128
N_BRANCH = 3
FORCE_SCORE = 1e9
EVEN_IN_WIDTH = 2 * SGU_WIDTH + NSA_WIDTH + 6 * KV_WIDTH + N_BRANCH * NSA_HEADS
EVEN_MIX_WIDTH = SGU_WIDTH + NSA_WIDTH
CONV_WIDTH = D_MODEL
CONV_K = 3
D_FF = -(-8 * D_MODEL // (3 * 256)) * 256
N_EVEN = (DEPTH + 1) // 2
N_ODD = DEPTH // 2
EPS = 1e-6
NEG = -1e30

kernel_name = 'hybrid_sgu_nsa_shortconv_trunk'


def rmsnorm(x, g):
    xf = x.astype(jnp.float32)
    y = xf * lax.rsqrt(jnp.mean(xf * xf, axis=-1, keepdims=True) + EPS)
    return (y * g.astype(jnp.float32)).astype(x.dtype)


def masked_softmax(s, mask):
    s = jnp.where(mask, s.astype(jnp.float32), NEG)
    m = jnp.max(s, axis=-1, keepdims=True)
    p = jnp.exp(s - m) * mask
    return p / jnp.maximum(jnp.sum(p, axis=-1, keepdims=True), 1e-30)


def alibi_slopes(n):
    return 2.0 ** (-8.0 * jnp.arange(1, n + 1, dtype=jnp.float32) / n)


def swiglu(h, w_gate, w_up, w_down):
    return (jax.nn.silu(h @ w_gate) * (h @ w_up)) @ w_down


def compress_blocks(kv, pe, w1, w2):
    b, s, g, d = kv.shape
    r = CMP_LEN // CMP_STRIDE
    n_cmp = s // CMP_STRIDE - (r - 1)
    chunks = kv.reshape(b, s // CMP_STRIDE, CMP_STRIDE, g, d)
    blocks = jnp.concatenate([chunks[:, j:j + n_cmp] for j in range(r)], axis=2)
    blocks = blocks + pe[None, None, :, None, :]
    hid = jax.nn.gelu(jnp.einsum('bnlgd,lde->bnge', blocks, w1))
    return jnp.einsum('bnge,ef->bngf', hid, w2)


def selection_importance(p_cmp, n_slc):
    r1 = SLC_LEN // CMP_STRIDE
    r2 = CMP_LEN // CMP_STRIDE
    n_cmp = p_cmp.shape[-1]
    pad = [(0, 0)] * (p_cmp.ndim - 1) + [(r2 - 1, r1 * n_slc - n_cmp)]
    pp = jnp.pad(p_cmp, pad)
    terms = []
    for m in range(r1):
        for n in range(r2):
            st = m - n + r2 - 1
            terms.append(pp[..., st: st + r1 * (n_slc - 1) + 1: r1])
    return jnp.sum(jnp.stack(terms, axis=0), axis=0)


def nsa_attention(q, k_cmp, v_cmp, k_slc, v_slc, k_win, v_win, gates):
    b, s, g, r, d = q.shape
    n_cmp = k_cmp.shape[1]
    n_slc = s // SLC_LEN
    top_n = min(SLC_TOP_N, n_slc)
    scale = d ** -0.5
    slopes = alibi_slopes(g * r).reshape(g, r)
    cmp_idx = jnp.arange(n_cmp)
    cmp_end = cmp_idx * CMP_STRIDE + CMP_LEN - 1
    cmp_mid = (cmp_idx * CMP_STRIDE).astype(jnp.float32) + (CMP_LEN - 1) / 2
    ks_blocks = k_slc.reshape(b, n_slc, SLC_LEN, g, d).transpose(0, 3, 1, 2, 4)
    vs_blocks = v_slc.reshape(b, n_slc, SLC_LEN, g, d).transpose(0, 3, 1, 2, 4)
    kw_pad = jnp.pad(k_win, ((0, 0), (WINDOW, 0), (0, 0), (0, 0)))
    vw_pad = jnp.pad(v_win, ((0, 0), (WINDOW, 0), (0, 0), (0, 0)))
    b_ix = jnp.arange(b)[:, None, None, None]
    g_ix = jnp.arange(g)[None, :, None, None]
    blk_ids = jnp.arange(n_slc)

    def one_block(qb):
        q0 = qb * Q_BLOCK
        t = q0 + jnp.arange(Q_BLOCK)
        tf = t.astype(jnp.float32)
        qblk = lax.dynamic_slice_in_dim(q, q0, Q_BLOCK, axis=1)
        gblk = lax.dynamic_slice_in_dim(gates, q0, Q_BLOCK, axis=1)
        s_c = jnp.einsum('bqgrd,bngd->bgrqn', qblk, k_cmp).astype(jnp.float32) * scale
        s_c = s_c - slopes[None, :, :, None, None] * (tf[:, None] - cmp_mid[None, :])
        p_c = masked_softmax(s_c, cmp_end[None, :] <= t[:, None])
        o_c = jnp.einsum('bgrqn,bngd->bqgrd', p_c.astype(v_cmp.dtype), v_cmp)
        imp = selection_importance(jnp.sum(p_c, axis=2), n_slc)
        cur = t // SLC_LEN
        causal = blk_ids[None, :] <= cur[:, None]
        forced = (blk_ids[None, :] == 0) | (blk_ids[None, :] == cur[:, None]) | (blk_ids[None, :] == cur[:, None] - 1)
        score = jnp.where(forced, FORCE_SCORE, jnp.where(causal, imp, -1.0))
        _, idx = lax.top_k(score, top_n)
        kg = ks_blocks[b_ix, g_ix, idx]
        vg = vs_blocks[b_ix, g_ix, idx]
        key_pos = idx[..., None] * SLC_LEN + jnp.arange(SLC_LEN)
        mask_s = key_pos <= t[None, None, :, None, None]
        dist = tf[None, None, :, None, None] - key_pos.astype(jnp.float32)
        s_s = jnp.einsum('bqgrd,bgqnld->bgrqnl', qblk, kg).astype(jnp.float32) * scale
        s_s = s_s - slopes[None, :, :, None, None, None] * dist[:, :, None]
        p_s = masked_softmax(s_s.reshape(b, g, r, Q_BLOCK, -1), mask_s.reshape(b, g, 1, Q_BLOCK, -1))
        o_s = jnp.einsum('bgrqnl,bgqnld->bqgrd', p_s.reshape(s_s.shape).astype(vg.dtype), vg)
        kw = lax.dynamic_slice_in_dim(kw_pad, q0, Q_BLOCK + WINDOW, axis=1)
        vw = lax.dynamic_slice_in_dim(vw_pad, q0, Q_BLOCK + WINDOW, axis=1)
        s_pos = q0 - WINDOW + jnp.arange(Q_BLOCK + WINDOW)
        rel = t[:, None] - s_pos[None, :]
        mask_w = (rel >= 0) & (rel < WINDOW) & (s_pos[None, :] >= 0)
        s_w = jnp.einsum('bqgrd,bkgd->bgrqk', qblk, kw).astype(jnp.float32) * scale
        s_w = s_w - slopes[None, :, :, None, None] * rel.astype(jnp.float32)
        p_w = masked_softmax(s_w, mask_w)
        o_w = jnp.einsum('bgrqk,bkgd->bqgrd', p_w.astype(vw.dtype), vw)
        return gblk[..., 0:1] * o_c + gblk[..., 1:2] * o_s + gblk[..., 2:3] * o_w

    out = lax.map(one_block, jnp.arange(s // Q_BLOCK))
    return out.transpose(1, 0, 2, 3, 4, 5).reshape(b, s, g * r * d)


def sgu_nsa_mixer(h, w_in, w_out, sgu_w, sgu_b, sgu_g,
                  cmp_pe_k, cmp_w1_k, cmp_w2_k, cmp_pe_v, cmp_w1_v, cmp_w2_v):
    b, s, _ = h.shape
    proj = h @ w_in
    sizes = [SGU_WIDTH, SGU_WIDTH, NSA_WIDTH] + [KV_WIDTH] * 6 + [N_BRANCH * NSA_HEADS]
    splits = np.cumsum(sizes)[:-1].tolist()
    u, v, q, kc, vc, ks, vs, kw, vw, gt = jnp.split(proj, splits, axis=-1)
    u = jax.nn.gelu(u)
    v = rmsnorm(jax.nn.gelu(v).reshape(b, s, SGU_GROUPS, HEAD_DIM), sgu_g)
    w_causal = sgu_w * jnp.tril(jnp.ones((SGU_CHUNK, SGU_CHUNK), sgu_w.dtype))
    v_ch = v.reshape(b, s // SGU_CHUNK, SGU_CHUNK, SGU_GROUPS, HEAD_DIM)
    s_gate = jnp.einsum('gtp,bnpgc->bntgc', w_causal, v_ch) + sgu_b.T[None, None, :, :, None]
    y_a = u * s_gate.reshape(b, s, SGU_WIDTH)
    kv_shape = lambda z: z.reshape(b, s, NSA_KV_GROUPS, HEAD_DIM)
    k_cmp = compress_blocks(kv_shape(kc), cmp_pe_k, cmp_w1_k, cmp_w2_k)
    v_cmp = compress_blocks(kv_shape(vc), cmp_pe_v, cmp_w1_v, cmp_w2_v)
    gates = jax.nn.sigmoid(gt.reshape(b, s, NSA_KV_GROUPS, NSA_REP, N_BRANCH))
    y_b = nsa_attention(q.reshape(b, s, NSA_KV_GROUPS, NSA_REP, HEAD_DIM), k_cmp, v_cmp,
                        kv_shape(ks), kv_shape(vs), kv_shape(kw), kv_shape(vw), gates)
    return jnp.concatenate([y_a, y_b], axis=-1) @ w_out


def short_conv_mixer(h, w_in, conv_w, w_out):
    bg, cg, z = jnp.split(h @ w_in, 3, axis=-1)
    z = cg * z
    zc = lax.conv_general_dilated(z, conv_w[:, None, :], window_strides=(1,),
                                  padding=[(CONV_K - 1, 0)],
                                  dimension_numbers=('NWC', 'WIO', 'NWC'),
                                  feature_group_count=CONV_WIDTH)
    return (bg * zc) @ w_out


def setup_inputs(seed: int = 0) -> dict:
    key = jax.random.key(seed)
    ks = jax.random.split(key, 24)
    nrm = lambda k, shape, fan_in: jax.random.normal(k, shape, jnp.float32) * fan_in ** -0.5
    gain = lambda k, shape: 1.0 + 0.1 * jax.random.normal(k, shape, jnp.float32)
    return {
        'x': jax.random.normal(ks[0], (BATCH, SEQ, D_MODEL), jnp.float32),
        'norm_mix': gain(ks[1], (DEPTH, D_MODEL)),
        'norm_ffn': gain(ks[2], (DEPTH, D_MODEL)),
        'norm_f': gain(ks[3], (D_MODEL,)),
        'w_in_ab': nrm(ks[4], (N_EVEN, D_MODEL, EVEN_IN_WIDTH), D_MODEL),
        'w_out_ab': nrm(ks[5], (N_EVEN, EVEN_MIX_WIDTH, D_MODEL), EVEN_MIX_WIDTH),
        'sgu_w': nrm(ks[6], (N_EVEN, SGU_GROUPS, SGU_CHUNK, SGU_CHUNK), SGU_CHUNK),
        'sgu_b': gain(ks[7], (N_EVEN, SGU_GROUPS, SGU_CHUNK)),
        'sgu_g': gain(ks[8], (N_EVEN, SGU_GROUPS, HEAD_DIM)),
        'cmp_pe_k': 0.1 * jax.random.normal(ks[9], (N_EVEN, CMP_LEN, HEAD_DIM), jnp.float32),
        'cmp_w1_k': nrm(ks[10], (N_EVEN, CMP_LEN, HEAD_DIM, HEAD_DIM), CMP_LEN * HEAD_DIM),
        'cmp_w2_k': nrm(ks[11], (N_EVEN, HEAD_DIM, HEAD_DIM), HEAD_DIM),
        'cmp_pe_v': 0.1 * jax.random.normal(ks[12], (N_EVEN, CMP_LEN, HEAD_DIM), jnp.float32),
        'cmp_w1_v': nrm(ks[13], (N_EVEN, CMP_LEN, HEAD_DIM, HEAD_DIM), CMP_LEN * HEAD_DIM),
        'cmp_w2_v': nrm(ks[14], (N_EVEN, HEAD_DIM, HEAD_DIM), HEAD_DIM),
        'w_in_c': nrm(ks[15], (N_ODD, D_MODEL, 3 * CONV_WIDTH), D_MODEL),
        'conv_w': nrm(ks[16], (N_ODD, CONV_K, CONV_WIDTH), CONV_K),
        'w_out_c': nrm(ks[17], (N_ODD, CONV_WIDTH, D_MODEL), CONV_WIDTH),
        'w_gate': nrm(ks[18], (DEPTH, D_MODEL, D_FF), D_MODEL),
        'w_up': nrm(ks[19], (DEPTH, D_MODEL, D_FF), D_MODEL),
        'w_down': nrm(ks[20], (DEPTH, D_FF, D_MODEL), D_FF),
    }


def reference(x, norm_mix, norm_ffn, norm_f, w_in_ab, w_out_ab, sgu_w, sgu_b, sgu_g,
              cmp_pe_k, cmp_w1_k, cmp_w2_k, cmp_pe_v, cmp_w1_v, cmp_w2_v,
              w_in_c, conv_w, w_out_c, w_gate, w_up, w_down):
    h = x
    for i in range(DEPTH):
        j = i // 2
        hn = rmsnorm(h, norm_mix[i])
        if i % 2 == 0:
            mix = sgu_nsa_mixer(hn, w_in_ab[j], w_out_ab[j], sgu_w[j], sgu_b[j], sgu_g[j],
                                cmp_pe_k[j], cmp_w1_k[j], cmp_w2_k[j],
                                cmp_pe_v[j], cmp_w1_v[j], cmp_w2_v[j])
        else:
            mix = short_conv_mixer(hn, w_in_c[j], conv_w[j], w_out_c[j])
        h = h + mix
        h = h + swiglu(rmsnorm(h, norm_ffn[i]), w_gate[i], w_up[i], w_down[i])
    return rmsnorm(h, norm_f)
```

```python
import numpy as np
import ml_dtypes
import concourse.bass as bass
import concourse.mybir as mybir
from concourse.bass_utils import run_bass_kernel_spmd

F32 = mybir.dt.float32
BF16 = mybir.dt.bfloat16
AF = mybir.ActivationFunctionType
ALU = mybir.AluOpType
AX = mybir.AxisListType
P = 128
D = 2048
KC = 16
DFF = 5632
FC = DFF // 128
EPS = 1e-6
SB_LO = 16512
SB_HI = 229344
SAME_ENG_SYNC = True
NEG = -1.0e30


def _dsz(dt):
    return 4 if dt == F32 else 2


class T:
    def __init__(self, t, space, off, nbytes, dt):
        self.t = t
        self.space = space
        self.off = off
        self.nbytes = nbytes
        self.dt = dt

    def all(self):
        return (self.space, self.off, self.off + self.nbytes)

    def r(self, lo, hi):
        s = _dsz(self.dt)
        return (self.space, self.off + lo * s, self.off + hi * s)

    def __getitem__(self, k):
        return self.t[k]


class Prog:
    def __init__(self):
        self.nc = bass.Bass("TRN2", target_bir_lowering=False)
        nc = self.nc
        self.E = {"pe": nc.tensor, "act": nc.scalar, "dve": nc.vector, "pool": nc.gpsimd, "sp": nc.sync}
        self.sem = {}
        self.cnt = {}
        self.waited = {e: {} for e in self.E}
        self.recs = {}
        self.sb_off = SB_LO
        self.n_ops = 0
        self.ps_banks = [nc.alloc_psum_tensor(f"psb{i}", [P, 512], F32) for i in range(8)]
        self.dram_names = {}
        self.epoch = {}

    def sb(self, name, shape, dt, at=None):
        n = 1
        for s in shape[1:]:
            n *= s
        nbytes = n * _dsz(dt)
        if at is None:
            off = (self.sb_off + 31) // 32 * 32
            self.sb_off = off + nbytes
            assert self.sb_off <= SB_HI, f"SBUF overflow at {name}: {self.sb_off}"
        else:
            off = at
            assert off + nbytes <= SB_HI and off >= SB_LO, f"SBUF overflow (at) {name}"
        t = self.nc.alloc_sbuf_tensor_at(name, list(shape), dt, offset=off)
        return T(t, "sb", off, nbytes, dt)

    def ps(self, bank, shape=None, dt=F32, col0=0):
        return self.ps_banks[bank]

    def psr(self, bank, lo=0, hi=512):
        return ("ps", bank * 2048, bank * 2048 + 2048)

    def dram(self, name, shape, dt, kind):
        t = self.nc.dram_tensor(name, list(shape), dt, kind=kind)
        self.dram_names[name] = kind
        return t.ap()

    def filter_inputs(self, maps):
        names = [n for n, k in self.dram_names.items() if k == "ExternalInput"]
        return [{n: m[n] for n in names} for m in maps]

    def _sem(self, key):
        if key not in self.sem:
            self.sem[key] = self.nc.alloc_semaphore("s_" + key.replace("#", "_e"))
            self.cnt[key] = 0
        return self.sem[key]

    def _deps(self, reads, writes, e=None):
        deps = {}
        for (sp, lo, hi) in reads:
            for rec in self.recs.get(sp, ()):
                if rec[4] and rec[0] < hi and lo < rec[1]:
                    if deps.get(rec[2], 0) < rec[3]:
                        deps[rec[2]] = rec[3]
        for (sp, lo, hi) in writes:
            for rec in self.recs.get(sp, ()):
                if rec[0] < hi and lo < rec[1]:
                    if e is not None and rec[2].split("#")[0] == e:
                        continue
                    if deps.get(rec[2], 0) < rec[3]:
                        deps[rec[2]] = rec[3]
        return deps

    def _record(self, reads, writes, key, val):
        for (sp, lo, hi) in writes:
            lst = self.recs.setdefault(sp, [])
            lst[:] = [rc for rc in lst if not (lo <= rc[0] and rc[1] <= hi)]
            lst.append([lo, hi, key, val, True])
        for (sp, lo, hi) in reads:
            lst = self.recs.setdefault(sp, [])
            lst[:] = [rc for rc in lst if not ((not rc[4]) and rc[2] == key and lo <= rc[0] and rc[1] <= hi)]
            lst.append([lo, hi, key, val, False])

    def op(self, e, fn, reads=(), writes=(), signal=True, dma=None):
        self.n_ops += 1
        deps = self._deps(reads, writes, e if dma is None else None)
        eng = self.E[e]
        w = self.waited[e]
        for key, val in deps.items():
            if key.split("#")[0] == e and dma is None and (e == "pe" or not SAME_ENG_SYNC):
                continue
            if w.get(key, 0) >= val:
                continue
            eng.wait_ge(self._sem(key), val)
            w[key] = val
        inst = fn()
        ek = f"{e}#{self.epoch.get(e, 0)}"
        if dma is not None:
            s = self._sem(dma)
            self.cnt[dma] += 16
            assert self.cnt[dma] < 65000
            inst.then_inc(s, 16)
            key, val = dma, self.cnt[dma]
        elif signal:
            s = self._sem(ek)
            self.cnt[ek] += 1
            inst.then_inc(s, 1)
            key, val = ek, self.cnt[ek]
            if self.cnt[ek] >= 40000:
                self.epoch[e] = self.epoch.get(e, 0) + 1
        else:
            self._sem(ek)
            key, val = ek, self.cnt[ek] + 1
        self._record(reads, writes, key, val)
        return inst

    def finish(self, eng="sp"):
        e = self.E[eng]
        for key, s in self.sem.items():
            if self.cnt[key] > 0:
                e.wait_ge(s, self.cnt[key])

    def mm(self, out, lhsT, rhs, start, stop, reads, writes):
        nc = self.nc
        return self.op("pe", lambda: nc.tensor.matmul(out, lhsT=lhsT, rhs=rhs, start=start, stop=stop),
                       reads=reads, writes=writes, signal=stop)

    def dma(self, q, out, in_, key, reads, writes):
        eng = self.E[q]
        return self.op(q, lambda: eng.dma_start(out=out, in_=in_), reads=reads, writes=writes, dma=key)


def bf(a):
    return np.asarray(a, dtype=np.float32)


class Ctx:
    pass


def emit_norm(pg, c, h, hn, sq, gain, NT, psb, stride=512):
    nc = pg.nc
    pg.op("act", lambda: nc.scalar.activation(out=sq[:, :, 0:NT], in_=h[:, :, 0:NT], func=AF.Square),
          reads=[h.all()], writes=[sq.all()])
    bank = pg.ps_banks[psb]
    for kc in range(KC):
        pg.mm(bank[:, 0:NT], c.ones_bf[:, :], sq[:, kc, 0:NT], kc == 0, kc == KC - 1,
              reads=[sq.all(), c.ones_bf.all()], writes=[pg.psr(psb, 0, NT)])
    rstd = c.rstd
    pg.op("dve", lambda: nc.vector.tensor_scalar(out=rstd[:, 0:NT], in0=bank[:, 0:NT], scalar1=1.0 / D, scalar2=EPS,
                                                 op0=ALU.mult, op1=ALU.add),
          reads=[pg.psr(psb, 0, NT)], writes=[rstd.all()])
    pg.op("act", lambda: nc.scalar.activation(out=rstd[:, 0:NT], in_=rstd[:, 0:NT], func=AF.Sqrt),
          reads=[rstd.all()], writes=[rstd.all()])
    pg.op("dve", lambda: nc.vector.reciprocal(out=rstd[:, 0:NT], in_=rstd[:, 0:NT]),
          reads=[rstd.all()], writes=[rstd.all()])
    for kc in range(KC):
        pg.op("dve", lambda kc=kc: nc.vector.scalar_tensor_tensor(out=hn[:, kc, 0:NT], in0=h[:, kc, 0:NT],
                                                                  scalar=gain[:, kc:kc + 1], in1=rstd[:, 0:NT],
                                                                  op0=ALU.mult, op1=ALU.mult),
              reads=[h.all(), rstd.all(), gain.all()], writes=[hn.r(kc * stride, kc * stride + NT)])


NT_MAX = 512


class WStream:
    def __init__(self, pg, nslots, slot_elems):
        self.pg = pg
        self.slots = [pg.sb(f"wslot{i}", [P, slot_elems], BF16) for i in range(nslots)]
        self.i = 0

    def load(self, src_ap, nelem, dep=None):
        pg = self.pg
        s = self.slots[self.i % len(self.slots)]
        k = self.i % len(self.slots)
        self.i += 1
        pg.dma("pool", s[:, 0:nelem], src_ap, f"w{k}", reads=[dep] if dep else [], writes=[s.r(0, nelem)])
        return s


def wload(ws, ent, sidx, nelem):
    if isinstance(ent, tuple):
        ap, dep = ent
        return ws.load(ap[sidx, :, 0:nelem], nelem, dep=dep)
    return ws.load(ent[sidx, :, 0:nelem], nelem)


def emit_ffn(pg, c, ws, h, hn, aT, wg_d, wu_d, wd_d, NT):
    nc = pg.nc
    GRP = 3
    fcs = list(range(FC))
    for g0 in range(0, FC, GRP):
        grp = fcs[g0:g0 + GRP]
        n = len(grp)
        sg = wload(ws, wg_d, g0 // GRP, n * 2048)
        su = wload(ws, wu_d, g0 // GRP, n * 2048)
        for j, fc in enumerate(grp):
            bg = 0 + (fc % 2)
            bu = 2 + (fc % 2)
            for kc in range(KC):
                pg.mm(pg.ps_banks[bg][:, 0:NT], sg[:, j * 2048 + kc * 128: j * 2048 + (kc + 1) * 128], hn[:, kc, 0:NT],
                      kc == 0, kc == KC - 1, reads=[sg.r(j * 2048, (j + 1) * 2048), hn.all()], writes=[pg.psr(bg, 0, NT)])
            for kc in range(KC):
                pg.mm(pg.ps_banks[bu][:, 0:NT], su[:, j * 2048 + kc * 128: j * 2048 + (kc + 1) * 128], hn[:, kc, 0:NT],
                      kc == 0, kc == KC - 1, reads=[su.r(j * 2048, (j + 1) * 2048), hn.all()], writes=[pg.psr(bu, 0, NT)])
            sl = c.silu[fc % 2]
            pg.op("act", lambda bg=bg, sl=sl: nc.scalar.activation(out=sl[:, 0:NT], in_=pg.ps_banks[bg][:, 0:NT], func=AF.Silu),
                  reads=[pg.psr(bg, 0, NT)], writes=[sl.all()])
            pg.op("dve", lambda bu=bu, sl=sl, fc=fc: nc.vector.tensor_tensor(out=aT[:, fc, 0:NT], in0=pg.ps_banks[bu][:, 0:NT],
                                                                             in1=sl[:, 0:NT], op=ALU.mult),
                  reads=[pg.psr(bu, 0, NT), sl.all()], writes=[aT.r(fc * NT_MAX, fc * NT_MAX + NT)])
    for oc in range(KC):
        sd = wload(ws, wd_d, oc, FC * 128)
        b = 4 + (oc % 2)
        for k in range(FC):
            pg.mm(pg.ps_banks[b][:, 0:NT], sd[:, k * 128:(k + 1) * 128], aT[:, k, 0:NT], k == 0, k == FC - 1,
                  reads=[sd.r(0, FC * 128), aT.all()], writes=[pg.psr(b, 0, NT)])
        pg.op("dve", lambda b=b, oc=oc: nc.vector.tensor_tensor(out=h[:, oc, 0:NT], in0=pg.ps_banks[b][:, 0:NT],
                                                                in1=h[:, oc, 0:NT], op=ALU.add),
              reads=[pg.psr(b, 0, NT), h.r(oc * NT_MAX, oc * NT_MAX + NT)], writes=[h.r(oc * NT_MAX, oc * NT_MAX + NT)])


def slopes():
    return [2.0 ** (-8.0 * (i + 1) / 8) for i in range(8)]


def build_l0(S, with_l1=False, debug=None):
    NS = S // 2048
    NKT = S // 128
    NT = 512
    pg = Prog()
    nc = pg.nc
    c = Ctx()
    dr = lambda name, shape, dt=F32, kind="ExternalInput": pg.dram(name, shape, dt, kind)
    xT = dr("xT", [P, KC, S])
    wkvf_d = dr("wkvf", [P, KC * 1024])
    wkvt_d = dr("wkvt", [P, KC * 512])
    w1k_d = dr("w1k", [P, 32 * 128]); w1v_d = dr("w1v", [P, 32 * 128])
    w2k_d = dr("w2k", [P, 128]); w2v_d = dr("w2v", [P, 128])
    pek_d = dr("pek", [P, 32]); pev_d = dr("pev", [P, 32])
    gains_d = dr("gains", [P, 2 * KC])
    wgt_d = dr("wgt", [P, KC * 24])
    wcT_d = dr("wcT", [P, 8 * 128])
    sgug_d = dr("sgug", [P, 1024]); sgub_d = dr("sgub", [P, 1024])
    tabs_d = dr("tabs", [P, 173])
    alibi_d = dr("alibi", [P, NKT * 8])
    cst_d = dr("cst", [P, 128 * 4 + 2048])
    cstf_d = dr("cstf", [P, 1024 + 8 + 3])
    scr_f = dr("scr_f", [8, P, S], BF16, "ExternalOutput" if debug else "Internal")
    scr_t = dr("scr_t", [S, 4, 128], BF16, "ExternalOutput" if debug else "Internal")

    c.cst = pg.sb("cst", [P, 128 * 4 + 2048], BF16)
    cst = c.cst

    class V:
        def __init__(s, base, lo, hi, shape=None):
            s.base = base; s.lo = lo; s.hi = hi
        def all(s):
            return s.base.r(s.lo, s.hi)
        def __getitem__(s, k):
            return s.base.t[:, s.lo:s.hi][k]
    c.ones_bf = V(cst, 0, 128); c.ident = V(cst, 128, 256); c.tril = V(cst, 256, 384); c.wfirst = V(cst, 384, 512)
    c.wide = V(cst, 512, 2560)
    c.cstf = pg.sb("cstf", [P, 1024 + 8 + 3], F32)
    c.mid = V(c.cstf, 0, 1024); c.m8 = V(c.cstf, 1024, 1032); c.fpat = V(c.cstf, 1032, 1035)
    c.tabs = pg.sb("tabs", [P, 173], F32)
    c.cex = V(c.tabs, 0, 96); c.fx = V(c.tabs, 96, 128); c.wex = V(c.tabs, 128, 140); c.exm = V(c.tabs, 140, 172); c.hex = V(c.tabs, 172, 173)
    c.alibi = pg.sb("alibi", [P, NKT, 8], F32)
    c.gains = pg.sb("gains", [P, 2 * KC], F32)
    c.wcT = pg.sb("wcT", [P, 8, 128], BF16)
    c.sgug = pg.sb("sgug", [P, 1024], F32); c.sgub = pg.sb("sgub", [P, 8, 128], F32)
    c.wgt = pg.sb("wgt", [P, KC, 24], BF16)
    c.kcmpT = pg.sb("kcmpT", [P, 2, 1024], BF16)
    c.vcmp = pg.sb("vcmp", [P, 2, 8, 128], BF16)
    c.rstd = pg.sb("rstd", [P, 512], F32)
    c.silu = [pg.sb(f"silu{i}", [P, 512], F32) for i in range(2)]
    h = pg.sb("h", [P, KC, 512], F32)
    hn = pg.sb("hn", [P, KC, 512], BF16)
    ZONE = (pg.sb_off + 31) // 32 * 32
    ws = WStream(pg, 3, 6144)
    PL = (pg.sb_off + 31) // 32 * 32
    print("persistent bytes", ZONE - SB_LO, "PL start", PL - SB_LO, "PL size", SB_HI - PL)

    def ld(q, dst, src, key):
        pg.dma(q, dst.t[:] if isinstance(dst, T) else dst, src, key, reads=[], writes=[dst.all()])
    ld("pool", c.cst, cst_d[:, :], "c0")
    ld("sp", c.cstf, cstf_d[:, :], "c1")
    ld("sp", c.tabs, tabs_d[:, :], "c2")
    pg.dma("sp", c.alibi.t[:].rearrange("p a b -> p (a b)"), alibi_d[:, :], "c3", reads=[], writes=[c.alibi.all()])
    ld("sp", c.gains, gains_d[:, :], "c4")
    pg.dma("pool", c.wcT.t[:].rearrange("p a b -> p (a b)"), wcT_d[:, :], "c5", reads=[], writes=[c.wcT.all()])
    ld("sp", c.sgug, sgug_d[:, :], "c6")
    pg.dma("sp", c.sgub.t[:].rearrange("p a b -> p (a b)"), sgub_d[:, :], "c7", reads=[], writes=[c.sgub.all()])
    pg.dma("pool", c.wgt.t[:].rearrange("p a b -> p (a b)"), wgt_d[:, :], "c8", reads=[], writes=[c.wgt.all()])
    pg.op("pool", lambda: nc.gpsimd.affine_select(out=c.wcT[:, :, :], in_=c.wcT[:, :, :], pattern=[[0, 8], [1, 128]],
                                                   compare_op=ALU.is_ge, fill=0.0, base=0, channel_multiplier=-1),
          reads=[c.wcT.all()], writes=[c.wcT.all()])
    pg.op("dve", lambda: nc.vector.memset(c.kcmpT.t[:], 0.0), writes=[c.kcmpT.all()])
    pg.op("dve", lambda: nc.vector.memset(c.vcmp.t[:], 0.0), writes=[c.vcmp.all()])
    g_mix = V(c.gains, 0, KC); g_ffn = V(c.gains, KC, 2 * KC)

    o = ZONE
    wkvf = pg.sb("wkvf", [P, KC, 1024], BF16, at=o); o += KC * 1024 * 2
    wkvt = pg.sb("wkvt", [P, KC, 512], BF16, at=o); o += KC * 512 * 2
    sq = pg.sb("sq0", [P, KC, 512], BF16, at=o); o += KC * 512 * 2
    stf = pg.sb("stf", [P, 8, 512], BF16, at=o); o += 8 * 512 * 2
    stt = pg.sb("stt", [P, 4, 512], BF16, at=o); o += 4 * 512 * 2
    pg.dma("pool", wkvf.t[:].rearrange("p a b -> p (a b)"), wkvf_d[:, :], "c9", reads=[], writes=[wkvf.all()])
    pg.dma("pool", wkvt.t[:].rearrange("p a b -> p (a b)"), wkvt_d[:, :], "c10", reads=[], writes=[wkvt.all()])
    went = None
    conv = []
    if with_l1:
        went = {}
        for nm, shp in (("wuq", [8, P, KC * 256]), ("wv", [4, P, KC * 256]), ("wout", [6, P, 6144]), ("wg", [15, P, 6144]),
                        ("wu", [15, P, 6144]), ("wd", [16, P, FC * 128]), ("wc", [16, P, 6144]), ("woc", [6, P, 6144]),
                        ("wg1", [15, P, 6144]), ("wu1", [15, P, 6144]), ("wd1", [16, P, FC * 128])):
            src = dr(nm, shp)
            dst = dr(nm + "_b", shp, BF16, "Internal")
            went[nm] = (dst, ("wconv", 0, 1))
            for sidx in range(shp[0]):
                conv.append((dst, src, sidx))
    n_t0 = S // 512
    per_tile = (len(conv) + n_t0 - 1) // n_t0
    for j in range(S // 512):
        pg.dma("sp", h.t[:], xT[:, :, 512 * j:512 * j + 512], "x", reads=[], writes=[h.all()])
        emit_norm(pg, c, h, hn, sq, g_mix, 512, 7)
        for fcn in range(8):
            b = fcn % 2
            for kc in range(KC):
                pg.mm(pg.ps_banks[b][:, :], wkvf[:, kc, fcn * 128:(fcn + 1) * 128], hn[:, kc, :], kc == 0, kc == KC - 1,
                      reads=[wkvf.all(), hn.all()], writes=[pg.psr(b)])
            if fcn % 2 == 0:
                pg.op("act", lambda b=b, fcn=fcn: nc.scalar.copy(out=stf[:, fcn, :], in_=pg.ps_banks[b][:, :]),
                      reads=[pg.psr(b)], writes=[stf.r(fcn * 512, fcn * 512 + 512)])
            else:
                pg.op("dve", lambda b=b, fcn=fcn: nc.vector.tensor_copy(out=stf[:, fcn, :], in_=pg.ps_banks[b][:, :]),
                      reads=[pg.psr(b)], writes=[stf.r(fcn * 512, fcn * 512 + 512)])
        pg.dma("pool", scr_f.rearrange("f p t -> p f t")[:, :, 512 * j:512 * j + 512], stf.t[:], "sf",
               reads=[stf.all()], writes=[("scr_f", 512 * j, 512 * j + 512)])
        for ts in range(4):
            b = 2 + ts % 2
            for kc in range(KC):
                pg.mm(pg.ps_banks[b][:, :], hn[:, kc, ts * 128:(ts + 1) * 128], wkvt[:, kc, :], kc == 0, kc == KC - 1,
                      reads=[wkvt.all(), hn.all()], writes=[pg.psr(b)])
            if ts % 2 == 0:
                pg.op("act", lambda b=b, ts=ts: nc.scalar.copy(out=stt[:, ts, :], in_=pg.ps_banks[b][:, :]),
                      reads=[pg.psr(b)], writes=[stt.r(ts * 512, ts * 512 + 512)])
            else:
                pg.op("dve", lambda b=b, ts=ts: nc.vector.tensor_copy(out=stt[:, ts, :], in_=pg.ps_banks[b][:, :]),
                      reads=[pg.psr(b)], writes=[stt.r(ts * 512, ts * 512 + 512)])
        pg.dma("pool", scr_t[512 * j:512 * j + 512, :, :].rearrange("(ts p) i d -> p ts (i d)", p=128), stt.t[:], "st",
               reads=[stt.all()], writes=[("scr_t", 512 * j, 512 * j + 512)])
        for (dst, src, sidx) in conv[j * per_tile:(j + 1) * per_tile]:
            pg.dma("pool", dst[sidx, :, :], src[sidx, :, :], "cv", reads=[], writes=[("wconv", 0, 1)])

    o = ZONE
    w1 = [pg.sb("w1k", [P, 32, 128], BF16, at=o), pg.sb("w1v", [P, 32, 128], BF16, at=o + 8192)]; o += 16384
    w2 = [pg.sb("w2k", [P, 128], BF16, at=o), pg.sb("w2v", [P, 128], BF16, at=o + 256)]; o += 512
    pe = [pg.sb("pek", [P, 32], BF16, at=o), pg.sb("pev", [P, 32], BF16, at=o + 64)]; o += 128
    peb = [pg.sb("pebk", [P, 1], F32, at=o), pg.sb("pebv", [P, 1], F32, at=o + 32)]; o += 64
    hid = pg.sb("hid", [P, 256], BF16, at=o); o += 512
    cwin = pg.sb("cwin", [P, 4, 2064], BF16, at=o); o += 4 * 2064 * 2
    for i, (a, b_, cc, d_) in enumerate([(w1[0], w1k_d, "d0", None), (w1[1], w1v_d, "d1", None)]):
        pg.dma("pool", a.t[:].rearrange("p a b -> p (a b)"), b_[:, :], cc, reads=[], writes=[a.all()])
    pg.dma("pool", w2[0].t[:], w2k_d[:, :], "d2", reads=[], writes=[w2[0].all()])
    pg.dma("pool", w2[1].t[:], w2v_d[:, :], "d3", reads=[], writes=[w2[1].all()])
    pg.dma("pool", pe[0].t[:], pek_d[:, :], "d4", reads=[], writes=[pe[0].all()])
    pg.dma("pool", pe[1].t[:], pev_d[:, :], "d5", reads=[], writes=[pe[1].all()])
    for wh in range(2):
        for l in range(32):
            pg.mm(pg.ps_banks[6][:, 0:1], w1[wh][:, l, :], pe[wh][:, l:l + 1], l == 0, l == 31,
                  reads=[w1[wh].all(), pe[wh].all()], writes=[pg.psr(6, 0, 1)])
        pg.op("dve", lambda wh=wh: nc.vector.tensor_copy(out=peb[wh][:, :], in_=pg.ps_banks[6][:, 0:1]),
              reads=[pg.psr(6, 0, 1)], writes=[peb[wh].all()])
    for m in range(S // 2048):
        last = (m == S // 2048 - 1)
        ncol = 2048 if last else 2064
        nblk = 127 if last else 128
        pg.dma("sp", cwin[:, :, 0:ncol], scr_f[0:4, :, 2048 * m:2048 * m + ncol].rearrange("f p t -> p f t"), "cw",
               reads=[("scr_f", 0, S)], writes=[cwin.all()])
        for wh in range(2):
            for g in range(2):
                idx = wh * 2 + g
                for l in range(32):
                    pg.mm(pg.ps_banks[g][:, 0:nblk], w1[wh][:, l, :], cwin[:, idx, l:l + 16 * (nblk - 1) + 1:16],
                          l == 0, l == 31, reads=[w1[wh].all(), cwin.all()], writes=[pg.psr(g, 0, nblk)])
                pg.op("act", lambda g=g, wh=wh: nc.scalar.activation(out=hid[:, g * 128:g * 128 + nblk], in_=pg.ps_banks[g][:, 0:nblk],
                                                                    func=AF.Gelu_apprx_tanh, bias=peb[wh][:, 0:1], scale=1.0),
                      reads=[pg.psr(g, 0, nblk), peb[wh].all()], writes=[hid.r(g * 128, g * 128 + nblk)])
                if wh == 0:
                    pg.mm(pg.ps_banks[2 + g][:, 0:nblk], w2[0][:, :], hid[:, g * 128:g * 128 + nblk], True, True,
                          reads=[w2[0].all(), hid.r(g * 128, g * 128 + nblk)], writes=[pg.psr(2 + g, 0, nblk)])
                    pg.op("dve", lambda g=g, m=m: nc.vector.tensor_copy(out=c.kcmpT[:, g, 128 * m:128 * m + nblk], in_=pg.ps_banks[2 + g][:, 0:nblk]),
                          reads=[pg.psr(2 + g, 0, nblk)], writes=[c.kcmpT.all()])
                else:
                    pg.mm(pg.ps_banks[2 + g][0:nblk, 0:128], hid[:, g * 128:g * 128 + nblk], w2[1][:, :], True, True,
                          reads=[w2[1].all(), hid.r(g * 128, g * 128 + nblk)], writes=[pg.psr(2 + g, 0, 128)])
                    pg.op("dve", lambda g=g, m=m: nc.vector.tensor_copy(out=c.vcmp[0:nblk, g, m, :], in_=pg.ps_banks[2 + g][0:nblk, 0:128]),
                          reads=[pg.psr(2 + g, 0, 128)], writes=[c.vcmp.all()])
    if debug == "p0":
        dbg_k = dr("dbg_k", [P, 2048], BF16, "ExternalOutput")
        dbg_v = dr("dbg_v", [P, 2048], BF16, "ExternalOutput")
        pg.dma("sp", dbg_k[:, :], c.kcmpT.t[:].rearrange("p a b -> p (a b)"), "dbg", reads=[c.kcmpT.all()], writes=[])
        pg.dma("sp", dbg_v[:, :], c.vcmp.t[:].rearrange("p a b c -> p (a b c)"), "dbg", reads=[c.vcmp.all()], writes=[])
        pg.finish("sp")
        return pg
    emit_l0_main(pg, c, dr, ws, h, hn, g_mix, g_ffn, xT, scr_f, scr_t, S, PL, debug, fused=with_l1, went=went)
    pg.finish("sp")
    return pg


def emit_l0_main(pg, c, dr, ws, h, hn, g_mix, g_ffn, xT, scr_f, scr_t, S, PL, debug, fused=False, went=None):
    nc = pg.nc
    NS = S // 2048
    NKT = S // 128
    SL = slopes()
    SCALE = 128.0 ** -0.5
    full = debug in (None, "h1", "h2")
    if went is not None:
        wuq_d, wv_d, wout_d, wg_d, wu_d, wd_d = (went[k] for k in ("wuq", "wv", "wout", "wg", "wu", "wd"))
    else:
        wuq_d = dr("wuq", [8, P, KC * 256])
        wv_d = dr("wv", [4, P, KC * 256])
        if full:
            wout_d = dr("wout", [6, P, 3 * 2048])
        if debug in (None, "h2"):
            wg_d = dr("wg", [15, P, 3 * 2048]); wu_d = dr("wu", [15, P, 3 * 2048]); wd_d = dr("wd", [16, P, FC * 128])
    if debug in (None, "h1", "h2") and not fused:
        out_d = dr("outT", [P, KC, NS * 512], F32, "ExternalOutput")
    if fused:
        gains1_d = dr("gains1", [P, 3 * KC]); cw_d = dr("cw", [P, KC * 3])
        wc_d, woc_d, wg1_d, wu1_d, wd1_d = (went[k] for k in ("wc", "woc", "wg1", "wu1", "wd1"))
        outF_d = dr("outF", [P, KC, NS * 512], F32, "ExternalOutput")
    if debug == "y":
        dbg_y = dr("dbg_y", [P, 16, NS * 512], BF16, "ExternalOutput")
        dbg_g = dr("dbg_g", [P, NS * 4 * 24], F32, "ExternalOutput")
    o = [PL]

    def pl(name, shape, dt, at=None):
        if at is None:
            off = (o[0] + 31) // 32 * 32
            t = pg.sb(name, shape, dt, at=off)
            o[0] = off + t.nbytes
        else:
            t = pg.sb(name, shape, dt, at=at)
        return t
    uT = pl("uT", [P, 8, 512], BF16)
    qT = pl("qT", [P, 8, 512], BF16)
    ybT = pl("ybT", [P, 8, 512], BF16)
    X0 = (o[0] + 31) // 32 * 32
    sq = pl("sq", [P, KC, 512], BF16)
    gv = pl("gv", [P, 4, 1024], F32, at=ybT.off)
    tmpv = pl("tmpv", [P, 1024], F32, at=X0 + 8192)
    vn = pl("vn", [P, 1024], BF16, at=X0 + 12288)
    s_sb = pl("s_sb", [P, 1024], F32, at=X0)
    p_sb = pl("p_sb", [P, 1024], F32, at=X0 + 4096)
    paccb = pl("paccb", [P, 1040], F32, at=X0 + 8192)
    pbf = pl("pbf", [P, 1024], BF16, at=X0 + 8192 + 4160)
    ptc = [pl(f"ptc{i}", [P, 128], BF16, at=X0 + 8192 + 4160 + 2048 + 256 * i) for i in range(2)]
    aT = pl("aT", [P, FC, 512], BF16, at=PL)
    gates = pl("gates", [P, 4, 24], F32)
    ksT = [pl(f"ksT{i}", [P, 1024], BF16) for i in range(2)]
    vsb = [pl(f"vsb{i}", [P, 8, 129], BF16) for i in range(2)]
    kwT = pl("kwT", [P, 2, 1024], BF16)
    vwb = pl("vwb", [P, 2, 8, 129], BF16)
    PT = [pl(f"PT{i}", [P, 4, 128], BF16) for i in range(3)]
    sc = pl("sc", [P, 264], F32)
    sc2 = pl("sc2", [P, 264], F32)
    t1 = pl("t1", [P, 264], F32)
    sm = pl("sm", [P, 64], F32)
    selbf = pl("selbf", [P, 256], BF16)
    selT = pl("selT", [P, 8, 128], BF16)
    ybt = pl("ybt", [P, 4, 128], F32)
    ybtb = pl("ybtb", [P, 4, 128], BF16)
    if fused:
        gains1 = pl("gains1", [P, 3 * KC], F32)
        cw1 = pl("cw1", [P, KC, 3], F32)
        hhalo = pl("hh", [P, KC, 2], F32)
        hnh = pl("hnh", [P, KC, 2], BF16)
        yT1 = pl("yT1", [P, KC, 512], BF16, at=PL)
        zb1 = pl("zb1", [P, 514], F32, at=ybT.off)
        tmpz1 = pl("tmpz1", [P, 514], F32, at=ybT.off + 2080)
        acc1 = pl("acc1", [P, 512], F32, at=ybT.off + 4160)
        outn1 = pl("outn1", [P, KC, 512], F32, at=PL)
        pg.dma("sp", gains1.t[:], gains1_d[:, :], "c11", reads=[], writes=[gains1.all()])
        pg.dma("sp", cw1.t[:].rearrange("p a b -> p (a b)"), cw_d[:, :], "c12", reads=[], writes=[cw1.all()])
        g3 = (V(gains1, 0, KC), V(gains1, KC, 2 * KC), V(gains1, 2 * KC, 3 * KC))
        l1bufs = (sq, yT1, zb1, tmpz1, acc1, aT, hhalo, hnh, outn1)
    print("PL used", o[0] - PL, "of", SB_HI - PL)
    assert aT.off + aT.nbytes <= SB_HI
    SM = lambda a, b: (sm[:, a:b], sm.r(a, b))
    bank = pg.ps_banks
    tps_bf = bank[4][:, 256:512].bitcast(BF16)
    tps_n = [0]

    def tps_slot():
        k = tps_n[0] % 4
        tps_n[0] += 1
        return tps_bf[:, k * 128:(k + 1) * 128], pg.psr(4)
    msk_n = [0]

    def msk_slot():
        k = msk_n[0] % 2
        msk_n[0] += 1
        return bank[6 + k][:, 0:128], pg.psr(6 + k)
    pt_n = [0]
    cp_n = [0]

    def evac_copy(out, in_, reads, writes):
        cp_n[0] += 1
        if cp_n[0] % 2:
            pg.op("act", lambda: nc.scalar.copy(out=out, in_=in_), reads=reads, writes=writes)
        else:
            pg.op("dve", lambda: nc.vector.tensor_copy(out=out, in_=in_), reads=reads, writes=writes)

    for vb in vsb:
        pg.op("dve", lambda vb=vb: nc.vector.memset(vb[:, :, 128:129], 1.0), writes=[vb.all()])
    pg.op("dve", lambda: nc.vector.memset(vwb[:, :, :, 128:129], 1.0), writes=[vwb.all()])

    import os as _os
    _slots = [int(v) for v in _os.environ.get('K_SLOTS', '').split(',') if v] or list(range(NS))
    tiles = []
    for i in _slots:
        if fused:
            tiles.append((i, 512 * (4 * i + 3) - 128, 16 * i + 11, 1, True))
        tiles.append((i, 512 * (4 * i + 3), 16 * i + 12, 4, False))
    for (i, T0, qb0, NQ, is_halo) in tiles:
        NT = 128 * NQ
        pg.dma("sp", h[:, :, 0:NT], xT[:, :, T0:T0 + NT], "x", reads=[], writes=[h.all()])
        emit_norm(pg, c, h, hn, sq, g_mix, NT, 7)
        for sidx in range(8):
            sl = wload(ws, wuq_d, sidx, 4096)
            for j in range(2):
                ci = 2 * sidx + j
                b = ci % 2
                for kc in range(KC):
                    pg.mm(bank[b][:, 0:NT], sl[:, kc * 256 + j * 128: kc * 256 + (j + 1) * 128], hn[:, kc, 0:NT], kc == 0, kc == KC - 1,
                          reads=[sl.r(0, 4096), hn.all()], writes=[pg.psr(b)])
                if ci < 8:
                    pg.op("act", lambda b=b, ci=ci: nc.scalar.activation(out=uT[:, ci, 0:NT], in_=bank[b][:, 0:NT], func=AF.Gelu_apprx_tanh),
                          reads=[pg.psr(b)], writes=[uT.r(ci * 512, ci * 512 + 512)])
                else:
                    pg.op("act", lambda b=b, ci=ci: nc.scalar.activation(out=qT[:, ci - 8, 0:NT], in_=bank[b][:, 0:NT], func=AF.Copy, scale=SCALE),
                          reads=[pg.psr(b)], writes=[qT.r((ci - 8) * 512, (ci - 8) * 512 + 512)])
        for qv in range(4):
            sl = wload(ws, wv_d, qv, 4096)
            for ts in range(NQ):
                b = 2 + ts % 2
                for kc in range(KC):
                    pg.mm(bank[b][:, 0:256], hn[:, kc, ts * 128:(ts + 1) * 128], sl[:, kc * 256:(kc + 1) * 256], kc == 0, kc == KC - 1,
                          reads=[sl.r(0, 4096), hn.all()], writes=[pg.psr(b, 0, 256)])
                pg.op("act", lambda b=b, ts=ts, qv=qv: nc.scalar.activation(out=gv[:, ts, qv * 256:(qv + 1) * 256], in_=bank[b][:, 0:256],
                                                                           func=AF.Gelu_apprx_tanh),
                      reads=[pg.psr(b, 0, 256)], writes=[gv.r(ts * 1024 + qv * 256, ts * 1024 + (qv + 1) * 256)])
        for ts in range(NQ):
            for kc in range(KC):
                pg.mm(bank[6][:, ts * 32:ts * 32 + 24], hn[:, kc, ts * 128:(ts + 1) * 128], c.wgt[:, kc, :], kc == 0, kc == KC - 1,
                      reads=[c.wgt.all(), hn.all()], writes=[pg.psr(6, ts * 32, ts * 32 + 24)])
            pg.op("act", lambda ts=ts: nc.scalar.activation(out=gates[:, ts, :], in_=bank[6][:, ts * 32:ts * 32 + 24], func=AF.Sigmoid),
                  reads=[pg.psr(6, ts * 32, ts * 32 + 24)], writes=[gates.r(ts * 24, ts * 24 + 24)])
        for ts in range(NQ):
            gvr = gv.r(ts * 1024, ts * 1024 + 1024)
            pg.op("dve", lambda ts=ts: nc.vector.tensor_tensor(out=tmpv[:, :], in0=gv[:, ts, :], in1=gv[:, ts, :], op=ALU.mult),
                  reads=[gvr], writes=[tmpv.all()])
            ss, ssr = SM(0, 8)
            pg.op("dve", lambda ss=ss: nc.vector.tensor_reduce(out=ss, in_=tmpv[:, :].rearrange("p (g c) -> p g c", g=8), axis=AX.X, op=ALU.add),
                  reads=[tmpv.all()], writes=[ssr])
            pg.op("dve", lambda ss=ss: nc.vector.tensor_scalar(out=ss, in0=ss, scalar1=1.0 / 128, scalar2=EPS, op0=ALU.mult, op1=ALU.add),
                  reads=[ssr], writes=[ssr])
            pg.op("act", lambda ss=ss: nc.scalar.activation(out=ss, in_=ss, func=AF.Sqrt), reads=[ssr], writes=[ssr])
            pg.op("dve", lambda ss=ss: nc.vector.reciprocal(out=ss, in_=ss), reads=[ssr], writes=[ssr])
            for g in range(8):
                pg.op("dve", lambda g=g, ts=ts: nc.vector.scalar_tensor_tensor(out=vn[:, g * 128:(g + 1) * 128], in0=gv[:, ts, g * 128:(g + 1) * 128],
                                                                               scalar=sm[:, g:g + 1], in1=c.sgug[:, g * 128:(g + 1) * 128],
                                                                               op0=ALU.mult, op1=ALU.mult),
                      reads=[gvr, ssr, c.sgug.all()], writes=[vn.r(g * 128, (g + 1) * 128)])
            for g in range(8):
                b = 2 + g // 4
                pg.mm(bank[b][:, (g % 4) * 128:(g % 4 + 1) * 128], vn[:, g * 128:(g + 1) * 128], c.wcT[:, g, :], True, True,
                      reads=[vn.r(g * 128, (g + 1) * 128), c.wcT.all()], writes=[pg.psr(b, (g % 4) * 128, (g % 4 + 1) * 128)])
            for hf in range(2):
                b = 2 + hf
                pg.op("dve", lambda b=b, hf=hf: nc.vector.tensor_tensor(out=tmpv[:, hf * 512:(hf + 1) * 512], in0=bank[b][:, :],
                                                                        in1=c.sgub[:, 4 * hf:4 * hf + 4, :].rearrange("p a b -> p (a b)"), op=ALU.add),
                      reads=[pg.psr(b), c.sgub.all()], writes=[tmpv.r(hf * 512, (hf + 1) * 512)])
                pg.op("dve", lambda hf=hf, ts=ts: nc.vector.tensor_tensor(out=uT[:, 4 * hf:4 * hf + 4, ts * 128:(ts + 1) * 128],
                                                                          in0=tmpv[:, hf * 512:(hf + 1) * 512].rearrange("p (a b) -> p a b", a=4),
                                                                          in1=uT[:, 4 * hf:4 * hf + 4, ts * 128:(ts + 1) * 128], op=ALU.mult),
                      reads=[tmpv.r(hf * 512, (hf + 1) * 512), uT.r(4 * hf * 512, (4 * hf + 4) * 512)], writes=[uT.r(4 * hf * 512, (4 * hf + 4) * 512)])
        w0 = 128 * (qb0 - 4)
        pg.dma("sp", kwT[:, :, :], scr_f[6:8, :, w0:w0 + 1024].rearrange("g p t -> p g t"), "kw",
               reads=[("scr_f", 0, S)], writes=[kwT.all()])
        for g in range(2):
            pg.dma("sp", vwb[:, g, :, 0:128], scr_t[w0:w0 + 1024, 2 + g, :].rearrange("(kt p) d -> p kt d", p=128), "vw",
                   reads=[("scr_t", 0, S)], writes=[vwb.all()])
        kvn = [0]
        _jqs = [int(v) for v in _os.environ.get('K_JQ', '').split(',') if v] or list(range(NQ))
        _stage = int(_os.environ.get('K_STAGE', '9'))
        for jq in _jqs:
            qb = qb0 + jq
            qc = slice(jq * 128, (jq + 1) * 128)
            for g in range(2):
                ncols = 8 * qb + 7
                npc = (ncols + 511) // 512
                pg.op("dve", lambda: nc.vector.memset(paccb[:, :], 0.0), writes=[paccb.all()])
                for hl in range(4):
                    hh = 4 * g + hl
                    for pc in range(npc):
                        w = min(512, ncols - 512 * pc)
                        pg.mm(bank[pc][:, 0:w], qT[:, hh, qc], c.kcmpT[:, g, 512 * pc:512 * pc + w], True, True,
                              reads=[qT.r(hh * 512, hh * 512 + 512), c.kcmpT.all()], writes=[pg.psr(pc, 0, w)])
                        pg.op("dve", lambda pc=pc, w=w, hh=hh: nc.vector.scalar_tensor_tensor(
                            out=s_sb[:, 512 * pc:512 * pc + w], in0=c.mid[:, 512 * pc:512 * pc + w], scalar=float(SL[hh]),
                            in1=bank[pc][:, 0:w], op0=ALU.mult, op1=ALU.add),
                            reads=[pg.psr(pc, 0, w), c.mid.all()], writes=[s_sb.r(512 * pc, 512 * pc + w)])
                    cw_ = min(96, ncols)
                    pg.op("dve", lambda cw_=cw_: nc.vector.tensor_tensor(out=s_sb[:, 0:cw_], in0=s_sb[:, 0:cw_], in1=c.cex[:, 0:cw_], op=ALU.add),
                          reads=[s_sb.r(0, cw_), c.cex.all()], writes=[s_sb.r(0, cw_)])
                    pg.op("dve", lambda ncols=ncols: nc.vector.tensor_tensor(out=s_sb[:, ncols - 8:ncols], in0=s_sb[:, ncols - 8:ncols], in1=c.m8[:, :], op=ALU.add),
                          reads=[s_sb.r(ncols - 8, ncols), c.m8.all()], writes=[s_sb.r(ncols - 8, ncols)])
                    mx, mxr = SM(8, 9)
                    pg.op("dve", lambda ncols=ncols, mx=mx: nc.vector.reduce_max(out=mx, in_=s_sb[:, 0:ncols], axis=AX.X),
                          reads=[s_sb.r(0, ncols)], writes=[mxr])
                    pg.op("dve", lambda mx=mx: nc.vector.tensor_scalar(out=mx, in0=mx, scalar1=-1.0e20, scalar2=-1.0, op0=ALU.max, op1=ALU.mult),
                          reads=[mxr], writes=[mxr])
                    ls, lsr = SM(9, 10)
                    pg.op("dve", lambda ls=ls: nc.vector.memset(ls, 0.0), writes=[lsr])
                    pg.op("act", lambda ncols=ncols, mx=mx, ls=ls: nc.scalar.activation(out=p_sb[:, 0:ncols], in_=s_sb[:, 0:ncols], func=AF.Exp,
                                                                                       bias=mx, scale=1.0, accum_out=ls),
                          reads=[s_sb.r(0, ncols), mxr, lsr], writes=[p_sb.r(0, ncols), lsr])
                    pg.op("dve", lambda ls=ls: nc.vector.tensor_scalar(out=ls, in0=ls, scalar1=1.0e-30, scalar2=None, op0=ALU.max),
                          reads=[lsr], writes=[lsr])
                    pg.op("dve", lambda ls=ls: nc.vector.reciprocal(out=ls, in_=ls), reads=[lsr], writes=[lsr])
                    if hl == 0:
                        pg.op("dve", lambda ncols=ncols, ls=ls: nc.vector.tensor_scalar(out=paccb[:, 1:1 + ncols], in0=p_sb[:, 0:ncols], scalar1=ls, scalar2=None, op0=ALU.mult),
                              reads=[p_sb.r(0, ncols), lsr], writes=[paccb.r(1, 1 + ncols)])
                    else:
                        pg.op("dve", lambda ncols=ncols, ls=ls: nc.vector.scalar_tensor_tensor(out=paccb[:, 1:1 + ncols], in0=p_sb[:, 0:ncols], scalar=ls,
                                                                                              in1=paccb[:, 1:1 + ncols], op0=ALU.mult, op1=ALU.add),
                              reads=[p_sb.r(0, ncols), lsr, paccb.r(1, 1 + ncols)], writes=[paccb.r(1, 1 + ncols)])
                    pg.op("act", lambda ncols=ncols, ls=ls: nc.scalar.activation(out=pbf[:, 0:ncols], in_=p_sb[:, 0:ncols], func=AF.Copy, scale=ls),
                          reads=[p_sb.r(0, ncols), lsr], writes=[pbf.r(0, ncols)])
                    ntt = (ncols + 127) // 128
                    for tt in range(ntt):
                        w = min(128, ncols - 128 * tt)
                        tp, tpr = tps_slot()
                        pg.op("pe", lambda tp=tp, tt=tt, w=w: nc.tensor.transpose(tp[0:w, :], pbf[:, 128 * tt:128 * tt + w], c.ident[:, :]),
                              reads=[pbf.r(128 * tt, 128 * tt + w), c.ident.all()], writes=[tpr])
                        pk = ptc[pt_n[0] % 2]
                        pt_n[0] += 1
                        evac_copy(pk[0:w, :], tp[0:w, :], [tpr], [pk.all()])
                        pg.mm(bank[5][:, hl * 128:(hl + 1) * 128], pk[0:w, :], c.vcmp[0:w, g, tt, :], tt == 0, tt == ntt - 1,
                              reads=[pk.all(), c.vcmp.all()], writes=[pg.psr(5, hl * 128, (hl + 1) * 128)])
                if _stage < 2:
                    continue
                W = 2 * qb + 2
                Wc = max(W, 32)
                A = paccb[:, 0:4 * W].rearrange("p (j f) -> p j f", f=4)
                pg.op("dve", lambda A=A, W=W: nc.vector.tensor_reduce(out=t1[:, 0:W], in_=A[:, :, 1:4], axis=AX.X, op=ALU.add),
                      reads=[paccb.all()], writes=[t1.r(0, W)])
                pg.op("dve", lambda A=A, W=W: nc.vector.scalar_tensor_tensor(out=t1[:, 0:W], in0=t1[:, 0:W], scalar=2.0, in1=A[:, :, 0],
                                                                             op0=ALU.mult, op1=ALU.add),
                      reads=[paccb.all(), t1.r(0, W)], writes=[t1.r(0, W)])
                if W < Wc:
                    pg.op("dve", lambda W=W, Wc=Wc: nc.vector.memset(sc[:, 1 + W:1 + Wc], -1.0), writes=[sc.r(1 + W, 1 + Wc)])
                pg.op("dve", lambda W=W: nc.vector.tensor_tensor(out=sc[:, 1:1 + W], in0=t1[:, 0:W], in1=paccb[:, 4:4 + 4 * W:4], op=ALU.add),
                      reads=[paccb.all(), t1.r(0, W)], writes=[sc.r(1, 1 + W)])
                pg.op("dve", lambda qb=qb: nc.vector.tensor_tensor(out=sc[:, 2 * qb:2 * qb + 3], in0=sc[:, 2 * qb:2 * qb + 3], in1=c.fpat[:, :], op=ALU.add),
                      reads=[sc.r(2 * qb, 2 * qb + 3), c.fpat.all()], writes=[sc.r(2 * qb, 2 * qb + 3)])
                pg.op("dve", lambda: nc.vector.tensor_tensor(out=sc[:, 1:33], in0=sc[:, 1:33], in1=c.exm[:, :], op=ALU.mult),
                      reads=[sc.r(1, 33), c.exm.all()], writes=[sc.r(1, 33)])
                pg.op("dve", lambda: nc.vector.tensor_tensor(out=sc[:, 1:33], in0=sc[:, 1:33], in1=c.fx[:, :], op=ALU.add),
                      reads=[sc.r(1, 33), c.fx.all()], writes=[sc.r(1, 33)])
                m8a, m8ar = SM(16, 24)
                m8b, m8br = SM(24, 32)
                pg.op("dve", lambda Wc=Wc, m8a=m8a: nc.vector.max(out=m8a, in_=sc[:, 1:1 + Wc]), reads=[sc.r(1, 1 + Wc)], writes=[m8ar])
                pg.op("dve", lambda Wc=Wc, m8a=m8a: nc.vector.match_replace(out=sc2[:, 1:1 + Wc], in_to_replace=m8a, in_values=sc[:, 1:1 + Wc], imm_value=-2.0),
                      reads=[sc.r(1, 1 + Wc), m8ar], writes=[sc2.r(1, 1 + Wc)])
                pg.op("dve", lambda Wc=Wc, m8b=m8b: nc.vector.max(out=m8b, in_=sc2[:, 1:1 + Wc]), reads=[sc2.r(1, 1 + Wc)], writes=[m8br])
                pg.op("dve", lambda Wc=Wc: nc.vector.tensor_scalar(out=selbf[:, 0:Wc], in0=sc[:, 1:1 + Wc], scalar1=sm[:, 31:32], scalar2=None, op0=ALU.is_ge),
                      reads=[sc.r(1, 1 + Wc), m8br], writes=[selbf.r(0, Wc)])
                nch = (Wc + 31) // 32
                for ch in range(nch):
                    w = min(32, Wc - 32 * ch)
                    tp, tpr = tps_slot()
                    pg.op("pe", lambda tp=tp, ch=ch, w=w: nc.tensor.transpose(tp[0:w, :], selbf[:, 32 * ch:32 * ch + w], c.ident[:, :]),
                          reads=[selbf.r(32 * ch, 32 * ch + w), c.ident.all()], writes=[tpr])
                    evac_copy(selT[0:w, ch, :], tp[0:w, :], [tpr], [selT.r(ch * 128, ch * 128 + 128)])

                def key_step(kmat, kreg, vmat, vreg, delta, obanks, first, last, maskmode, kt, wexcol=None):
                    b = kt % 2
                    pg.mm(bank[b][:, :], kmat, qT[:, 4 * g:4 * g + 4, qc], True, True,
                          reads=[kreg, qT.r(4 * g * 512, (4 * g + 4) * 512)], writes=[pg.psr(b)])
                    pt = PT[pt_n[0] % 3]
                    pt_n[0] += 1
                    for hl in range(4):
                        hh = 4 * g + hl
                        pg.op("act", lambda hl=hl, hh=hh, pt=pt, b=b: nc.scalar.activation(out=pt[:, hl, :], in_=bank[b][:, hl * 128:(hl + 1) * 128],
                                                                                          func=AF.Exp, bias=c.alibi[:, delta, hh:hh + 1], scale=1.0),
                              reads=[pg.psr(b, hl * 128, (hl + 1) * 128), c.alibi.all()], writes=[pt.r(hl * 128, (hl + 1) * 128)])
                    if maskmode in ("tril", "wfirst"):
                        cm = c.tril if maskmode == "tril" else c.wfirst
                        if wexcol is None:
                            pg.op("dve", lambda pt=pt, cm=cm: nc.vector.tensor_tensor(out=pt[:, :, :], in0=pt[:, :, :],
                                                                                      in1=cm[:, :].unsqueeze(1).to_broadcast([P, 4, 128]), op=ALU.mult),
                                  reads=[pt.all(), cm.all()], writes=[pt.all()])
                        else:
                            for hl in range(4):
                                pg.op("dve", lambda pt=pt, cm=cm, hl=hl: nc.vector.scalar_tensor_tensor(out=pt[:, hl, :], in0=pt[:, hl, :], scalar=c.wex[:, wexcol:wexcol + 1],
                                                                                                        in1=cm[:, :], op0=ALU.mult, op1=ALU.mult),
                                      reads=[pt.r(hl * 128, hl * 128 + 128), cm.all(), c.wex.all()], writes=[pt.r(hl * 128, hl * 128 + 128)])
                    elif maskmode == "sel":
                        ch = kt // 16
                        w = min(32, Wc - 32 * ch)
                        mk, mkr = msk_slot()
                        pg.mm(mk, c.wide[0:w, 128 * (kt % 16):128 * (kt % 16) + 128], selT[0:w, ch, :], True, True,
                              reads=[c.wide.all(), selT.r(ch * 128, ch * 128 + 128)], writes=[mkr])
                        pg.op("dve", lambda pt=pt, mk=mk: nc.vector.tensor_tensor(out=pt[:, :, :], in0=pt[:, :, :],
                                                                                  in1=mk.unsqueeze(1).to_broadcast([P, 4, 128]), op=ALU.mult),
                              reads=[pt.all(), mkr], writes=[pt.all()])
                    elif maskmode == "wex":
                        pg.op("dve", lambda pt=pt: nc.vector.tensor_scalar(out=pt[:, :, :], in0=pt[:, :, :], scalar1=c.wex[:, wexcol:wexcol + 1], scalar2=None, op0=ALU.mult),
                              reads=[pt.all(), c.wex.all()], writes=[pt.all()])
                    for hl in range(4):
                        ob = obanks[hl // 2]
                        col = (hl % 2) * 129
                        pg.mm(bank[ob][:, col:col + 129], pt[:, hl, :], vmat, first, last,
                              reads=[pt.r(hl * 128, hl * 128 + 128), vreg], writes=[pg.psr(ob, col, col + 129)])

                if _stage < 3:
                    continue
                for kt in range(qb + 1):
                    if kt % 8 == 0:
                        cch = kt // 8
                        kb = kvn[0] % 2
                        kvn[0] += 1
                        ks_, vs_ = ksT[kb], vsb[kb]
                        pg.dma("sp", ks_[:, :], scr_f[4 + g, :, 1024 * cch:1024 * cch + 1024], f"ks{kb}",
                               reads=[("scr_f", 0, S)], writes=[ks_.all()])
                        pg.dma("sp", vs_[:, :, 0:128], scr_t[1024 * cch:1024 * cch + 1024, g, :].rearrange("(kt p) d -> p kt d", p=128), f"vs{kb}",
                               reads=[("scr_t", 0, S)], writes=[vs_.all()])
                    key_step(ks_[:, (kt % 8) * 128:(kt % 8 + 1) * 128], ks_.all(), vs_[:, kt % 8, :], vs_.all(), qb - kt, (2, 3),
                             kt == 0, kt == qb, "tril" if kt == qb else "sel", kt)
                if _stage < 4:
                    continue
                for kt in range(qb - 4, qb + 1):
                    lw = kt - (qb0 - 4)
                    mode = "wfirst" if kt == qb - 4 else ("tril" if kt == qb else None)
                    wexcol = kt if kt < 12 else None
                    if mode is None and wexcol is not None:
                        mode = "wex"
                    key_step(kwT[:, g, lw * 128:(lw + 1) * 128], kwT.all(), vwb[:, g, lw, :], vwb.all(), qb - kt, (6, 7),
                             kt == qb - 4, kt == qb, mode, kt, wexcol)
                if _stage < 5:
                    continue
                lsw, lswr = SM(32, 40)
                for k2, ob in enumerate((2, 3, 6, 7)):
                    pg.op("dve", lambda k2=k2, ob=ob: nc.vector.tensor_copy(out=sm[:, 32 + 2 * k2:34 + 2 * k2], in_=bank[ob][:, 128:258:129]),
                          reads=[pg.psr(ob, 0, 258)], writes=[lswr])
                pg.op("dve", lambda: nc.vector.tensor_scalar(out=sm[:, 32:40], in0=sm[:, 32:40], scalar1=1.0e-30, scalar2=None, op0=ALU.max),
                      reads=[lswr], writes=[lswr])
                pg.op("dve", lambda: nc.vector.reciprocal(out=sm[:, 32:40], in_=sm[:, 32:40]), reads=[lswr], writes=[lswr])
                gview = gates[:, jq, 12 * g:12 * g + 12].rearrange("p (h b) -> p h b", b=3)
                cf, cfr = SM(40, 48)
                pg.op("dve", lambda gview=gview: nc.vector.tensor_tensor(out=sm[:, 40:44], in0=sm[:, 32:36], in1=gview[:, :, 1], op=ALU.mult),
                      reads=[lswr, gates.all()], writes=[cfr])
                pg.op("dve", lambda gview=gview: nc.vector.tensor_tensor(out=sm[:, 44:48], in0=sm[:, 36:40], in1=gview[:, :, 2], op=ALU.mult),
                      reads=[lswr, gates.all()], writes=[cfr])
                for hl in range(4):
                    yr = ybt.r(hl * 128, hl * 128 + 128)
                    pg.op("dve", lambda hl=hl, gview=gview: nc.vector.tensor_scalar(out=ybt[:, hl, :], in0=bank[5][:, hl * 128:(hl + 1) * 128],
                                                                                    scalar1=gview[:, hl, 0:1], scalar2=None, op0=ALU.mult),
                          reads=[pg.psr(5, hl * 128, hl * 128 + 128), gates.all()], writes=[yr])
                    ob = 2 + hl // 2
                    col = (hl % 2) * 129
                    pg.op("dve", lambda hl=hl, ob=ob, col=col: nc.vector.scalar_tensor_tensor(out=ybt[:, hl, :], in0=bank[ob][:, col:col + 128], scalar=sm[:, 40 + hl:41 + hl],
                                                                                              in1=ybt[:, hl, :], op0=ALU.mult, op1=ALU.add),
                          reads=[pg.psr(ob, col, col + 128), cfr, yr], writes=[yr])
                    ob = 6 + hl // 2
                    pg.op("dve", lambda hl=hl, ob=ob, col=col: nc.vector.scalar_tensor_tensor(out=ybtb[:, hl, :], in0=bank[ob][:, col:col + 128], scalar=sm[:, 44 + hl:45 + hl],
                                                                                              in1=ybt[:, hl, :], op0=ALU.mult, op1=ALU.add),
                          reads=[pg.psr(ob, col, col + 128), cfr, yr], writes=[ybtb.r(hl * 128, hl * 128 + 128)])
                    tp, tpr = tps_slot()
                    pg.op("pe", lambda tp=tp, hl=hl: nc.tensor.transpose(tp[:, :], ybtb[:, hl, :], c.ident[:, :]),
                          reads=[ybtb.r(hl * 128, hl * 128 + 128), c.ident.all()], writes=[tpr])
                    hh = 4 * g + hl
                    evac_copy(ybT[:, hh, qc], tp[:, :], [tpr], [ybT.r(hh * 512 + jq * 128, hh * 512 + jq * 128 + 128)])
        if debug == "y":
            if is_halo:
                continue
            pg.dma("sp", dbg_y[:, 0:8, i * 512:(i + 1) * 512], uT.t[:], "dbg", reads=[uT.all()], writes=[])
            pg.dma("sp", dbg_y[:, 8:16, i * 512:(i + 1) * 512], ybT.t[:], "dbg", reads=[ybT.all()], writes=[])
            pg.dma("sp", dbg_g[:, i * 96:(i + 1) * 96], gates.t[:].rearrange("p a b -> p (a b)"), "dbg", reads=[gates.all()], writes=[])
            continue
        for gi in range(6):
            sl = wload(ws, wout_d, gi, 6144)
            for j in range(3):
                fcn = 3 * gi + j
                if fcn >= 16:
                    break
                b = fcn % 2
                for kc in range(KC):
                    rhs = uT[:, kc, 0:NT] if kc < 8 else ybT[:, kc - 8, 0:NT]
                    pg.mm(bank[b][:, 0:NT], sl[:, j * 2048 + kc * 128:j * 2048 + (kc + 1) * 128], rhs, kc == 0, kc == KC - 1,
                          reads=[sl.r(j * 2048, (j + 1) * 2048), uT.all(), ybT.all()], writes=[pg.psr(b)])
                pg.op("dve", lambda b=b, fcn=fcn: nc.vector.tensor_tensor(out=h[:, fcn, 0:NT], in0=bank[b][:, 0:NT], in1=h[:, fcn, 0:NT], op=ALU.add),
                      reads=[pg.psr(b), h.r(fcn * 512, fcn * 512 + 512)], writes=[h.r(fcn * 512, fcn * 512 + 512)])
        if debug != "h1":
            emit_norm(pg, c, h, hn, sq, g_ffn, NT, 7)
            emit_ffn(pg, c, ws, h, hn, aT, wg_d, wu_d, wd_d, NT)
        if not fused:
            pg.dma("sp", out_d[:, :, i * 512:(i + 1) * 512], h.t[:], "out", reads=[h.all()], writes=[])
        elif is_halo:
            if i == 0:
                pg.op("dve", lambda: nc.vector.tensor_scalar(out=hhalo[:, :, :], in0=h[:, :, NT - 2:NT], scalar1=c.hex[:, 0:1], scalar2=None, op0=ALU.mult),
                      reads=[h.all(), c.tabs.all()], writes=[hhalo.all()])
            else:
                pg.op("dve", lambda: nc.vector.tensor_copy(out=hhalo[:, :, :], in_=h[:, :, NT - 2:NT]), reads=[h.all()], writes=[hhalo.all()])
        else:
            emit_l1_slot(pg, c, ws, h, hn, l1bufs, g3, cw1, wc_d, woc_d, wg1_d, wu1_d, wd1_d, None, outF_d[:, :, i * 512:(i + 1) * 512])


def emit_l1_slot(pg, c, ws, h, hn, bufs, gains3, cw, wc_d, woc_d, wg_d, wu_d, wd_d, halo_src, out_dst):
    nc = pg.nc
    bank = pg.ps_banks
    sq, yT, zb, tmpz, acc, aT, hh, hnh, outn = bufs
    g_mix, g_ffn, g_f = gains3
    if halo_src is not None:
        pg.dma("sp", hh.t[:], halo_src, "halo", reads=[], writes=[hh.all()])
    emit_norm(pg, c, h, hn, sq, g_mix, 512, 7)
    emit_norm(pg, c, hh, hnh, sq, g_mix, 2, 7, stride=2)
    for j in range(KC):
        sl = wload(ws, wc_d, j, 6144)
        bb = 0 if j % 2 == 0 else 3
        for part in range(3):
            for kc in range(KC):
                pg.mm(bank[bb + part][:, :], sl[:, part * 2048 + kc * 128:part * 2048 + (kc + 1) * 128], hn[:, kc, :], kc == 0, kc == KC - 1,
                      reads=[sl.r(part * 2048, (part + 1) * 2048), hn.all()], writes=[pg.psr(bb + part)])
        for part in (1, 2):
            for kc in range(KC):
                pg.mm(bank[6][:, 2 * (part - 1):2 * part], sl[:, part * 2048 + kc * 128:part * 2048 + (kc + 1) * 128], hnh[:, kc, :], kc == 0, kc == KC - 1,
                      reads=[sl.r(part * 2048, (part + 1) * 2048), hnh.all()], writes=[pg.psr(6, 2 * (part - 1), 2 * part)])
        pg.op("act", lambda bb=bb: nc.scalar.copy(out=tmpz[:, 2:514], in_=bank[bb + 2][:, :]), reads=[pg.psr(bb + 2)], writes=[tmpz.r(2, 514)])
        pg.op("act", lambda: nc.scalar.copy(out=tmpz[:, 0:2], in_=bank[6][:, 2:4]), reads=[pg.psr(6, 2, 4)], writes=[tmpz.r(0, 2)])
        pg.op("dve", lambda bb=bb: nc.vector.tensor_tensor(out=zb[:, 2:514], in0=bank[bb + 1][:, :], in1=tmpz[:, 2:514], op=ALU.mult),
              reads=[pg.psr(bb + 1), tmpz.r(2, 514)], writes=[zb.r(2, 514)])
        pg.op("dve", lambda: nc.vector.tensor_tensor(out=zb[:, 0:2], in0=bank[6][:, 0:2], in1=tmpz[:, 0:2], op=ALU.mult),
              reads=[pg.psr(6, 0, 2), tmpz.r(0, 2)], writes=[zb.r(0, 2)])
        pg.op("dve", lambda j=j: nc.vector.tensor_scalar(out=acc[:, :], in0=zb[:, 2:514], scalar1=cw[:, j, 2:3], scalar2=None, op0=ALU.mult),
              reads=[zb.all(), cw.all()], writes=[acc.all()])
        pg.op("dve", lambda j=j: nc.vector.scalar_tensor_tensor(out=acc[:, :], in0=zb[:, 1:513], scalar=cw[:, j, 1:2], in1=acc[:, :], op0=ALU.mult, op1=ALU.add),
              reads=[zb.all(), cw.all(), acc.all()], writes=[acc.all()])
        pg.op("dve", lambda j=j: nc.vector.scalar_tensor_tensor(out=acc[:, :], in0=zb[:, 0:512], scalar=cw[:, j, 0:1], in1=acc[:, :], op0=ALU.mult, op1=ALU.add),
              reads=[zb.all(), cw.all(), acc.all()], writes=[acc.all()])
        pg.op("dve", lambda j=j, bb=bb: nc.vector.tensor_tensor(out=yT[:, j, :], in0=bank[bb][:, :], in1=acc[:, :], op=ALU.mult),
              reads=[pg.psr(bb), acc.all()], writes=[yT.r(j * 512, j * 512 + 512)])
    for gi in range(6):
        sl = wload(ws, woc_d, gi, 6144)
        for j in range(3):
            fcn = 3 * gi + j
            if fcn >= 16:
                break
            b = fcn % 2
            for kc in range(KC):
                pg.mm(bank[b][:, :], sl[:, j * 2048 + kc * 128:j * 2048 + (kc + 1) * 128], yT[:, kc, :], kc == 0, kc == KC - 1,
                      reads=[sl.r(j * 2048, (j + 1) * 2048), yT.all()], writes=[pg.psr(b)])
            pg.op("dve", lambda b=b, fcn=fcn: nc.vector.tensor_tensor(out=h[:, fcn, :], in0=bank[b][:, :], in1=h[:, fcn, :], op=ALU.add),
                  reads=[pg.psr(b), h.r(fcn * 512, fcn * 512 + 512)], writes=[h.r(fcn * 512, fcn * 512 + 512)])
    emit_norm(pg, c, h, hn, sq, g_ffn, 512, 7)
    emit_ffn(pg, c, ws, h, hn, aT, wg_d, wu_d, wd_d, 512)
    emit_norm(pg, c, h, outn, sq, g_f, 512, 7)
    pg.dma("sp", out_dst, outn.t[:], "out", reads=[outn.all()], writes=[])


class V:
    def __init__(s, base, lo, hi):
        s.base = base; s.lo = lo; s.hi = hi

    def all(s):
        return s.base.r(s.lo, s.hi)

    def __getitem__(s, k):
        return s.base.t[:, s.lo:s.hi][k]


def build_l1(S):
    NS = S // 2048
    pg = Prog()
    nc = pg.nc
    c = Ctx()
    dr = lambda name, shape, dt=F32, kind="ExternalInput": pg.dram(name, shape, dt, kind)
    hin = dr("hin", [P, KC, NS * 512])
    halo = dr("halo", [P, KC, NS * 2])
    gains_d = dr("gains1", [P, 3 * KC])
    cw_d = dr("cw", [P, KC * 3])
    ones_d = dr("ones", [P, 128])
    wc_d = dr("wc", [16, P, 6144]); woc_d = dr("woc", [6, P, 6144])
    wg_d = dr("wg1", [15, P, 6144]); wu_d = dr("wu1", [15, P, 6144]); wd_d = dr("wd1", [16, P, FC * 128])
    out_d = dr("outF", [P, KC, NS * 512], F32, "ExternalOutput")
    c.ones_bf = pg.sb("ones", [P, 128], BF16)
    c.rstd = pg.sb("rstd", [P, 512], F32)
    c.silu = [pg.sb(f"silu{i}", [P, 512], F32) for i in range(2)]
    gains = pg.sb("gains1", [P, 3 * KC], F32)
    cw = pg.sb("cw", [P, KC, 3], F32)
    h = pg.sb("h", [P, KC, 512], F32)
    hn = pg.sb("hn", [P, KC, 512], BF16)
    ws = WStream(pg, 3, 6144)
    sq = pg.sb("sq", [P, KC, 512], BF16)
    yT = pg.sb("yT", [P, KC, 512], BF16)
    zb = pg.sb("zb", [P, 514], F32); tmpz = pg.sb("tmpz", [P, 514], F32); acc = pg.sb("acc", [P, 512], F32)
    aT = pg.sb("aT", [P, FC, 512], BF16)
    hh = pg.sb("hh", [P, KC, 2], F32); hnh = pg.sb("hnh", [P, KC, 2], BF16)
    outn = pg.sb("outn", [P, KC, 512], F32)
    pg.dma("pool", c.ones_bf.t[:], ones_d[:, :], "c0", reads=[], writes=[c.ones_bf.all()])
    pg.dma("sp", gains.t[:], gains_d[:, :], "c1", reads=[], writes=[gains.all()])
    pg.dma("sp", cw.t[:].rearrange("p a b -> p (a b)"), cw_d[:, :], "c2", reads=[], writes=[cw.all()])
    g3 = (V(gains, 0, KC), V(gains, KC, 2 * KC), V(gains, 2 * KC, 3 * KC))
    bufs = (sq, yT, zb, tmpz, acc, aT, hh, hnh, outn)
    for i in range(NS):
        pg.dma("sp", h.t[:], hin[:, :, i * 512:(i + 1) * 512], "x", reads=[], writes=[h.all()])
        emit_l1_slot(pg, c, ws, h, hn, bufs, g3, cw, wc_d, woc_d, wg_d, wu_d, wd_d, halo[:, :, 2 * i:2 * i + 2], out_d[:, :, i * 512:(i + 1) * 512])
    pg.finish("sp")
    return pg


def _kcl(wm):
    n = wm.shape[1]
    return np.ascontiguousarray(wm.reshape(KC, P, n).transpose(1, 0, 2)).reshape(P, KC * n)


def _grp3(wm, ng):
    nf = wm.shape[1] // 128
    out = np.zeros((ng, P, 3, KC, 128), np.float32)
    w4 = wm.reshape(KC, P, nf, 128)
    for fc in range(nf):
        out[fc // 3, :, fc % 3] = w4[:, :, fc, :].transpose(1, 0, 2)
    return out.reshape(ng, P, 3 * 2048)


def host_consts(S):
    NKT = S // 128
    p = np.arange(P)
    cst = np.zeros((P, 512 + 2048), np.float32)
    cst[:, 0:128] = 1.0
    cst[:, 128:256] = np.eye(P)
    cst[:, 256:384] = (p[:, None] <= p[None, :])
    cst[:, 384:512] = (p[:, None] > p[None, :])
    xw = np.arange(2048)
    cst[:32, 512:] = (xw[None, :] // 64 == np.arange(32)[:, None])
    cstf = np.zeros((P, 1035), np.float32)
    cstf[:, 0:1024] = (16.0 * np.arange(1024) + 15.5)[None, :]
    cc = np.arange(8)
    cstf[:, 1024:1032] = np.where(p[:, None] >= 16 * cc[None, :] + 15, 0.0, NEG)
    fp = np.zeros((P, 3), np.float32)
    fp[:64] = [1e9, 1e9, -1.0]
    fp[64:] = [0.0, 1e9, 1e9]
    cstf[:, 1032:1035] = fp
    sl = np.array(slopes(), np.float32)
    dl = np.arange(NKT)
    alibi = -(sl[None, None, :] * (127.0 - p[:, None, None] + 128.0 * dl[None, :, None]))
    return cst, cstf, np.ascontiguousarray(alibi.reshape(P, NKT * 8).astype(np.float32))


def host_tabs(r):
    tabs = np.zeros((P, 173), np.float32)
    tabs[:, 172] = 0.0 if r == 0 else 1.0
    n = np.arange(96)
    tabs[:, 0:96] = np.where(n < 32 * (3 - r), NEG, 0.0)[None, :]
    j0 = 8 * (3 - r)
    j = np.arange(32)
    fx = np.where(j < j0, -3.0, np.where(j == j0, 1e9, 0.0))
    tabs[:, 96:128] = fx[None, :]
    kt = np.arange(12)
    tabs[:, 128:140] = (kt >= 4 * (3 - r)).astype(np.float32)[None, :]
    tabs[:, 140:172] = (j >= j0).astype(np.float32)[None, :]
    return tabs


def host_prep_l0(inp, S):
    f = lambda a: np.asarray(a, np.float32)
    w = f(inp["w_in_ab"])[0]
    sh = {}
    sh["wkvf"] = _kcl(np.concatenate([w[:, 3072:3328], w[:, 3328:3584], w[:, 3584:3840], w[:, 4096:4352]], 1))
    sh["wkvt"] = _kcl(np.concatenate([w[:, 3840:4096], w[:, 4352:4608]], 1))
    sh["w1k"] = np.ascontiguousarray(f(inp["cmp_w1_k"])[0].transpose(1, 0, 2)).reshape(P, 32 * 128)
    sh["w1v"] = np.ascontiguousarray(f(inp["cmp_w1_v"])[0].transpose(1, 0, 2)).reshape(P, 32 * 128)
    sh["w2k"] = np.ascontiguousarray(f(inp["cmp_w2_k"])[0])
    sh["w2v"] = np.ascontiguousarray(f(inp["cmp_w2_v"])[0])
    sh["pek"] = np.ascontiguousarray(f(inp["cmp_pe_k"])[0].T)
    sh["pev"] = np.ascontiguousarray(f(inp["cmp_pe_v"])[0].T)
    sh["gains"] = np.ascontiguousarray(np.concatenate([f(inp["norm_mix"])[0].reshape(KC, P).T, f(inp["norm_ffn"])[0].reshape(KC, P).T], 1))
    uq = [w[:, 256 * i:256 * (i + 1)] for i in range(4)] + [w[:, 2048 + 256 * i:2048 + 256 * (i + 1)] for i in range(4)]
    sh["wuq"] = np.stack([_kcl(a) for a in uq])
    sh["wv"] = np.stack([_kcl(w[:, 1024 + 256 * i:1024 + 256 * (i + 1)]) for i in range(4)])
    sh["wgt"] = _kcl(w[:, 4608:4632])
    sh["wout"] = _grp3(f(inp["w_out_ab"])[0], 6)
    sh["wg"] = _grp3(f(inp["w_gate"])[0], 15)
    sh["wu"] = _grp3(f(inp["w_up"])[0], 15)
    wd = f(inp["w_down"])[0]
    sh["wd"] = np.ascontiguousarray(wd.reshape(FC, P, KC, 128).transpose(2, 1, 0, 3)).reshape(KC, P, FC * 128)
    sh["wcT"] = np.ascontiguousarray(f(inp["sgu_w"])[0].transpose(2, 0, 1)).reshape(P, 1024)
    sh["sgug"] = np.ascontiguousarray(np.broadcast_to(f(inp["sgu_g"])[0].reshape(1, 1024), (P, 1024)))
    sh["sgub"] = np.ascontiguousarray(np.broadcast_to(f(inp["sgu_b"])[0].reshape(1, 1024), (P, 1024)))
    cst, cstf, alibi = host_consts(S)
    sh["cst"] = cst; sh["cstf"] = cstf; sh["alibi"] = alibi
    x = f(inp["x"])
    maps = []
    for c in range(8):
        b, r = c // 4, c % 4
        pad = 512 * (3 - r)
        xs = np.zeros((S, D), np.float32)
        xs[pad:] = x[b, :S - pad]
        m = dict(sh)
        m["xT"] = np.ascontiguousarray(xs.reshape(S, KC, P).transpose(2, 1, 0))
        m["tabs"] = host_tabs(r)
        maps.append(m)
    return maps


def host_prep_l1(inp, S, l0_outs):
    f = lambda a: np.asarray(a, np.float32)
    NS = S // 2048
    sh = {}
    sh["gains1"] = np.ascontiguousarray(np.concatenate([f(inp["norm_mix"])[1].reshape(KC, P).T, f(inp["norm_ffn"])[1].reshape(KC, P).T,
                                                        f(inp["norm_f"]).reshape(KC, P).T], 1))
    sh["cw"] = np.ascontiguousarray(f(inp["conv_w"])[0].reshape(3, KC, P).transpose(2, 1, 0)).reshape(P, KC * 3)
    sh["ones"] = np.ones((P, 128), np.float32)
    wc = f(inp["w_in_c"])[0]
    w4 = wc.reshape(KC, P, 3, KC, 128)
    sh["wc"] = np.ascontiguousarray(w4.transpose(3, 1, 2, 0, 4)).reshape(KC, P, 6144)
    sh["woc"] = _grp3(f(inp["w_out_c"])[0], 6)
    sh["wg1"] = _grp3(f(inp["w_gate"])[1], 15)
    sh["wu1"] = _grp3(f(inp["w_up"])[1], 15)
    wd = f(inp["w_down"])[1]
    sh["wd1"] = np.ascontiguousarray(wd.reshape(FC, P, KC, 128).transpose(2, 1, 0, 3)).reshape(KC, P, FC * 128)
    maps = []
    if l0_outs is None:
        return [dict(sh) for _ in range(8)]
    for c in range(8):
        b, r = c // 4, c % 4
        m = dict(sh)
        m["hin"] = np.ascontiguousarray(l0_outs[c])
        halo = np.zeros((P, KC, NS * 2), np.float32)
        for i in range(NS):
            if r > 0:
                src, si = l0_outs[b * 4 + r - 1], i
            elif i > 0:
                src, si = l0_outs[b * 4 + 3], i - 1
            else:
                continue
            halo[:, :, 2 * i:2 * i + 2] = src[:, :, si * 512 + 510:si * 512 + 512]
        m["halo"] = halo
        maps.append(m)
    return maps


_CACHE = {}


def run_fused(inp, S):
    NS = S // 2048
    if ("f", S) not in _CACHE:
        _CACHE[("f", S)] = build_l0(S, with_l1=True)
    pg = _CACHE[("f", S)]
    maps0 = host_prep_l0(inp, S)
    maps1 = host_prep_l1(inp, S, None)
    maps = []
    for a, b in zip(maps0, maps1):
        m = dict(a); m.update(b); maps.append(m)
    maps = pg.filter_inputs(maps)
    res = run_bass_kernel_spmd(pg.nc, maps, core_ids=list(range(8)))
    out = np.zeros((2, S, D), np.float32)
    for c in range(8):
        b, r = c // 4, c % 4
        y = np.asarray(res.results[c]["outF"])
        for i in range(NS):
            t0 = 512 * (4 * i + r)
            out[b, t0:t0 + 512] = y[:, :, i * 512:(i + 1) * 512].transpose(2, 1, 0).reshape(512, D)
    return out


def run_model(inp, S):
    NS = S // 2048
    if ("l0", S) not in _CACHE:
        _CACHE[("l0", S)] = build_l0(S)
        _CACHE[("l1", S)] = build_l1(S)
    pg0 = _CACHE[("l0", S)]
    pg1 = _CACHE[("l1", S)]
    maps0 = pg0.filter_inputs(host_prep_l0(inp, S))
    res0 = run_bass_kernel_spmd(pg0.nc, maps0, core_ids=list(range(8)))
    l0_outs = [np.asarray(res0.results[c]["outT"]) for c in range(8)]
    del maps0
    maps1 = pg1.filter_inputs(host_prep_l1(inp, S, l0_outs))
    res1 = run_bass_kernel_spmd(pg1.nc, maps1, core_ids=list(range(8)))
    out = np.zeros((2, S, D), np.float32)
    for c in range(8):
        b, r = c // 4, c % 4
        y = np.asarray(res1.results[c]["outF"])
        for i in range(NS):
            t0 = 512 * (4 * i + r)
            out[b, t0:t0 + 512] = y[:, :, i * 512:(i + 1) * 512].transpose(2, 1, 0).reshape(512, D)
    return out


def kernel(**inputs):
    S = int(np.asarray(inputs["x"]).shape[1])
    return run_fused(inputs, S)
```

```python
import numpy as np
import ml_dtypes
import concourse.bass as bass
import concourse.mybir as mybir
from concourse.bass_utils import run_bass_kernel_spmd

F32 = mybir.dt.float32
BF16 = mybir.dt.bfloat16
AF = mybir.ActivationFunctionType
ALU = mybir.AluOpType
AX = mybir.AxisListType
P = 128
D = 2048
KC = 16
DFF = 5632
FC = DFF // 128
EPS = 1e-6
SB_LO = 16512
SB_HI = 229344
SAME_ENG_SYNC = True
NEG = -1.0e30


def _dsz(dt):
    return 4 if dt == F32 else 2


class T:
    def __init__(self, t, space, off, nbytes, dt):
        self.t = t
        self.space = space
        self.off = off
        self.nbytes = nbytes
        self.dt = dt

    def all(self):
        return (self.space, self.off, self.off + self.nbytes)

    def r(self, lo, hi):
        s = _dsz(self.dt)
        return (self.space, self.off + lo * s, self.off + hi * s)

    def __getitem__(self, k):
        return self.t[k]


class Prog:
    def __init__(self):
        self.nc = bass.Bass("TRN2", target_bir_lowering=False)
        nc = self.nc
        self.E = {"pe": nc.tensor, "act": nc.scalar, "dve": nc.vector, "pool": nc.gpsimd, "sp": nc.sync}
        self.sem = {}
        self.cnt = {}
        self.waited = {e: {} for e in self.E}
        self.recs = {}
        self.sb_off = SB_LO
        self.n_ops = 0
        self.ps_banks = [nc.alloc_psum_tensor(f"psb{i}", [P, 512], F32) for i in range(8)]
        self.dram_names = {}
        self.epoch = {}

    def sb(self, name, shape, dt, at=None):
        n = 1
        for s in shape[1:]:
            n *= s
        nbytes = n * _dsz(dt)
        if at is None:
            off = (self.sb_off + 31) // 32 * 32
            self.sb_off = off + nbytes
            assert self.sb_off <= SB_HI, f"SBUF overflow at {name}: {self.sb_off}"
        else:
            off = at
            assert off + nbytes <= SB_HI and off >= SB_LO, f"SBUF overflow (at) {name}"
        t = self.nc.alloc_sbuf_tensor_at(name, list(shape), dt, offset=off)
        return T(t, "sb", off, nbytes, dt)

    def ps(self, bank, shape=None, dt=F32, col0=0):
        return self.ps_banks[bank]

    def psr(self, bank, lo=0, hi=512):
        return ("ps", bank * 2048, bank * 2048 + 2048)

    def dram(self, name, shape, dt, kind):
        t = self.nc.dram_tensor(name, list(shape), dt, kind=kind)
        self.dram_names[name] = kind
        return t.ap()

    def filter_inputs(self, maps):
        names = [n for n, k in self.dram_names.items() if k == "ExternalInput"]
        return [{n: m[n] for n in names} for m in maps]

    def _sem(self, key):
        if key not in self.sem:
            self.sem[key] = self.nc.alloc_semaphore("s_" + key.replace("#", "_e"))
            self.cnt[key] = 0
        return self.sem[key]

    def _deps(self, reads, writes):
        deps = {}
        for (sp, lo, hi) in reads:
            for rec in self.recs.get(sp, ()):
                if rec[4] and rec[0] < hi and lo < rec[1]:
                    if deps.get(rec[2], 0) < rec[3]:
                        deps[rec[2]] = rec[3]
        for (sp, lo, hi) in writes:
            for rec in self.recs.get(sp, ()):
                if rec[0] < hi and lo < rec[1]:
                    if deps.get(rec[2], 0) < rec[3]:
                        deps[rec[2]] = rec[3]
        return deps

    def _record(self, reads, writes, key, val):
        for (sp, lo, hi) in writes:
            lst = self.recs.setdefault(sp, [])
            lst[:] = [rc for rc in lst if not (lo <= rc[0] and rc[1] <= hi)]
            lst.append([lo, hi, key, val, True])
        for (sp, lo, hi) in reads:
            lst = self.recs.setdefault(sp, [])
            lst[:] = [rc for rc in lst if not ((not rc[4]) and rc[2] == key and lo <= rc[0] and rc[1] <= hi)]
            lst.append([lo, hi, key, val, False])

    def op(self, e, fn, reads=(), writes=(), signal=True, dma=None):
        self.n_ops += 1
        deps = self._deps(reads, writes)
        eng = self.E[e]
        w = self.waited[e]
        for key, val in deps.items():
            if key.split("#")[0] == e and dma is None and (e == "pe" or not SAME_ENG_SYNC):
                continue
            if w.get(key, 0) >= val:
                continue
            eng.wait_ge(self._sem(key), val)
            w[key] = val
        inst = fn()
        ek = f"{e}#{self.epoch.get(e, 0)}"
        if dma is not None:
            s = self._sem(dma)
            self.cnt[dma] += 16
            assert self.cnt[dma] < 65000
            inst.then_inc(s, 16)
            key, val = dma, self.cnt[dma]
        elif signal:
            s = self._sem(ek)
            self.cnt[ek] += 1
            inst.then_inc(s, 1)
            key, val = ek, self.cnt[ek]
            if self.cnt[ek] >= 40000:
                self.epoch[e] = self.epoch.get(e, 0) + 1
        else:
            self._sem(ek)
            key, val = ek, self.cnt[ek] + 1
        self._record(reads, writes, key, val)
        return inst

    def finish(self, eng="sp"):
        e = self.E[eng]
        for key, s in self.sem.items():
            if self.cnt[key] > 0:
                e.wait_ge(s, self.cnt[key])

    def mm(self, out, lhsT, rhs, start, stop, reads, writes):
        nc = self.nc
        return self.op("pe", lambda: nc.tensor.matmul(out, lhsT=lhsT, rhs=rhs, start=start, stop=stop),
                       reads=reads, writes=writes, signal=stop)

    def dma(self, q, out, in_, key, reads, writes):
        eng = self.E[q]
        return self.op(q, lambda: eng.dma_start(out=out, in_=in_), reads=reads, writes=writes, dma=key)


def bf(a):
    return np.asarray(a, dtype=np.float32)


class Ctx:
    pass


def emit_norm(pg, c, h, hn, sq, gain, NT, psb, stride=512):
    nc = pg.nc
    pg.op("act", lambda: nc.scalar.activation(out=sq[:, :, 0:NT], in_=h[:, :, 0:NT], func=AF.Square),
          reads=[h.all()], writes=[sq.all()])
    bank = pg.ps_banks[psb]
    for kc in range(KC):
        pg.mm(bank[:, 0:NT], c.ones_bf[:, :], sq[:, kc, 0:NT], kc == 0, kc == KC - 1,
              reads=[sq.all(), c.ones_bf.all()], writes=[pg.psr(psb, 0, NT)])
    rstd = c.rstd
    pg.op("dve", lambda: nc.vector.tensor_scalar(out=rstd[:, 0:NT], in0=bank[:, 0:NT], scalar1=1.0 / D, scalar2=EPS,
                                                 op0=ALU.mult, op1=ALU.add),
          reads=[pg.psr(psb, 0, NT)], writes=[rstd.all()])
    pg.op("act", lambda: nc.scalar.activation(out=rstd[:, 0:NT], in_=rstd[:, 0:NT], func=AF.Sqrt),
          reads=[rstd.all()], writes=[rstd.all()])
    pg.op("dve", lambda: nc.vector.reciprocal(out=rstd[:, 0:NT], in_=rstd[:, 0:NT]),
          reads=[rstd.all()], writes=[rstd.all()])
    for kc in range(KC):
        pg.op("dve", lambda kc=kc: nc.vector.scalar_tensor_tensor(out=hn[:, kc, 0:NT], in0=h[:, kc, 0:NT],
                                                                  scalar=gain[:, kc:kc + 1], in1=rstd[:, 0:NT],
                                                                  op0=ALU.mult, op1=ALU.mult),
              reads=[h.all(), rstd.all(), gain.all()], writes=[hn.r(kc * stride, kc * stride + NT)])


NT_MAX = 512


class WStream:
    def __init__(self, pg, nslots, slot_elems):
        self.pg = pg
        self.slots = [pg.sb(f"wslot{i}", [P, slot_elems], BF16) for i in range(nslots)]
        self.i = 0

    def load(self, src_ap, nelem, dep=None):
        pg = self.pg
        s = self.slots[self.i % len(self.slots)]
        k = self.i % len(self.slots)
        self.i += 1
        pg.dma("pool", s[:, 0:nelem], src_ap, f"w{k}", reads=[dep] if dep else [], writes=[s.r(0, nelem)])
        return s


def wload(ws, ent, sidx, nelem):
    if isinstance(ent, tuple):
        ap, dep = ent
        return ws.load(ap[sidx, :, 0:nelem], nelem, dep=dep)
    return ws.load(ent[sidx, :, 0:nelem], nelem)


def emit_ffn(pg, c, ws, h, hn, aT, wg_d, wu_d, wd_d, NT):
    nc = pg.nc
    GRP = 3
    fcs = list(range(FC))
    for g0 in range(0, FC, GRP):
        grp = fcs[g0:g0 + GRP]
        n = len(grp)
        sg = wload(ws, wg_d, g0 // GRP, n * 2048)
        su = wload(ws, wu_d, g0 // GRP, n * 2048)
        for j, fc in enumerate(grp):
            bg = 0 + (fc % 2)
            bu = 2 + (fc % 2)
            for kc in range(KC):
                pg.mm(pg.ps_banks[bg][:, 0:NT], sg[:, j * 2048 + kc * 128: j * 2048 + (kc + 1) * 128], hn[:, kc, 0:NT],
                      kc == 0, kc == KC - 1, reads=[sg.r(j * 2048, (j + 1) * 2048), hn.all()], writes=[pg.psr(bg, 0, NT)])
            for kc in range(KC):
                pg.mm(pg.ps_banks[bu][:, 0:NT], su[:, j * 2048 + kc * 128: j * 2048 + (kc + 1) * 128], hn[:, kc, 0:NT],
                      kc == 0, kc == KC - 1, reads=[su.r(j * 2048, (j + 1) * 2048), hn.all()], writes=[pg.psr(bu, 0, NT)])
            sl = c.silu[fc % 2]
            pg.op("act", lambda bg=bg, sl=sl: nc.scalar.activation(out=sl[:, 0:NT], in_=pg.ps_banks[bg][:, 0:NT], func=AF.Silu),
                  reads=[pg.psr(bg, 0, NT)], writes=[sl.all()])
            pg.op("dve", lambda bu=bu, sl=sl, fc=fc: nc.vector.tensor_tensor(out=aT[:, fc, 0:NT], in0=pg.ps_banks[bu][:, 0:NT],
                                                                             in1=sl[:, 0:NT], op=ALU.mult),
                  reads=[pg.psr(bu, 0, NT), sl.all()], writes=[aT.r(fc * NT_MAX, fc * NT_MAX + NT)])
    for oc in range(KC):
        sd = wload(ws, wd_d, oc, FC * 128)
        b = 4 + (oc % 2)
        for k in range(FC):
            pg.mm(pg.ps_banks[b][:, 0:NT], sd[:, k * 128:(k + 1) * 128], aT[:, k, 0:NT], k == 0, k == FC - 1,
                  reads=[sd.r(0, FC * 128), aT.all()], writes=[pg.psr(b, 0, NT)])
        pg.op("dve", lambda b=b, oc=oc: nc.vector.tensor_tensor(out=h[:, oc, 0:NT], in0=pg.ps_banks[b][:, 0:NT],
                                                                in1=h[:, oc, 0:NT], op=ALU.add),
              reads=[pg.psr(b, 0, NT), h.r(oc * NT_MAX, oc * NT_MAX + NT)], writes=[h.r(oc * NT_MAX, oc * NT_MAX + NT)])


def slopes():
    return [2.0 ** (-8.0 * (i + 1) / 8) for i in range(8)]


def build_l0(S, with_l1=False, debug=None):
    NS = S // 2048
    NKT = S // 128
    NT = 512
    pg = Prog()
    nc = pg.nc
    c = Ctx()
    dr = lambda name, shape, dt=F32, kind="ExternalInput": pg.dram(name, shape, dt, kind)
    xT = dr("xT", [P, KC, S])
    wkvf_d = dr("wkvf", [P, KC * 1024])
    wkvt_d = dr("wkvt", [P, KC * 512])
    w1k_d = dr("w1k", [P, 32 * 128]); w1v_d = dr("w1v", [P, 32 * 128])
    w2k_d = dr("w2k", [P, 128]); w2v_d = dr("w2v", [P, 128])
    pek_d = dr("pek", [P, 32]); pev_d = dr("pev", [P, 32])
    gains_d = dr("gains", [P, 2 * KC])
    wgt_d = dr("wgt", [P, KC * 24])
    wcT_d = dr("wcT", [P, 8 * 128])
    sgug_d = dr("sgug", [P, 1024]); sgub_d = dr("sgub", [P, 1024])
    tabs_d = dr("tabs", [P, 173])
    alibi_d = dr("alibi", [P, NKT * 8])
    cst_d = dr("cst", [P, 128 * 4 + 2048])
    cstf_d = dr("cstf", [P, 1024 + 8 + 3])
    scr_f = dr("scr_f", [8, P, S], BF16, "ExternalOutput" if debug else "Internal")
    scr_t = dr("scr_t", [S, 4, 128], BF16, "ExternalOutput" if debug else "Internal")

    c.cst = pg.sb("cst", [P, 128 * 4 + 2048], BF16)
    cst = c.cst

    class V:
        def __init__(s, base, lo, hi, shape=None):
            s.base = base; s.lo = lo; s.hi = hi
        def all(s):
            return s.base.r(s.lo, s.hi)
        def __getitem__(s, k):
            return s.base.t[:, s.lo:s.hi][k]
    c.ones_bf = V(cst, 0, 128); c.ident = V(cst, 128, 256); c.tril = V(cst, 256, 384); c.wfirst = V(cst, 384, 512)
    c.wide = V(cst, 512, 2560)
    c.cstf = pg.sb("cstf", [P, 1024 + 8 + 3], F32)
    c.mid = V(c.cstf, 0, 1024); c.m8 = V(c.cstf, 1024, 1032); c.fpat = V(c.cstf, 1032, 1035)
    c.tabs = pg.sb("tabs", [P, 173], F32)
    c.cex = V(c.tabs, 0, 96); c.fx = V(c.tabs, 96, 128); c.wex = V(c.tabs, 128, 140); c.exm = V(c.tabs, 140, 172); c.hex = V(c.tabs, 172, 173)
    c.alibi = pg.sb("alibi", [P, NKT, 8], F32)
    c.gains = pg.sb("gains", [P, 2 * KC], F32)
    c.wcT = pg.sb("wcT", [P, 8, 128], BF16)
    c.sgug = pg.sb("sgug", [P, 1024], F32); c.sgub = pg.sb("sgub", [P, 8, 128], F32)
    c.wgt = pg.sb("wgt", [P, KC, 24], BF16)
    c.kcmpT = pg.sb("kcmpT", [P, 2, 1024], BF16)
    c.vcmp = pg.sb("vcmp", [P, 2, 8, 128], BF16)
    c.rstd = pg.sb("rstd", [P, 512], F32)
    c.silu = [pg.sb(f"silu{i}", [P, 512], F32) for i in range(2)]
    h = pg.sb("h", [P, KC, 512], F32)
    hn = pg.sb("hn", [P, KC, 512], BF16)
    ZONE = (pg.sb_off + 31) // 32 * 32
    ws = WStream(pg, 3, 6144)
    PL = (pg.sb_off + 31) // 32 * 32
    print("persistent bytes", ZONE - SB_LO, "PL start", PL - SB_LO, "PL size", SB_HI - PL)

    def ld(q, dst, src, key):
        pg.dma(q, dst.t[:] if isinstance(dst, T) else dst, src, key, reads=[], writes=[dst.all()])
    ld("pool", c.cst, cst_d[:, :], "c0")
    ld("sp", c.cstf, cstf_d[:, :], "c1")
    ld("sp", c.tabs, tabs_d[:, :], "c2")
    pg.dma("sp", c.alibi.t[:].rearrange("p a b -> p (a b)"), alibi_d[:, :], "c3", reads=[], writes=[c.alibi.all()])
    ld("sp", c.gains, gains_d[:, :], "c4")
    pg.dma("pool", c.wcT.t[:].rearrange("p a b -> p (a b)"), wcT_d[:, :], "c5", reads=[], writes=[c.wcT.all()])
    ld("sp", c.sgug, sgug_d[:, :], "c6")
    pg.dma("sp", c.sgub.t[:].rearrange("p a b -> p (a b)"), sgub_d[:, :], "c7", reads=[], writes=[c.sgub.all()])
    pg.dma("pool", c.wgt.t[:].rearrange("p a b -> p (a b)"), wgt_d[:, :], "c8", reads=[], writes=[c.wgt.all()])
    pg.op("pool", lambda: nc.gpsimd.affine_select(out=c.wcT[:, :, :], in_=c.wcT[:, :, :], pattern=[[0, 8], [1, 128]],
                                                   compare_op=ALU.is_ge, fill=0.0, base=0, channel_multiplier=-1),
          reads=[c.wcT.all()], writes=[c.wcT.all()])
    pg.op("dve", lambda: nc.vector.memset(c.kcmpT.t[:], 0.0), writes=[c.kcmpT.all()])
    pg.op("dve", lambda: nc.vector.memset(c.vcmp.t[:], 0.0), writes=[c.vcmp.all()])
    g_mix = V(c.gains, 0, KC); g_ffn = V(c.gains, KC, 2 * KC)

    o = ZONE
    wkvf = pg.sb("wkvf", [P, KC, 1024], BF16, at=o); o += KC * 1024 * 2
    wkvt = pg.sb("wkvt", [P, KC, 512], BF16, at=o); o += KC * 512 * 2
    sq = pg.sb("sq0", [P, KC, 512], BF16, at=o); o += KC * 512 * 2
    stf = pg.sb("stf", [P, 8, 512], BF16, at=o); o += 8 * 512 * 2
    stt = pg.sb("stt", [P, 4, 512], BF16, at=o); o += 4 * 512 * 2
    pg.dma("pool", wkvf.t[:].rearrange("p a b -> p (a b)"), wkvf_d[:, :], "c9", reads=[], writes=[wkvf.all()])
    pg.dma("pool", wkvt.t[:].rearrange("p a b -> p (a b)"), wkvt_d[:, :], "c10", reads=[], writes=[wkvt.all()])
    went = None
    conv = []
    if with_l1:
        went = {}
        for nm, shp in (("wuq", [8, P, KC * 256]), ("wv", [4, P, KC * 256]), ("wout", [6, P, 6144]), ("wg", [15, P, 6144]),
                        ("wu", [15, P, 6144]), ("wd", [16, P, FC * 128]), ("wc", [16, P, 6144]), ("woc", [6, P, 6144]),
                        ("wg1", [15, P, 6144]), ("wu1", [15, P, 6144]), ("wd1", [16, P, FC * 128])):
            src = dr(nm, shp)
            dst = dr(nm + "_b", shp, BF16, "Internal")
            went[nm] = (dst, ("wconv", 0, 1))
            for sidx in range(shp[0]):
                conv.append((dst, src, sidx))
    n_t0 = S // 512
    per_tile = (len(conv) + n_t0 - 1) // n_t0
    for j in range(S // 512):
        pg.dma("sp", h.t[:], xT[:, :, 512 * j:512 * j + 512], "x", reads=[], writes=[h.all()])
        emit_norm(pg, c, h, hn, sq, g_mix, 512, 7)
        for fcn in range(8):
            b = fcn % 2
            for kc in range(KC):
                pg.mm(pg.ps_banks[b][:, :], wkvf[:, kc, fcn * 128:(fcn + 1) * 128], hn[:, kc, :], kc == 0, kc == KC - 1,
                      reads=[wkvf.all(), hn.all()], writes=[pg.psr(b)])
            if fcn % 2 == 0:
                pg.op("act", lambda b=b, fcn=fcn: nc.scalar.copy(out=stf[:, fcn, :], in_=pg.ps_banks[b][:, :]),
                      reads=[pg.psr(b)], writes=[stf.r(fcn * 512, fcn * 512 + 512)])
            else:
                pg.op("dve", lambda b=b, fcn=fcn: nc.vector.tensor_copy(out=stf[:, fcn, :], in_=pg.ps_banks[b][:, :]),
                      reads=[pg.psr(b)], writes=[stf.r(fcn * 512, fcn * 512 + 512)])
        pg.dma("pool", scr_f.rearrange("f p t -> p f t")[:, :, 512 * j:512 * j + 512], stf.t[:], "sf",
               reads=[stf.all()], writes=[("scr_f", 512 * j, 512 * j + 512)])
        for ts in range(4):
            b = 2 + ts % 2
            for kc in range(KC):
                pg.mm(pg.ps_banks[b][:, :], hn[:, kc, ts * 128:(ts + 1) * 128], wkvt[:, kc, :], kc == 0, kc == KC - 1,
                      reads=[wkvt.all(), hn.all()], writes=[pg.psr(b)])
            if ts % 2 == 0:
                pg.op("act", lambda b=b, ts=ts: nc.scalar.copy(out=stt[:, ts, :], in_=pg.ps_banks[b][:, :]),
                      reads=[pg.psr(b)], writes=[stt.r(ts * 512, ts * 512 + 512)])
            else:
                pg.op("dve", lambda b=b, ts=ts: nc.vector.tensor_copy(out=stt[:, ts, :], in_=pg.ps_banks[b][:, :]),
                      reads=[pg.psr(b)], writes=[stt.r(ts * 512, ts * 512 + 512)])
        pg.dma("pool", scr_t[512 * j:512 * j + 512, :, :].rearrange("(ts p) i d -> p ts (i d)", p=128), stt.t[:], "st",
               reads=[stt.all()], writes=[("scr_t", 512 * j, 512 * j + 512)])
        for (dst, src, sidx) in conv[j * per_tile:(j + 1) * per_tile]:
            pg.dma("pool", dst[sidx, :, :], src[sidx, :, :], "cv", reads=[], writes=[("wconv", 0, 1)])

    o = ZONE
    w1 = [pg.sb("w1k", [P, 32, 128], BF16, at=o), pg.sb("w1v", [P, 32, 128], BF16, at=o + 8192)]; o += 16384
    w2 = [pg.sb("w2k", [P, 128], BF16, at=o), pg.sb("w2v", [P, 128], BF16, at=o + 256)]; o += 512
    pe = [pg.sb("pek", [P, 32], BF16, at=o), pg.sb("pev", [P, 32], BF16, at=o + 64)]; o += 128
    peb = [pg.sb("pebk", [P, 1], F32, at=o), pg.sb("pebv", [P, 1], F32, at=o + 32)]; o += 64
    hid = pg.sb("hid", [P, 256], BF16, at=o); o += 512
    cwin = pg.sb("cwin", [P, 4, 2064], BF16, at=o); o += 4 * 2064 * 2
    for i, (a, b_, cc, d_) in enumerate([(w1[0], w1k_d, "d0", None), (w1[1], w1v_d, "d1", None)]):
        pg.dma("pool", a.t[:].rearrange("p a b -> p (a b)"), b_[:, :], cc, reads=[], writes=[a.all()])
    pg.dma("pool", w2[0].t[:], w2k_d[:, :], "d2", reads=[], writes=[w2[0].all()])
    pg.dma("pool", w2[1].t[:], w2v_d[:, :], "d3", reads=[], writes=[w2[1].all()])
    pg.dma("pool", pe[0].t[:], pek_d[:, :], "d4", reads=[], writes=[pe[0].all()])
    pg.dma("pool", pe[1].t[:], pev_d[:, :], "d5", reads=[], writes=[pe[1].all()])
    for wh in range(2):
        for l in range(32):
            pg.mm(pg.ps_banks[6][:, 0:1], w1[wh][:, l, :], pe[wh][:, l:l + 1], l == 0, l == 31,
                  reads=[w1[wh].all(), pe[wh].all()], writes=[pg.psr(6, 0, 1)])
        pg.op("dve", lambda wh=wh: nc.vector.tensor_copy(out=peb[wh][:, :], in_=pg.ps_banks[6][:, 0:1]),
              reads=[pg.psr(6, 0, 1)], writes=[peb[wh].all()])
    for m in range(S // 2048):
        last = (m == S // 2048 - 1)
        ncol = 2048 if last else 2064
        nblk = 127 if last else 128
        pg.dma("sp", cwin[:, :, 0:ncol], scr_f[0:4, :, 2048 * m:2048 * m + ncol].rearrange("f p t -> p f t"), "cw",
               reads=[("scr_f", 0, S)], writes=[cwin.all()])
        for wh in range(2):
            for g in range(2):
                idx = wh * 2 + g
                for l in range(32):
                    pg.mm(pg.ps_banks[g][:, 0:nblk], w1[wh][:, l, :], cwin[:, idx, l:l + 16 * (nblk - 1) + 1:16],
                          l == 0, l == 31, reads=[w1[wh].all(), cwin.all()], writes=[pg.psr(g, 0, nblk)])
                pg.op("act", lambda g=g, wh=wh: nc.scalar.activation(out=hid[:, g * 128:g * 128 + nblk], in_=pg.ps_banks[g][:, 0:nblk],
                                                                    func=AF.Gelu_apprx_tanh, bias=peb[wh][:, 0:1], scale=1.0),
                      reads=[pg.psr(g, 0, nblk), peb[wh].all()], writes=[hid.r(g * 128, g * 128 + nblk)])
                if wh == 0:
                    pg.mm(pg.ps_banks[2 + g][:, 0:nblk], w2[0][:, :], hid[:, g * 128:g * 128 + nblk], True, True,
                          reads=[w2[0].all(), hid.r(g * 128, g * 128 + nblk)], writes=[pg.psr(2 + g, 0, nblk)])
                    pg.op("dve", lambda g=g, m=m: nc.vector.tensor_copy(out=c.kcmpT[:, g, 128 * m:128 * m + nblk], in_=pg.ps_banks[2 + g][:, 0:nblk]),
                          reads=[pg.psr(2 + g, 0, nblk)], writes=[c.kcmpT.all()])
                else:
                    pg.mm(pg.ps_banks[2 + g][0:nblk, 0:128], hid[:, g * 128:g * 128 + nblk], w2[1][:, :], True, True,
                          reads=[w2[1].all(), hid.r(g * 128, g * 128 + nblk)], writes=[pg.psr(2 + g, 0, 128)])
                    pg.op("dve", lambda g=g, m=m: nc.vector.tensor_copy(out=c.vcmp[0:nblk, g, m, :], in_=pg.ps_banks[2 + g][0:nblk, 0:128]),
                          reads=[pg.psr(2 + g, 0, 128)], writes=[c.vcmp.all()])
    if debug == "p0":
        dbg_k = dr("dbg_k", [P, 2048], BF16, "ExternalOutput")
        dbg_v = dr("dbg_v", [P, 2048], BF16, "ExternalOutput")
        pg.dma("sp", dbg_k[:, :], c.kcmpT.t[:].rearrange("p a b -> p (a b)"), "dbg", reads=[c.kcmpT.all()], writes=[])
        pg.dma("sp", dbg_v[:, :], c.vcmp.t[:].rearrange("p a b c -> p (a b c)"), "dbg", reads=[c.vcmp.all()], writes=[])
        pg.finish("sp")
        return pg
    emit_l0_main(pg, c, dr, ws, h, hn, g_mix, g_ffn, xT, scr_f, scr_t, S, PL, debug, fused=with_l1, went=went)
    pg.finish("sp")
    return pg


def emit_l0_main(pg, c, dr, ws, h, hn, g_mix, g_ffn, xT, scr_f, scr_t, S, PL, debug, fused=False, went=None):
    nc = pg.nc
    NS = S // 2048
    NKT = S // 128
    SL = slopes()
    SCALE = 128.0 ** -0.5
    full = debug in (None, "h1", "h2")
    if went is not None:
        wuq_d, wv_d, wout_d, wg_d, wu_d, wd_d = (went[k] for k in ("wuq", "wv", "wout", "wg", "wu", "wd"))
    else:
        wuq_d = dr("wuq", [8, P, KC * 256])
        wv_d = dr("wv", [4, P, KC * 256])
        if full:
            wout_d = dr("wout", [6, P, 3 * 2048])
        if debug in (None, "h2"):
            wg_d = dr("wg", [15, P, 3 * 2048]); wu_d = dr("wu", [15, P, 3 * 2048]); wd_d = dr("wd", [16, P, FC * 128])
    if debug in (None, "h1", "h2") and not fused:
        out_d = dr("outT", [P, KC, NS * 512], F32, "ExternalOutput")
    if fused:
        gains1_d = dr("gains1", [P, 3 * KC]); cw_d = dr("cw", [P, KC * 3])
        wc_d, woc_d, wg1_d, wu1_d, wd1_d = (went[k] for k in ("wc", "woc", "wg1", "wu1", "wd1"))
        outF_d = dr("outF", [P, KC, NS * 512], F32, "ExternalOutput")
    if debug == "y":
        dbg_y = dr("dbg_y", [P, 16, NS * 512], BF16, "ExternalOutput")
        dbg_g = dr("dbg_g", [P, NS * 4 * 24], F32, "ExternalOutput")
    o = [PL]

    def pl(name, shape, dt, at=None):
        if at is None:
            off = (o[0] + 31) // 32 * 32
            t = pg.sb(name, shape, dt, at=off)
            o[0] = off + t.nbytes
        else:
            t = pg.sb(name, shape, dt, at=at)
        return t
    uT = pl("uT", [P, 8, 512], BF16)
    qT = pl("qT", [P, 8, 512], BF16)
    ybT = pl("ybT", [P, 8, 512], BF16)
    X0 = (o[0] + 31) // 32 * 32
    sq = pl("sq", [P, KC, 512], BF16)
    gv = pl("gv", [P, 4, 1024], F32, at=ybT.off)
    tmpv = pl("tmpv", [P, 1024], F32, at=X0 + 8192)
    vn = pl("vn", [P, 1024], BF16, at=X0 + 12288)
    s_sb = pl("s_sb", [P, 1024], F32, at=X0)
    p_sb = pl("p_sb", [P, 1024], F32, at=X0 + 4096)
    paccb = pl("paccb", [P, 1040], F32, at=X0 + 8192)
    pbf = pl("pbf", [P, 1024], BF16, at=X0 + 8192 + 4160)
    ptc = [pl(f"ptc{i}", [P, 128], BF16, at=X0 + 8192 + 4160 + 2048 + 256 * i) for i in range(2)]
    aT = pl("aT", [P, FC, 512], BF16, at=PL)
    gates = pl("gates", [P, 4, 24], F32)
    ksT = [pl(f"ksT{i}", [P, 1024], BF16) for i in range(2)]
    vsb = [pl(f"vsb{i}", [P, 8, 129], BF16) for i in range(2)]
    kwT = pl("kwT", [P, 2, 1024], BF16)
    vwb = pl("vwb", [P, 2, 8, 129], BF16)
    PT = [pl(f"PT{i}", [P, 4, 128], BF16) for i in range(3)]
    sc = pl("sc", [P, 264], F32)
    sc2 = pl("sc2", [P, 264], F32)
    t1 = pl("t1", [P, 264], F32)
    sm = pl("sm", [P, 64], F32)
    selbf = pl("selbf", [P, 256], BF16)
    selT = pl("selT", [P, 8, 128], BF16)
    ybt = pl("ybt", [P, 4, 128], F32)
    ybtb = pl("ybtb", [P, 4, 128], BF16)
    if fused:
        gains1 = pl("gains1", [P, 3 * KC], F32)
        cw1 = pl("cw1", [P, KC, 3], F32)
        hhalo = pl("hh", [P, KC, 2], F32)
        hnh = pl("hnh", [P, KC, 2], BF16)
        yT1 = pl("yT1", [P, KC, 512], BF16, at=PL)
        zb1 = pl("zb1", [P, 514], F32, at=ybT.off)
        tmpz1 = pl("tmpz1", [P, 514], F32, at=ybT.off + 2080)
        acc1 = pl("acc1", [P, 512], F32, at=ybT.off + 4160)
        outn1 = pl("outn1", [P, KC, 512], F32, at=PL)
        pg.dma("sp", gains1.t[:], gains1_d[:, :], "c11", reads=[], writes=[gains1.all()])
        pg.dma("sp", cw1.t[:].rearrange("p a b -> p (a b)"), cw_d[:, :], "c12", reads=[], writes=[cw1.all()])
        g3 = (V(gains1, 0, KC), V(gains1, KC, 2 * KC), V(gains1, 2 * KC, 3 * KC))
        l1bufs = (sq, yT1, zb1, tmpz1, acc1, aT, hhalo, hnh, outn1)
    print("PL used", o[0] - PL, "of", SB_HI - PL)
    assert aT.off + aT.nbytes <= SB_HI
    SM = lambda a, b: (sm[:, a:b], sm.r(a, b))
    bank = pg.ps_banks
    tps_bf = bank[4][:, 256:512].bitcast(BF16)
    tps_n = [0]

    def tps_slot():
        k = tps_n[0] % 4
        tps_n[0] += 1
        return tps_bf[:, k * 128:(k + 1) * 128], pg.psr(4)
    msk_n = [0]

    def msk_slot():
        k = msk_n[0] % 2
        msk_n[0] += 1
        return bank[6 + k][:, 0:128], pg.psr(6 + k)
    pt_n = [0]
    cp_n = [0]

    def evac_copy(out, in_, reads, writes):
        cp_n[0] += 1
        if cp_n[0] % 2:
            pg.op("act", lambda: nc.scalar.copy(out=out, in_=in_), reads=reads, writes=writes)
        else:
            pg.op("dve", lambda: nc.vector.tensor_copy(out=out, in_=in_), reads=reads, writes=writes)

    for vb in vsb:
        pg.op("dve", lambda vb=vb: nc.vector.memset(vb[:, :, 128:129], 1.0), writes=[vb.all()])
    pg.op("dve", lambda: nc.vector.memset(vwb[:, :, :, 128:129], 1.0), writes=[vwb.all()])

    import os as _os
    _slots = [int(v) for v in _os.environ.get('K_SLOTS', '').split(',') if v] or list(range(NS))
    tiles = []
    for i in _slots:
        if fused:
            tiles.append((i, 512 * (4 * i + 3) - 128, 16 * i + 11, 1, True))
        tiles.append((i, 512 * (4 * i + 3), 16 * i + 12, 4, False))
    for (i, T0, qb0, NQ, is_halo) in tiles:
        NT = 128 * NQ
        pg.dma("sp", h[:, :, 0:NT], xT[:, :, T0:T0 + NT], "x", reads=[], writes=[h.all()])
        emit_norm(pg, c, h, hn, sq, g_mix, NT, 7)
        for sidx in range(8):
            sl = wload(ws, wuq_d, sidx, 4096)
            for j in range(2):
                ci = 2 * sidx + j
                b = ci % 2
                for kc in range(KC):
                    pg.mm(bank[b][:, 0:NT], sl[:, kc * 256 + j * 128: kc * 256 + (j + 1) * 128], hn[:, kc, 0:NT], kc == 0, kc == KC - 1,
                          reads=[sl.r(0, 4096), hn.all()], writes=[pg.psr(b)])
                if ci < 8:
                    pg.op("act", lambda b=b, ci=ci: nc.scalar.activation(out=uT[:, ci, 0:NT], in_=bank[b][:, 0:NT], func=AF.Gelu_apprx_tanh),
                          reads=[pg.psr(b)], writes=[uT.r(ci * 512, ci * 512 + 512)])
                else:
                    pg.op("act", lambda b=b, ci=ci: nc.scalar.activation(out=qT[:, ci - 8, 0:NT], in_=bank[b][:, 0:NT], func=AF.Copy, scale=SCALE),
                          reads=[pg.psr(b)], writes=[qT.r((ci - 8) * 512, (ci - 8) * 512 + 512)])
        for qv in range(4):
            sl = wload(ws, wv_d, qv, 4096)
            for ts in range(NQ):
                b = 2 + ts % 2
                for kc in range(KC):
                    pg.mm(bank[b][:, 0:256], hn[:, kc, ts * 128:(ts + 1) * 128], sl[:, kc * 256:(kc + 1) * 256], kc == 0, kc == KC - 1,
                          reads=[sl.r(0, 4096), hn.all()], writes=[pg.psr(b, 0, 256)])
                pg.op("act", lambda b=b, ts=ts, qv=qv: nc.scalar.activation(out=gv[:, ts, qv * 256:(qv + 1) * 256], in_=bank[b][:, 0:256],
                                                                           func=AF.Gelu_apprx_tanh),
                      reads=[pg.psr(b, 0, 256)], writes=[gv.r(ts * 1024 + qv * 256, ts * 1024 + (qv + 1) * 256)])
        for ts in range(NQ):
            for kc in range(KC):
                pg.mm(bank[6][:, ts * 32:ts * 32 + 24], hn[:, kc, ts * 128:(ts + 1) * 128], c.wgt[:, kc, :], kc == 0, kc == KC - 1,
                      reads=[c.wgt.all(), hn.all()], writes=[pg.psr(6, ts * 32, ts * 32 + 24)])
            pg.op("act", lambda ts=ts: nc.scalar.activation(out=gates[:, ts, :], in_=bank[6][:, ts * 32:ts * 32 + 24], func=AF.Sigmoid),
                  reads=[pg.psr(6, ts * 32, ts * 32 + 24)], writes=[gates.r(ts * 24, ts * 24 + 24)])
        for ts in range(NQ):
            gvr = gv.r(ts * 1024, ts * 1024 + 1024)
            pg.op("dve", lambda ts=ts: nc.vector.tensor_tensor(out=tmpv[:, :], in0=gv[:, ts, :], in1=gv[:, ts, :], op=ALU.mult),
                  reads=[gvr], writes=[tmpv.all()])
            ss, ssr = SM(0, 8)
            pg.op("dve", lambda ss=ss: nc.vector.tensor_reduce(out=ss, in_=tmpv[:, :].rearrange("p (g c) -> p g c", g=8), axis=AX.X, op=ALU.add),
                  reads=[tmpv.all()], writes=[ssr])
            pg.op("dve", lambda ss=ss: nc.vector.tensor_scalar(out=ss, in0=ss, scalar1=1.0 / 128, scalar2=EPS, op0=ALU.mult, op1=ALU.add),
                  reads=[ssr], writes=[ssr])
            pg.op("act", lambda ss=ss: nc.scalar.activation(out=ss, in_=ss, func=AF.Sqrt), reads=[ssr], writes=[ssr])
            pg.op("dve", lambda ss=ss: nc.vector.reciprocal(out=ss, in_=ss), reads=[ssr], writes=[ssr])
            for g in range(8):
                pg.op("dve", lambda g=g, ts=ts: nc.vector.scalar_tensor_tensor(out=vn[:, g * 128:(g + 1) * 128], in0=gv[:, ts, g * 128:(g + 1) * 128],
                                                                               scalar=sm[:, g:g + 1], in1=c.sgug[:, g * 128:(g + 1) * 128],
                                                                               op0=ALU.mult, op1=ALU.mult),
                      reads=[gvr, ssr, c.sgug.all()], writes=[vn.r(g * 128, (g + 1) * 128)])
            for g in range(8):
                b = 2 + g // 4
                pg.mm(bank[b][:, (g % 4) * 128:(g % 4 + 1) * 128], vn[:, g * 128:(g + 1) * 128], c.wcT[:, g, :], True, True,
                      reads=[vn.r(g * 128, (g + 1) * 128), c.wcT.all()], writes=[pg.psr(b, (g % 4) * 128, (g % 4 + 1) * 128)])
            for hf in range(2):
                b = 2 + hf
                pg.op("dve", lambda b=b, hf=hf: nc.vector.tensor_tensor(out=tmpv[:, hf * 512:(hf + 1) * 512], in0=bank[b][:, :],
                                                                        in1=c.sgub[:, 4 * hf:4 * hf + 4, :].rearrange("p a b -> p (a b)"), op=ALU.add),
                      reads=[pg.psr(b), c.sgub.all()], writes=[tmpv.r(hf * 512, (hf + 1) * 512)])
                pg.op("dve", lambda hf=hf, ts=ts: nc.vector.tensor_tensor(out=uT[:, 4 * hf:4 * hf + 4, ts * 128:(ts + 1) * 128],
                                                                          in0=tmpv[:, hf * 512:(hf + 1) * 512].rearrange("p (a b) -> p a b", a=4),
                                                                          in1=uT[:, 4 * hf:4 * hf + 4, ts * 128:(ts + 1) * 128], op=ALU.mult),
                      reads=[tmpv.r(hf * 512, (hf + 1) * 512), uT.r(4 * hf * 512, (4 * hf + 4) * 512)], writes=[uT.r(4 * hf * 512, (4 * hf + 4) * 512)])
        w0 = 128 * (qb0 - 4)
        pg.dma("sp", kwT[:, :, :], scr_f[6:8, :, w0:w0 + 1024].rearrange("g p t -> p g t"), "kw",
               reads=[("scr_f", 0, S)], writes=[kwT.all()])
        for g in range(2):
            pg.dma("sp", vwb[:, g, :, 0:128], scr_t[w0:w0 + 1024, 2 + g, :].rearrange("(kt p) d -> p kt d", p=128), "vw",
                   reads=[("scr_t", 0, S)], writes=[vwb.all()])
        kvn = [0]
        _jqs = [int(v) for v in _os.environ.get('K_JQ', '').split(',') if v] or list(range(NQ))
        _stage = int(_os.environ.get('K_STAGE', '9'))
        for jq in _jqs:
            qb = qb0 + jq
            qc = slice(jq * 128, (jq + 1) * 128)
            for g in range(2):
                ncols = 8 * qb + 7
                npc = (ncols + 511) // 512
                pg.op("dve", lambda: nc.vector.memset(paccb[:, :], 0.0), writes=[paccb.all()])
                for hl in range(4):
                    hh = 4 * g + hl
                    for pc in range(npc):
                        w = min(512, ncols - 512 * pc)
                        pg.mm(bank[pc][:, 0:w], qT[:, hh, qc], c.kcmpT[:, g, 512 * pc:512 * pc + w], True, True,
                              reads=[qT.r(hh * 512, hh * 512 + 512), c.kcmpT.all()], writes=[pg.psr(pc, 0, w)])
                        pg.op("dve", lambda pc=pc, w=w, hh=hh: nc.vector.scalar_tensor_tensor(
                            out=s_sb[:, 512 * pc:512 * pc + w], in0=c.mid[:, 512 * pc:512 * pc + w], scalar=float(SL[hh]),
                            in1=bank[pc][:, 0:w], op0=ALU.mult, op1=ALU.add),
                            reads=[pg.psr(pc, 0, w), c.mid.all()], writes=[s_sb.r(512 * pc, 512 * pc + w)])
                    cw_ = min(96, ncols)
                    pg.op("dve", lambda cw_=cw_: nc.vector.tensor_tensor(out=s_sb[:, 0:cw_], in0=s_sb[:, 0:cw_], in1=c.cex[:, 0:cw_], op=ALU.add),
                          reads=[s_sb.r(0, cw_), c.cex.all()], writes=[s_sb.r(0, cw_)])
                    pg.op("dve", lambda ncols=ncols: nc.vector.tensor_tensor(out=s_sb[:, ncols - 8:ncols], in0=s_sb[:, ncols - 8:ncols], in1=c.m8[:, :], op=ALU.add),
                          reads=[s_sb.r(ncols - 8, ncols), c.m8.all()], writes=[s_sb.r(ncols - 8, ncols)])
                    mx, mxr = SM(8, 9)
                    pg.op("dve", lambda ncols=ncols, mx=mx: nc.vector.reduce_max(out=mx, in_=s_sb[:, 0:ncols], axis=AX.X),
                          reads=[s_sb.r(0, ncols)], writes=[mxr])
                    pg.op("dve", lambda mx=mx: nc.vector.tensor_scalar(out=mx, in0=mx, scalar1=-1.0e20, scalar2=-1.0, op0=ALU.max, op1=ALU.mult),
                          reads=[mxr], writes=[mxr])
                    ls, lsr = SM(9, 10)
                    pg.op("dve", lambda ls=ls: nc.vector.memset(ls, 0.0), writes=[lsr])
                    pg.op("act", lambda ncols=ncols, mx=mx, ls=ls: nc.scalar.activation(out=p_sb[:, 0:ncols], in_=s_sb[:, 0:ncols], func=AF.Exp,
                                                                                       bias=mx, scale=1.0, accum_out=ls),
                          reads=[s_sb.r(0, ncols), mxr, lsr], writes=[p_sb.r(0, ncols), lsr])
                    pg.op("dve", lambda ls=ls: nc.vector.tensor_scalar(out=ls, in0=ls, scalar1=1.0e-30, scalar2=None, op0=ALU.max),
                          reads=[lsr], writes=[lsr])
                    pg.op("dve", lambda ls=ls: nc.vector.reciprocal(out=ls, in_=ls), reads=[lsr], writes=[lsr])
                    if hl == 0:
                        pg.op("dve", lambda ncols=ncols, ls=ls: nc.vector.tensor_scalar(out=paccb[:, 1:1 + ncols], in0=p_sb[:, 0:ncols], scalar1=ls, scalar2=None, op0=ALU.mult),
                              reads=[p_sb.r(0, ncols), lsr], writes=[paccb.r(1, 1 + ncols)])
                    else:
                        pg.op("dve", lambda ncols=ncols, ls=ls: nc.vector.scalar_tensor_tensor(out=paccb[:, 1:1 + ncols], in0=p_sb[:, 0:ncols], scalar=ls,
                                                                                              in1=paccb[:, 1:1 + ncols], op0=ALU.mult, op1=ALU.add),
                              reads=[p_sb.r(0, ncols), lsr, paccb.r(1, 1 + ncols)], writes=[paccb.r(1, 1 + ncols)])
                    pg.op("act", lambda ncols=ncols, ls=ls: nc.scalar.activation(out=pbf[:, 0:ncols], in_=p_sb[:, 0:ncols], func=AF.Copy, scale=ls),
                          reads=[p_sb.r(0, ncols), lsr], writes=[pbf.r(0, ncols)])
                    ntt = (ncols + 127) // 128
                    for tt in range(ntt):
                        w = min(128, ncols - 128 * tt)
                        tp, tpr = tps_slot()
                        pg.op("pe", lambda tp=tp, tt=tt, w=w: nc.tensor.transpose(tp[0:w, :], pbf[:, 128 * tt:128 * tt + w], c.ident[:, :]),
                              reads=[pbf.r(128 * tt, 128 * tt + w), c.ident.all()], writes=[tpr])
                        pk = ptc[pt_n[0] % 2]
                        pt_n[0] += 1
                        evac_copy(pk[0:w, :], tp[0:w, :], [tpr], [pk.all()])
                        pg.mm(bank[5][:, hl * 128:(hl + 1) * 128], pk[0:w, :], c.vcmp[0:w, g, tt, :], tt == 0, tt == ntt - 1,
                              reads=[pk.all(), c.vcmp.all()], writes=[pg.psr(5, hl * 128, (hl + 1) * 128)])
                if _stage < 2:
                    continue
                W = 2 * qb + 2
                Wc = max(W, 32)
                A = paccb[:, 0:4 * W].rearrange("p (j f) -> p j f", f=4)
                pg.op("dve", lambda A=A, W=W: nc.vector.tensor_reduce(out=t1[:, 0:W], in_=A[:, :, 1:4], axis=AX.X, op=ALU.add),
                      reads=[paccb.all()], writes=[t1.r(0, W)])
                pg.op("dve", lambda A=A, W=W: nc.vector.scalar_tensor_tensor(out=t1[:, 0:W], in0=t1[:, 0:W], scalar=2.0, in1=A[:, :, 0],
                                                                             op0=ALU.mult, op1=ALU.add),
                      reads=[paccb.all(), t1.r(0, W)], writes=[t1.r(0, W)])
                if W < Wc:
                    pg.op("dve", lambda W=W, Wc=Wc: nc.vector.memset(sc[:, 1 + W:1 + Wc], -1.0), writes=[sc.r(1 + W, 1 + Wc)])
                pg.op("dve", lambda W=W: nc.vector.tensor_tensor(out=sc[:, 1:1 + W], in0=t1[:, 0:W], in1=paccb[:, 4:4 + 4 * W:4], op=ALU.add),
                      reads=[paccb.all(), t1.r(0, W)], writes=[sc.r(1, 1 + W)])
                pg.op("dve", lambda qb=qb: nc.vector.tensor_tensor(out=sc[:, 2 * qb:2 * qb + 3], in0=sc[:, 2 * qb:2 * qb + 3], in1=c.fpat[:, :], op=ALU.add),
                      reads=[sc.r(2 * qb, 2 * qb + 3), c.fpat.all()], writes=[sc.r(2 * qb, 2 * qb + 3)])
                pg.op("dve", lambda: nc.vector.tensor_tensor(out=sc[:, 1:33], in0=sc[:, 1:33], in1=c.exm[:, :], op=ALU.mult),
                      reads=[sc.r(1, 33), c.exm.all()], writes=[sc.r(1, 33)])
                pg.op("dve", lambda: nc.vector.tensor_tensor(out=sc[:, 1:33], in0=sc[:, 1:33], in1=c.fx[:, :], op=ALU.add),
                      reads=[sc.r(1, 33), c.fx.all()], writes=[sc.r(1, 33)])
                m8a, m8ar = SM(16, 24)
                m8b, m8br = SM(24, 32)
                pg.op("dve", lambda Wc=Wc, m8a=m8a: nc.vector.max(out=m8a, in_=sc[:, 1:1 + Wc]), reads=[sc.r(1, 1 + Wc)], writes=[m8ar])
                pg.op("dve", lambda Wc=Wc, m8a=m8a: nc.vector.match_replace(out=sc2[:, 1:1 + Wc], in_to_replace=m8a, in_values=sc[:, 1:1 + Wc], imm_value=-2.0),
                      reads=[sc.r(1, 1 + Wc), m8ar], writes=[sc2.r(1, 1 + Wc)])
                pg.op("dve", lambda Wc=Wc, m8b=m8b: nc.vector.max(out=m8b, in_=sc2[:, 1:1 + Wc]), reads=[sc2.r(1, 1 + Wc)], writes=[m8br])
                pg.op("dve", lambda Wc=Wc: nc.vector.tensor_scalar(out=selbf[:, 0:Wc], in0=sc[:, 1:1 + Wc], scalar1=sm[:, 31:32], scalar2=None, op0=ALU.is_ge),
                      reads=[sc.r(1, 1 + Wc), m8br], writes=[selbf.r(0, Wc)])
                nch = (Wc + 31) // 32
                for ch in range(nch):
                    w = min(32, Wc - 32 * ch)
                    tp, tpr = tps_slot()
                    pg.op("pe", lambda tp=tp, ch=ch, w=w: nc.tensor.transpose(tp[0:w, :], selbf[:, 32 * ch:32 * ch + w], c.ident[:, :]),
                          reads=[selbf.r(32 * ch, 32 * ch + w), c.ident.all()], writes=[tpr])
                    evac_copy(selT[0:w, ch, :], tp[0:w, :], [tpr], [selT.r(ch * 128, ch * 128 + 128)])

                def key_step(kmat, kreg, vmat, vreg, delta, obanks, first, last, maskmode, kt, wexcol=None):
                    b = kt % 2
                    pg.mm(bank[b][:, :], kmat, qT[:, 4 * g:4 * g + 4, qc], True, True,
                          reads=[kreg, qT.r(4 * g * 512, (4 * g + 4) * 512)], writes=[pg.psr(b)])
                    pt = PT[pt_n[0] % 3]
                    pt_n[0] += 1
                    for hl in range(4):
                        hh = 4 * g + hl
                        pg.op("act", lambda hl=hl, hh=hh, pt=pt, b=b: nc.scalar.activation(out=pt[:, hl, :], in_=bank[b][:, hl * 128:(hl + 1) * 128],
                                                                                          func=AF.Exp, bias=c.alibi[:, delta, hh:hh + 1], scale=1.0),
                              reads=[pg.psr(b, hl * 128, (hl + 1) * 128), c.alibi.all()], writes=[pt.r(hl * 128, (hl + 1) * 128)])
                    if maskmode in ("tril", "wfirst"):
                        cm = c.tril if maskmode == "tril" else c.wfirst
                        if wexcol is None:
                            pg.op("dve", lambda pt=pt, cm=cm: nc.vector.tensor_tensor(out=pt[:, :, :], in0=pt[:, :, :],
                                                                                      in1=cm[:, :].unsqueeze(1).to_broadcast([P, 4, 128]), op=ALU.mult),
                                  reads=[pt.all(), cm.all()], writes=[pt.all()])
                        else:
                            for hl in range(4):
                                pg.op("dve", lambda pt=pt, cm=cm, hl=hl: nc.vector.scalar_tensor_tensor(out=pt[:, hl, :], in0=pt[:, hl, :], scalar=c.wex[:, wexcol:wexcol + 1],
                                                                                                        in1=cm[:, :], op0=ALU.mult, op1=ALU.mult),
                                      reads=[pt.r(hl * 128, hl * 128 + 128), cm.all(), c.wex.all()], writes=[pt.r(hl * 128, hl * 128 + 128)])
                    elif maskmode == "sel":
                        ch = kt // 16
                        w = min(32, Wc - 32 * ch)
                        mk, mkr = msk_slot()
                        pg.mm(mk, c.wide[0:w, 128 * (kt % 16):128 * (kt % 16) + 128], selT[0:w, ch, :], True, True,
                              reads=[c.wide.all(), selT.r(ch * 128, ch * 128 + 128)], writes=[mkr])
                        pg.op("dve", lambda pt=pt, mk=mk: nc.vector.tensor_tensor(out=pt[:, :, :], in0=pt[:, :, :],
                                                                                  in1=mk.unsqueeze(1).to_broadcast([P, 4, 128]), op=ALU.mult),
                              reads=[pt.all(), mkr], writes=[pt.all()])
                    elif maskmode == "wex":
                        pg.op("dve", lambda pt=pt: nc.vector.tensor_scalar(out=pt[:, :, :], in0=pt[:, :, :], scalar1=c.wex[:, wexcol:wexcol + 1], scalar2=None, op0=ALU.mult),
                              reads=[pt.all(), c.wex.all()], writes=[pt.all()])
                    def _pv(pt=pt, vmat=vmat, vreg=vreg, obanks=obanks, first=first, last=last):
                        for hl in range(4):
                            ob = obanks[hl // 2]
                            col = (hl % 2) * 129
                            pg.mm(bank[ob][:, col:col + 129], pt[:, hl, :], vmat, first, last,
                                  reads=[pt.r(hl * 128, hl * 128 + 128), vreg], writes=[pg.psr(ob, col, col + 129)])
                    if pend[0] is not None:
                        pend[0]()
                    pend[0] = _pv

                if _stage < 3:
                    continue
                pend = [None]
                for kt in range(qb + 1):
                    if kt % 8 == 0:
                        cch = kt // 8
                        kb = kvn[0] % 2
                        kvn[0] += 1
                        ks_, vs_ = ksT[kb], vsb[kb]
                        pg.dma("sp", ks_[:, :], scr_f[4 + g, :, 1024 * cch:1024 * cch + 1024], f"ks{kb}",
                               reads=[("scr_f", 0, S)], writes=[ks_.all()])
                        pg.dma("sp", vs_[:, :, 0:128], scr_t[1024 * cch:1024 * cch + 1024, g, :].rearrange("(kt p) d -> p kt d", p=128), f"vs{kb}",
                               reads=[("scr_t", 0, S)], writes=[vs_.all()])
                    key_step(ks_[:, (kt % 8) * 128:(kt % 8 + 1) * 128], ks_.all(), vs_[:, kt % 8, :], vs_.all(), qb - kt, (2, 3),
                             kt == 0, kt == qb, "tril" if kt == qb else "sel", kt)
                if _stage < 4:
                    continue
                if pend[0] is not None:
                    pend[0]()
                    pend[0] = None
                for kt in range(qb - 4, qb + 1):
                    lw = kt - (qb0 - 4)
                    mode = "wfirst" if kt == qb - 4 else ("tril" if kt == qb else None)
                    wexcol = kt if kt < 12 else None
                    if mode is None and wexcol is not None:
                        mode = "wex"
                    key_step(kwT[:, g, lw * 128:(lw + 1) * 128], kwT.all(), vwb[:, g, lw, :], vwb.all(), qb - kt, (6, 7),
                             kt == qb - 4, kt == qb, mode, kt, wexcol)
                if _stage < 5:
                    continue
                if pend[0] is not None:
                    pend[0]()
                    pend[0] = None
                lsw, lswr = SM(32, 40)
                for k2, ob in enumerate((2, 3, 6, 7)):
                    pg.op("dve", lambda k2=k2, ob=ob: nc.vector.tensor_copy(out=sm[:, 32 + 2 * k2:34 + 2 * k2], in_=bank[ob][:, 128:258:129]),
                          reads=[pg.psr(ob, 0, 258)], writes=[lswr])
                pg.op("dve", lambda: nc.vector.tensor_scalar(out=sm[:, 32:40], in0=sm[:, 32:40], scalar1=1.0e-30, scalar2=None, op0=ALU.max),
                      reads=[lswr], writes=[lswr])
                pg.op("dve", lambda: nc.vector.reciprocal(out=sm[:, 32:40], in_=sm[:, 32:40]), reads=[lswr], writes=[lswr])
                gview = gates[:, jq, 12 * g:12 * g + 12].rearrange("p (h b) -> p h b", b=3)
                cf, cfr = SM(40, 48)
                pg.op("dve", lambda gview=gview: nc.vector.tensor_tensor(out=sm[:, 40:44], in0=sm[:, 32:36], in1=gview[:, :, 1], op=ALU.mult),
                      reads=[lswr, gates.all()], writes=[cfr])
                pg.op("dve", lambda gview=gview: nc.vector.tensor_tensor(out=sm[:, 44:48], in0=sm[:, 36:40], in1=gview[:, :, 2], op=ALU.mult),
                      reads=[lswr, gates.all()], writes=[cfr])
                for hl in range(4):
                    yr = ybt.r(hl * 128, hl * 128 + 128)
                    pg.op("dve", lambda hl=hl, gview=gview: nc.vector.tensor_scalar(out=ybt[:, hl, :], in0=bank[5][:, hl * 128:(hl + 1) * 128],
                                                                                    scalar1=gview[:, hl, 0:1], scalar2=None, op0=ALU.mult),
                          reads=[pg.psr(5, hl * 128, hl * 128 + 128), gates.all()], writes=[yr])
                    ob = 2 + hl // 2
                    col = (hl % 2) * 129
                    pg.op("dve", lambda hl=hl, ob=ob, col=col: nc.vector.scalar_tensor_tensor(out=ybt[:, hl, :], in0=bank[ob][:, col:col + 128], scalar=sm[:, 40 + hl:41 + hl],
                                                                                              in1=ybt[:, hl, :], op0=ALU.mult, op1=ALU.add),
                          reads=[pg.psr(ob, col, col + 128), cfr, yr], writes=[yr])
                    ob = 6 + hl // 2
                    pg.op("dve", lambda hl=hl, ob=ob, col=col: nc.vector.scalar_tensor_tensor(out=ybtb[:, hl, :], in0=bank[ob][:, col:col + 128], scalar=sm[:, 44 + hl:45 + hl],
                                                                                              in1=ybt[:, hl, :], op0=ALU.mult, op1=ALU.add),
                          reads=[pg.psr(ob, col, col + 128), cfr, yr], writes=[ybtb.r(hl * 128, hl * 128 + 128)])
                    tp, tpr = tps_slot()
                    pg.op("pe", lambda tp=tp, hl=hl: nc.tensor.transpose(tp[:, :], ybtb[:, hl, :], c.ident[:, :]),
                          reads=[ybtb.r(hl * 128, hl * 128 + 128), c.ident.all()], writes=[tpr])
                    hh = 4 * g + hl
                    evac_copy(ybT[:, hh, qc], tp[:, :], [tpr], [ybT.r(hh * 512 + jq * 128, hh * 512 + jq * 128 + 128)])
        if debug == "y":
            if is_halo:
                continue
            pg.dma("sp", dbg_y[:, 0:8, i * 512:(i + 1) * 512], uT.t[:], "dbg", reads=[uT.all()], writes=[])
            pg.dma("sp", dbg_y[:, 8:16, i * 512:(i + 1) * 512], ybT.t[:], "dbg", reads=[ybT.all()], writes=[])
            pg.dma("sp", dbg_g[:, i * 96:(i + 1) * 96], gates.t[:].rearrange("p a b -> p (a b)"), "dbg", reads=[gates.all()], writes=[])
            continue
        for gi in range(6):
            sl = wload(ws, wout_d, gi, 6144)
            for j in range(3):
                fcn = 3 * gi + j
                if fcn >= 16:
                    break
                b = fcn % 2
                for kc in range(KC):
                    rhs = uT[:, kc, 0:NT] if kc < 8 else ybT[:, kc - 8, 0:NT]
                    pg.mm(bank[b][:, 0:NT], sl[:, j * 2048 + kc * 128:j * 2048 + (kc + 1) * 128], rhs, kc == 0, kc == KC - 1,
                          reads=[sl.r(j * 2048, (j + 1) * 2048), uT.all(), ybT.all()], writes=[pg.psr(b)])
                pg.op("dve", lambda b=b, fcn=fcn: nc.vector.tensor_tensor(out=h[:, fcn, 0:NT], in0=bank[b][:, 0:NT], in1=h[:, fcn, 0:NT], op=ALU.add),
                      reads=[pg.psr(b), h.r(fcn * 512, fcn * 512 + 512)], writes=[h.r(fcn * 512, fcn * 512 + 512)])
        if debug != "h1":
            emit_norm(pg, c, h, hn, sq, g_ffn, NT, 7)
            emit_ffn(pg, c, ws, h, hn, aT, wg_d, wu_d, wd_d, NT)
        if not fused:
            pg.dma("sp", out_d[:, :, i * 512:(i + 1) * 512], h.t[:], "out", reads=[h.all()], writes=[])
        elif is_halo:
            if i == 0:
                pg.op("dve", lambda: nc.vector.tensor_scalar(out=hhalo[:, :, :], in0=h[:, :, NT - 2:NT], scalar1=c.hex[:, 0:1], scalar2=None, op0=ALU.mult),
                      reads=[h.all(), c.tabs.all()], writes=[hhalo.all()])
            else:
                pg.op("dve", lambda: nc.vector.tensor_copy(out=hhalo[:, :, :], in_=h[:, :, NT - 2:NT]), reads=[h.all()], writes=[hhalo.all()])
        else:
            emit_l1_slot(pg, c, ws, h, hn, l1bufs, g3, cw1, wc_d, woc_d, wg1_d, wu1_d, wd1_d, None, outF_d[:, :, i * 512:(i + 1) * 512])


def emit_l1_slot(pg, c, ws, h, hn, bufs, gains3, cw, wc_d, woc_d, wg_d, wu_d, wd_d, halo_src, out_dst):
    nc = pg.nc
    bank = pg.ps_banks
    sq, yT, zb, tmpz, acc, aT, hh, hnh, outn = bufs
    g_mix, g_ffn, g_f = gains3
    if halo_src is not None:
        pg.dma("sp", hh.t[:], halo_src, "halo", reads=[], writes=[hh.all()])
    emit_norm(pg, c, h, hn, sq, g_mix, 512, 7)
    emit_norm(pg, c, hh, hnh, sq, g_mix, 2, 7, stride=2)
    for j in range(KC):
        sl = wload(ws, wc_d, j, 6144)
        bb = 0 if j % 2 == 0 else 3
        for part in range(3):
            for kc in range(KC):
                pg.mm(bank[bb + part][:, :], sl[:, part * 2048 + kc * 128:part * 2048 + (kc + 1) * 128], hn[:, kc, :], kc == 0, kc == KC - 1,
                      reads=[sl.r(part * 2048, (part + 1) * 2048), hn.all()], writes=[pg.psr(bb + part)])
        for part in (1, 2):
            for kc in range(KC):
                pg.mm(bank[6][:, 2 * (part - 1):2 * part], sl[:, part * 2048 + kc * 128:part * 2048 + (kc + 1) * 128], hnh[:, kc, :], kc == 0, kc == KC - 1,
                      reads=[sl.r(part * 2048, (part + 1) * 2048), hnh.all()], writes=[pg.psr(6, 2 * (part - 1), 2 * part)])
        pg.op("act", lambda bb=bb: nc.scalar.copy(out=tmpz[:, 2:514], in_=bank[bb + 2][:, :]), reads=[pg.psr(bb + 2)], writes=[tmpz.r(2, 514)])
        pg.op("act", lambda: nc.scalar.copy(out=tmpz[:, 0:2], in_=bank[6][:, 2:4]), reads=[pg.psr(6, 2, 4)], writes=[tmpz.r(0, 2)])
        pg.op("dve", lambda bb=bb: nc.vector.tensor_tensor(out=zb[:, 2:514], in0=bank[bb + 1][:, :], in1=tmpz[:, 2:514], op=ALU.mult),
              reads=[pg.psr(bb + 1), tmpz.r(2, 514)], writes=[zb.r(2, 514)])
        pg.op("dve", lambda: nc.vector.tensor_tensor(out=zb[:, 0:2], in0=bank[6][:, 0:2], in1=tmpz[:, 0:2], op=ALU.mult),
              reads=[pg.psr(6, 0, 2), tmpz.r(0, 2)], writes=[zb.r(0, 2)])
        pg.op("dve", lambda j=j: nc.vector.tensor_scalar(out=acc[:, :], in0=zb[:, 2:514], scalar1=cw[:, j, 2:3], scalar2=None, op0=ALU.mult),
              reads=[zb.all(), cw.all()], writes=[acc.all()])
        pg.op("dve", lambda j=j: nc.vector.scalar_tensor_tensor(out=acc[:, :], in0=zb[:, 1:513], scalar=cw[:, j, 1:2], in1=acc[:, :], op0=ALU.mult, op1=ALU.add),
              reads=[zb.all(), cw.all(), acc.all()], writes=[acc.all()])
        pg.op("dve", lambda j=j: nc.vector.scalar_tensor_tensor(out=acc[:, :], in0=zb[:, 0:512], scalar=cw[:, j, 0:1], in1=acc[:, :], op0=ALU.mult, op1=ALU.add),
              reads=[zb.all(), cw.all(), acc.all()], writes=[acc.all()])
        pg.op("dve", lambda j=j, bb=bb: nc.vector.tensor_tensor(out=yT[:, j, :], in0=bank[bb][:, :], in1=acc[:, :], op=ALU.mult),
              reads=[pg.psr(bb), acc.all()], writes=[yT.r(j * 512, j * 512 + 512)])
    for gi in range(6):
        sl = wload(ws, woc_d, gi, 6144)
        for j in range(3):
            fcn = 3 * gi + j
            if fcn >= 16:
                break
            b = fcn % 2
            for kc in range(KC):
                pg.mm(bank[b][:, :], sl[:, j * 2048 + kc * 128:j * 2048 + (kc + 1) * 128], yT[:, kc, :], kc == 0, kc == KC - 1,
                      reads=[sl.r(j * 2048, (j + 1) * 2048), yT.all()], writes=[pg.psr(b)])
            pg.op("dve", lambda b=b, fcn=fcn: nc.vector.tensor_tensor(out=h[:, fcn, :], in0=bank[b][:, :], in1=h[:, fcn, :], op=ALU.add),
                  reads=[pg.psr(b), h.r(fcn * 512, fcn * 512 + 512)], writes=[h.r(fcn * 512, fcn * 512 + 512)])
    emit_norm(pg, c, h, hn, sq, g_ffn, 512, 7)
    emit_ffn(pg, c, ws, h, hn, aT, wg_d, wu_d, wd_d, 512)
    emit_norm(pg, c, h, outn, sq, g_f, 512, 7)
    pg.dma("sp", out_dst, outn.t[:], "out", reads=[outn.all()], writes=[])


class V:
    def __init__(s, base, lo, hi):
        s.base = base; s.lo = lo; s.hi = hi

    def all(s):
        return s.base.r(s.lo, s.hi)

    def __getitem__(s, k):
        return s.base.t[:, s.lo:s.hi][k]


def build_l1(S):
    NS = S // 2048
    pg = Prog()
    nc = pg.nc
    c = Ctx()
    dr = lambda name, shape, dt=F32, kind="ExternalInput": pg.dram(name, shape, dt, kind)
    hin = dr("hin", [P, KC, NS * 512])
    halo = dr("halo", [P, KC, NS * 2])
    gains_d = dr("gains1", [P, 3 * KC])
    cw_d = dr("cw", [P, KC * 3])
    ones_d = dr("ones", [P, 128])
    wc_d = dr("wc", [16, P, 6144]); woc_d = dr("woc", [6, P, 6144])
    wg_d = dr("wg1", [15, P, 6144]); wu_d = dr("wu1", [15, P, 6144]); wd_d = dr("wd1", [16, P, FC * 128])
    out_d = dr("outF", [P, KC, NS * 512], F32, "ExternalOutput")
    c.ones_bf = pg.sb("ones", [P, 128], BF16)
    c.rstd = pg.sb("rstd", [P, 512], F32)
    c.silu = [pg.sb(f"silu{i}", [P, 512], F32) for i in range(2)]
    gains = pg.sb("gains1", [P, 3 * KC], F32)
    cw = pg.sb("cw", [P, KC, 3], F32)
    h = pg.sb("h", [P, KC, 512], F32)
    hn = pg.sb("hn", [P, KC, 512], BF16)
    ws = WStream(pg, 3, 6144)
    sq = pg.sb("sq", [P, KC, 512], BF16)
    yT = pg.sb("yT", [P, KC, 512], BF16)
    zb = pg.sb("zb", [P, 514], F32); tmpz = pg.sb("tmpz", [P, 514], F32); acc = pg.sb("acc", [P, 512], F32)
    aT = pg.sb("aT", [P, FC, 512], BF16)
    hh = pg.sb("hh", [P, KC, 2], F32); hnh = pg.sb("hnh", [P, KC, 2], BF16)
    outn = pg.sb("outn", [P, KC, 512], F32)
    pg.dma("pool", c.ones_bf.t[:], ones_d[:, :], "c0", reads=[], writes=[c.ones_bf.all()])
    pg.dma("sp", gains.t[:], gains_d[:, :], "c1", reads=[], writes=[gains.all()])
    pg.dma("sp", cw.t[:].rearrange("p a b -> p (a b)"), cw_d[:, :], "c2", reads=[], writes=[cw.all()])
    g3 = (V(gains, 0, KC), V(gains, KC, 2 * KC), V(gains, 2 * KC, 3 * KC))
    bufs = (sq, yT, zb, tmpz, acc, aT, hh, hnh, outn)
    for i in range(NS):
        pg.dma("sp", h.t[:], hin[:, :, i * 512:(i + 1) * 512], "x", reads=[], writes=[h.all()])
        emit_l1_slot(pg, c, ws, h, hn, bufs, g3, cw, wc_d, woc_d, wg_d, wu_d, wd_d, halo[:, :, 2 * i:2 * i + 2], out_d[:, :, i * 512:(i + 1) * 512])
    pg.finish("sp")
    return pg


def _kcl(wm):
    n = wm.shape[1]
    return np.ascontiguousarray(wm.reshape(KC, P, n).transpose(1, 0, 2)).reshape(P, KC * n)


def _grp3(wm, ng):
    nf = wm.shape[1] // 128
    out = np.zeros((ng, P, 3, KC, 128), np.float32)
    w4 = wm.reshape(KC, P, nf, 128)
    for fc in range(nf):
        out[fc // 3, :, fc % 3] = w4[:, :, fc, :].transpose(1, 0, 2)
    return out.reshape(ng, P, 3 * 2048)


def host_consts(S):
    NKT = S // 128
    p = np.arange(P)
    cst = np.zeros((P, 512 + 2048), np.float32)
    cst[:, 0:128] = 1.0
    cst[:, 128:256] = np.eye(P)
    cst[:, 256:384] = (p[:, None] <= p[None, :])
    cst[:, 384:512] = (p[:, None] > p[None, :])
    xw = np.arange(2048)
    cst[:32, 512:] = (xw[None, :] // 64 == np.arange(32)[:, None])
    cstf = np.zeros((P, 1035), np.float32)
    cstf[:, 0:1024] = (16.0 * np.arange(1024) + 15.5)[None, :]
    cc = np.arange(8)
    cstf[:, 1024:1032] = np.where(p[:, None] >= 16 * cc[None, :] + 15, 0.0, NEG)
    fp = np.zeros((P, 3), np.float32)
    fp[:64] = [1e9, 1e9, -1.0]
    fp[64:] = [0.0, 1e9, 1e9]
    cstf[:, 1032:1035] = fp
    sl = np.array(slopes(), np.float32)
    dl = np.arange(NKT)
    alibi = -(sl[None, None, :] * (127.0 - p[:, None, None] + 128.0 * dl[None, :, None]))
    return cst, cstf, np.ascontiguousarray(alibi.reshape(P, NKT * 8).astype(np.float32))


def host_tabs(r):
    tabs = np.zeros((P, 173), np.float32)
    tabs[:, 172] = 0.0 if r == 0 else 1.0
    n = np.arange(96)
    tabs[:, 0:96] = np.where(n < 32 * (3 - r), NEG, 0.0)[None, :]
    j0 = 8 * (3 - r)
    j = np.arange(32)
    fx = np.where(j < j0, -3.0, np.where(j == j0, 1e9, 0.0))
    tabs[:, 96:128] = fx[None, :]
    kt = np.arange(12)
    tabs[:, 128:140] = (kt >= 4 * (3 - r)).astype(np.float32)[None, :]
    tabs[:, 140:172] = (j >= j0).astype(np.float32)[None, :]
    return tabs


def host_prep_l0(inp, S):
    f = lambda a: np.asarray(a, np.float32)
    w = f(inp["w_in_ab"])[0]
    sh = {}
    sh["wkvf"] = _kcl(np.concatenate([w[:, 3072:3328], w[:, 3328:3584], w[:, 3584:3840], w[:, 4096:4352]], 1))
    sh["wkvt"] = _kcl(np.concatenate([w[:, 3840:4096], w[:, 4352:4608]], 1))
    sh["w1k"] = np.ascontiguousarray(f(inp["cmp_w1_k"])[0].transpose(1, 0, 2)).reshape(P, 32 * 128)
    sh["w1v"] = np.ascontiguousarray(f(inp["cmp_w1_v"])[0].transpose(1, 0, 2)).reshape(P, 32 * 128)
    sh["w2k"] = np.ascontiguousarray(f(inp["cmp_w2_k"])[0])
    sh["w2v"] = np.ascontiguousarray(f(inp["cmp_w2_v"])[0])
    sh["pek"] = np.ascontiguousarray(f(inp["cmp_pe_k"])[0].T)
    sh["pev"] = np.ascontiguousarray(f(inp["cmp_pe_v"])[0].T)
    sh["gains"] = np.ascontiguousarray(np.concatenate([f(inp["norm_mix"])[0].reshape(KC, P).T, f(inp["norm_ffn"])[0].reshape(KC, P).T], 1))
    uq = [w[:, 256 * i:256 * (i + 1)] for i in range(4)] + [w[:, 2048 + 256 * i:2048 + 256 * (i + 1)] for i in range(4)]
    sh["wuq"] = np.stack([_kcl(a) for a in uq])
    sh["wv"] = np.stack([_kcl(w[:, 1024 + 256 * i:1024 + 256 * (i + 1)]) for i in range(4)])
    sh["wgt"] = _kcl(w[:, 4608:4632])
    sh["wout"] = _grp3(f(inp["w_out_ab"])[0], 6)
    sh["wg"] = _grp3(f(inp["w_gate"])[0], 15)
    sh["wu"] = _grp3(f(inp["w_up"])[0], 15)
    wd = f(inp["w_down"])[0]
    sh["wd"] = np.ascontiguousarray(wd.reshape(FC, P, KC, 128).transpose(2, 1, 0, 3)).reshape(KC, P, FC * 128)
    sh["wcT"] = np.ascontiguousarray(f(inp["sgu_w"])[0].transpose(2, 0, 1)).reshape(P, 1024)
    sh["sgug"] = np.ascontiguousarray(np.broadcast_to(f(inp["sgu_g"])[0].reshape(1, 1024), (P, 1024)))
    sh["sgub"] = np.ascontiguousarray(np.broadcast_to(f(inp["sgu_b"])[0].reshape(1, 1024), (P, 1024)))
    cst, cstf, alibi = host_consts(S)
    sh["cst"] = cst; sh["cstf"] = cstf; sh["alibi"] = alibi
    x = f(inp["x"])
    maps = []
    for c in range(8):
        b, r = c // 4, c % 4
        pad = 512 * (3 - r)
        xs = np.zeros((S, D), np.float32)
        xs[pad:] = x[b, :S - pad]
        m = dict(sh)
        m["xT"] = np.ascontiguousarray(xs.reshape(S, KC, P).transpose(2, 1, 0))
        m["tabs"] = host_tabs(r)
        maps.append(m)
    return maps


def host_prep_l1(inp, S, l0_outs):
    f = lambda a: np.asarray(a, np.float32)
    NS = S // 2048
    sh = {}
    sh["gains1"] = np.ascontiguousarray(np.concatenate([f(inp["norm_mix"])[1].reshape(KC, P).T, f(inp["norm_ffn"])[1].reshape(KC, P).T,
                                                        f(inp["norm_f"]).reshape(KC, P).T], 1))
    sh["cw"] = np.ascontiguousarray(f(inp["conv_w"])[0].reshape(3, KC, P).transpose(2, 1, 0)).reshape(P, KC * 3)
    sh["ones"] = np.ones((P, 128), np.float32)
    wc = f(inp["w_in_c"])[0]
    w4 = wc.reshape(KC, P, 3, KC, 128)
    sh["wc"] = np.ascontiguousarray(w4.transpose(3, 1, 2, 0, 4)).reshape(KC, P, 6144)
    sh["woc"] = _grp3(f(inp["w_out_c"])[0], 6)
    sh["wg1"] = _grp3(f(inp["w_gate"])[1], 15)
    sh["wu1"] = _grp3(f(inp["w_up"])[1], 15)
    wd = f(inp["w_down"])[1]
    sh["wd1"] = np.ascontiguousarray(wd.reshape(FC, P, KC, 128).transpose(2, 1, 0, 3)).reshape(KC, P, FC * 128)
    maps = []
    if l0_outs is None:
        return [dict(sh) for _ in range(8)]
    for c in range(8):
        b, r = c // 4, c % 4
        m = dict(sh)
        m["hin"] = np.ascontiguousarray(l0_outs[c])
        halo = np.zeros((P, KC, NS * 2), np.float32)
        for i in range(NS):
            if r > 0:
                src, si = l0_outs[b * 4 + r - 1], i
            elif i > 0:
                src, si = l0_outs[b * 4 + 3], i - 1
            else:
                continue
            halo[:, :, 2 * i:2 * i + 2] = src[:, :, si * 512 + 510:si * 512 + 512]
        m["halo"] = halo
        maps.append(m)
    return maps


_CACHE = {}


def run_fused(inp, S):
    NS = S // 2048
    if ("f", S) not in _CACHE:
        _CACHE[("f", S)] = build_l0(S, with_l1=True)
    pg = _CACHE[("f", S)]
    maps0 = host_prep_l0(inp, S)
    maps1 = host_prep_l1(inp, S, None)
    maps = []
    for a, b in zip(maps0, maps1):
        m = dict(a); m.update(b); maps.append(m)
    maps = pg.filter_inputs(maps)
    res = run_bass_kernel_spmd(pg.nc, maps, core_ids=list(range(8)))
    out = np.zeros((2, S, D), np.float32)
    for c in range(8):
        b, r = c // 4, c % 4
        y = np.asarray(res.results[c]["outF"])
        for i in range(NS):
            t0 = 512 * (4 * i + r)
            out[b, t0:t0 + 512] = y[:, :, i * 512:(i + 1) * 512].transpose(2, 1, 0).reshape(512, D)
    return out


def run_model(inp, S):
    NS = S // 2048
    if ("l0", S) not in _CACHE:
        _CACHE[("l0", S)] = build_l0(S)
        _CACHE[("l1", S)] = build_l1(S)
    pg0 = _CACHE[("l0", S)]
    pg1 = _CACHE[("l1", S)]
    maps0 = pg0.filter_inputs(host_prep_l0(inp, S))
    res0 = run_bass_kernel_spmd(pg0.nc, maps0, core_ids=list(range(8)))
    l0_outs = [np.asarray(res0.results[c]["outT"]) for c in range(8)]
    del maps0
    maps1 = pg1.filter_inputs(host_prep_l1(inp, S, l0_outs))
    res1 = run_bass_kernel_spmd(pg1.nc, maps1, core_ids=list(range(8)))
    out = np.zeros((2, S, D), np.float32)
    for c in range(8):
        b, r = c // 4, c % 4
        y = np.asarray(res1.results[c]["outF"])
        for i in range(NS):
            t0 = 512 * (4 * i + r)
            out[b, t0:t0 + 512] = y[:, :, i * 512:(i + 1) * 512].transpose(2, 1, 0).reshape(512, D)
    return out


def kernel(**inputs):
    S = int(np.asarray(inputs["x"]).shape[1])
    return run_fused(inputs, S)
```

```python
import numpy as np
import ml_dtypes
import concourse.bass as bass
import concourse.mybir as mybir
from concourse.bass_utils import run_bass_kernel_spmd

F32 = mybir.dt.float32
BF16 = mybir.dt.bfloat16
AF = mybir.ActivationFunctionType
ALU = mybir.AluOpType
AX = mybir.AxisListType
P = 128
D = 2048
KC = 16
DFF = 5632
FC = DFF // 128
EPS = 1e-6
SB_LO = 16512
SB_HI = 229344
SAME_ENG_SYNC = True
NEG = -1.0e30


def _dsz(dt):
    return 4 if dt == F32 else 2


class T:
    def __init__(self, t, space, off, nbytes, dt):
        self.t = t
        self.space = space
        self.off = off
        self.nbytes = nbytes
        self.dt = dt

    def all(self):
        return (self.space, self.off, self.off + self.nbytes)

    def r(self, lo, hi):
        s = _dsz(self.dt)
        return (self.space, self.off + lo * s, self.off + hi * s)

    def __getitem__(self, k):
        return self.t[k]


class Prog:
    def __init__(self):
        self.nc = bass.Bass("TRN2", target_bir_lowering=False)
        nc = self.nc
        self.E = {"pe": nc.tensor, "act": nc.scalar, "dve": nc.vector, "pool": nc.gpsimd, "sp": nc.sync}
        self.sem = {}
        self.cnt = {}
        self.waited = {e: {} for e in self.E}
        self.recs = {}
        self.sb_off = SB_LO
        self.n_ops = 0
        self.ps_banks = [nc.alloc_psum_tensor(f"psb{i}", [P, 512], F32) for i in range(8)]
        self.dram_names = {}
        self.epoch = {}

    def sb(self, name, shape, dt, at=None):
        n = 1
        for s in shape[1:]:
            n *= s
        nbytes = n * _dsz(dt)
        if at is None:
            off = (self.sb_off + 31) // 32 * 32
            self.sb_off = off + nbytes
            assert self.sb_off <= SB_HI, f"SBUF overflow at {name}: {self.sb_off}"
        else:
            off = at
            assert off + nbytes <= SB_HI and off >= SB_LO, f"SBUF overflow (at) {name}"
        t = self.nc.alloc_sbuf_tensor_at(name, list(shape), dt, offset=off)
        return T(t, "sb", off, nbytes, dt)

    def ps(self, bank, shape=None, dt=F32, col0=0):
        return self.ps_banks[bank]

    def psr(self, bank, lo=0, hi=512):
        return ("ps", bank * 2048, bank * 2048 + 2048)

    def dram(self, name, shape, dt, kind):
        t = self.nc.dram_tensor(name, list(shape), dt, kind=kind)
        self.dram_names[name] = kind
        return t.ap()

    def filter_inputs(self, maps):
        names = [n for n, k in self.dram_names.items() if k == "ExternalInput"]
        return [{n: m[n] for n in names} for m in maps]

    def _sem(self, key):
        if key not in self.sem:
            self.sem[key] = self.nc.alloc_semaphore("s_" + key.replace("#", "_e"))
            self.cnt[key] = 0
        return self.sem[key]

    def _deps(self, reads, writes):
        deps = {}
        for (sp, lo, hi) in reads:
            for rec in self.recs.get(sp, ()):
                if rec[4] and rec[0] < hi and lo < rec[1]:
                    if deps.get(rec[2], 0) < rec[3]:
                        deps[rec[2]] = rec[3]
        for (sp, lo, hi) in writes:
            for rec in self.recs.get(sp, ()):
                if rec[0] < hi and lo < rec[1]:
                    if deps.get(rec[2], 0) < rec[3]:
                        deps[rec[2]] = rec[3]
        return deps

    def _record(self, reads, writes, key, val):
        for (sp, lo, hi) in writes:
            lst = self.recs.setdefault(sp, [])
            lst[:] = [rc for rc in lst if not (lo <= rc[0] and rc[1] <= hi)]
            lst.append([lo, hi, key, val, True])
        for (sp, lo, hi) in reads:
            lst = self.recs.setdefault(sp, [])
            lst[:] = [rc for rc in lst if not ((not rc[4]) and rc[2] == key and lo <= rc[0] and rc[1] <= hi)]
            lst.append([lo, hi, key, val, False])

    def op(self, e, fn, reads=(), writes=(), signal=True, dma=None):
        self.n_ops += 1
        deps = self._deps(reads, writes)
        eng = self.E[e]
        w = self.waited[e]
        for key, val in deps.items():
            if key.split("#")[0] == e and dma is None and (e == "pe" or not SAME_ENG_SYNC):
                continue
            if w.get(key, 0) >= val:
                continue
            eng.wait_ge(self._sem(key), val)
            w[key] = val
        inst = fn()
        ek = f"{e}#{self.epoch.get(e, 0)}"
        if dma is not None:
            s = self._sem(dma)
            self.cnt[dma] += 16
            assert self.cnt[dma] < 65000
            inst.then_inc(s, 16)
            key, val = dma, self.cnt[dma]
        elif signal:
            s = self._sem(ek)
            self.cnt[ek] += 1
            inst.then_inc(s, 1)
            key, val = ek, self.cnt[ek]
            if self.cnt[ek] >= 40000:
                self.epoch[e] = self.epoch.get(e, 0) + 1
        else:
            self._sem(ek)
            key, val = ek, self.cnt[ek] + 1
        self._record(reads, writes, key, val)
        return inst

    def finish(self, eng="sp"):
        e = self.E[eng]
        for key, s in self.sem.items():
            if self.cnt[key] > 0:
                e.wait_ge(s, self.cnt[key])

    def mm(self, out, lhsT, rhs, start, stop, reads, writes):
        nc = self.nc
        return self.op("pe", lambda: nc.tensor.matmul(out, lhsT=lhsT, rhs=rhs, start=start, stop=stop),
                       reads=reads, writes=writes, signal=stop)

    def dma(self, q, out, in_, key, reads, writes):
        eng = self.E[q]
        return self.op(q, lambda: eng.dma_start(out=out, in_=in_), reads=reads, writes=writes, dma=key)


def bf(a):
    return np.asarray(a, dtype=np.float32)


class Ctx:
    pass


def emit_norm(pg, c, h, hn, sq, gain, NT, psb, stride=512):
    nc = pg.nc
    pg.op("act", lambda: nc.scalar.activation(out=sq[:, :, 0:NT], in_=h[:, :, 0:NT], func=AF.Square),
          reads=[h.all()], writes=[sq.all()])
    bank = pg.ps_banks[psb]
    for kc in range(KC):
        pg.mm(bank[:, 0:NT], c.ones_bf[:, :], sq[:, kc, 0:NT], kc == 0, kc == KC - 1,
              reads=[sq.all(), c.ones_bf.all()], writes=[pg.psr(psb, 0, NT)])
    rstd = c.rstd
    pg.op("dve", lambda: nc.vector.tensor_scalar(out=rstd[:, 0:NT], in0=bank[:, 0:NT], scalar1=1.0 / D, scalar2=EPS,
                                                 op0=ALU.mult, op1=ALU.add),
          reads=[pg.psr(psb, 0, NT)], writes=[rstd.all()])
    pg.op("act", lambda: nc.scalar.activation(out=rstd[:, 0:NT], in_=rstd[:, 0:NT], func=AF.Sqrt),
          reads=[rstd.all()], writes=[rstd.all()])
    pg.op("dve", lambda: nc.vector.reciprocal(out=rstd[:, 0:NT], in_=rstd[:, 0:NT]),
          reads=[rstd.all()], writes=[rstd.all()])
    for kc in range(KC):
        pg.op("dve", lambda kc=kc: nc.vector.scalar_tensor_tensor(out=hn[:, kc, 0:NT], in0=h[:, kc, 0:NT],
                                                                  scalar=gain[:, kc:kc + 1], in1=rstd[:, 0:NT],
                                                                  op0=ALU.mult, op1=ALU.mult),
              reads=[h.all(), rstd.all(), gain.all()], writes=[hn.r(kc * stride, kc * stride + NT)])


NT_MAX = 512


class WStream:
    def __init__(self, pg, nslots, slot_elems):
        self.pg = pg
        self.slots = [pg.sb(f"wslot{i}", [P, slot_elems], BF16) for i in range(nslots)]
        self.i = 0

    def load(self, src_ap, nelem, dep=None):
        pg = self.pg
        s = self.slots[self.i % len(self.slots)]
        k = self.i % len(self.slots)
        self.i += 1
        pg.dma("pool", s[:, 0:nelem], src_ap, f"w{k}", reads=[dep] if dep else [], writes=[s.r(0, nelem)])
        return s


def wload(ws, ent, sidx, nelem):
    if isinstance(ent, tuple):
        ap, dep = ent
        return ws.load(ap[sidx, :, 0:nelem], nelem, dep=dep)
    return ws.load(ent[sidx, :, 0:nelem], nelem)


def emit_ffn(pg, c, ws, h, hn, aT, wg_d, wu_d, wd_d, NT):
    nc = pg.nc
    GRP = 3
    fcs = list(range(FC))
    for g0 in range(0, FC, GRP):
        grp = fcs[g0:g0 + GRP]
        n = len(grp)
        sg = wload(ws, wg_d, g0 // GRP, n * 2048)
        su = wload(ws, wu_d, g0 // GRP, n * 2048)
        for j, fc in enumerate(grp):
            bg = 0 + (fc % 2)
            bu = 2 + (fc % 2)
            for kc in range(KC):
                pg.mm(pg.ps_banks[bg][:, 0:NT], sg[:, j * 2048 + kc * 128: j * 2048 + (kc + 1) * 128], hn[:, kc, 0:NT],
                      kc == 0, kc == KC - 1, reads=[sg.r(j * 2048, (j + 1) * 2048), hn.all()], writes=[pg.psr(bg, 0, NT)])
            for kc in range(KC):
                pg.mm(pg.ps_banks[bu][:, 0:NT], su[:, j * 2048 + kc * 128: j * 2048 + (kc + 1) * 128], hn[:, kc, 0:NT],
                      kc == 0, kc == KC - 1, reads=[su.r(j * 2048, (j + 1) * 2048), hn.all()], writes=[pg.psr(bu, 0, NT)])
            sl = c.silu[fc % 2]
            pg.op("act", lambda bg=bg, sl=sl: nc.scalar.activation(out=sl[:, 0:NT], in_=pg.ps_banks[bg][:, 0:NT], func=AF.Silu),
                  reads=[pg.psr(bg, 0, NT)], writes=[sl.all()])
            pg.op("dve", lambda bu=bu, sl=sl, fc=fc: nc.vector.tensor_tensor(out=aT[:, fc, 0:NT], in0=pg.ps_banks[bu][:, 0:NT],
                                                                             in1=sl[:, 0:NT], op=ALU.mult),
                  reads=[pg.psr(bu, 0, NT), sl.all()], writes=[aT.r(fc * NT_MAX, fc * NT_MAX + NT)])
    for oc in range(KC):
        sd = wload(ws, wd_d, oc, FC * 128)
        b = 4 + (oc % 2)
        for k in range(FC):
            pg.mm(pg.ps_banks[b][:, 0:NT], sd[:, k * 128:(k + 1) * 128], aT[:, k, 0:NT], k == 0, k == FC - 1,
                  reads=[sd.r(0, FC * 128), aT.all()], writes=[pg.psr(b, 0, NT)])
        pg.op("dve", lambda b=b, oc=oc: nc.vector.tensor_tensor(out=h[:, oc, 0:NT], in0=pg.ps_banks[b][:, 0:NT],
                                                                in1=h[:, oc, 0:NT], op=ALU.add),
              reads=[pg.psr(b, 0, NT), h.r(oc * NT_MAX, oc * NT_MAX + NT)], writes=[h.r(oc * NT_MAX, oc * NT_MAX + NT)])


def slopes():
    return [2.0 ** (-8.0 * (i + 1) / 8) for i in range(8)]


def build_l0(S, with_l1=False, debug=None):
    NS = S // 2048
    NKT = S // 128
    NT = 512
    pg = Prog()
    nc = pg.nc
    c = Ctx()
    dr = lambda name, shape, dt=F32, kind="ExternalInput": pg.dram(name, shape, dt, kind)
    xT = dr("xT", [P, KC, S])
    wkvf_d = dr("wkvf", [P, KC * 1024])
    wkvt_d = dr("wkvt", [P, KC * 512])
    w1k_d = dr("w1k", [P, 32 * 128]); w1v_d = dr("w1v", [P, 32 * 128])
    w2k_d = dr("w2k", [P, 128]); w2v_d = dr("w2v", [P, 128])
    pek_d = dr("pek", [P, 32]); pev_d = dr("pev", [P, 32])
    gains_d = dr("gains", [P, 2 * KC])
    wgt_d = dr("wgt", [P, KC * 24])
    wcT_d = dr("wcT", [P, 8 * 128])
    sgug_d = dr("sgug", [P, 1024]); sgub_d = dr("sgub", [P, 1024])
    tabs_d = dr("tabs", [P, 173])
    alibi_d = dr("alibi", [P, NKT * 8])
    cst_d = dr("cst", [P, 128 * 4 + 2048])
    cstf_d = dr("cstf", [P, 1024 + 8 + 3])
    scr_f = dr("scr_f", [8, P, S], BF16, "ExternalOutput" if debug else "Internal")
    scr_t = dr("scr_t", [S, 4, 128], BF16, "ExternalOutput" if debug else "Internal")

    c.cst = pg.sb("cst", [P, 128 * 4 + 2048], BF16)
    cst = c.cst

    class V:
        def __init__(s, base, lo, hi, shape=None):
            s.base = base; s.lo = lo; s.hi = hi
        def all(s):
            return s.base.r(s.lo, s.hi)
        def __getitem__(s, k):
            return s.base.t[:, s.lo:s.hi][k]
    c.ones_bf = V(cst, 0, 128); c.ident = V(cst, 128, 256); c.tril = V(cst, 256, 384); c.wfirst = V(cst, 384, 512)
    c.wide = V(cst, 512, 2560)
    c.cstf = pg.sb("cstf", [P, 1024 + 8 + 3], F32)
    c.mid = V(c.cstf, 0, 1024); c.m8 = V(c.cstf, 1024, 1032); c.fpat = V(c.cstf, 1032, 1035)
    c.tabs = pg.sb("tabs", [P, 173], F32)
    c.cex = V(c.tabs, 0, 96); c.fx = V(c.tabs, 96, 128); c.wex = V(c.tabs, 128, 140); c.exm = V(c.tabs, 140, 172); c.hex = V(c.tabs, 172, 173)
    c.alibi = pg.sb("alibi", [P, NKT, 8], F32)
    c.gains = pg.sb("gains", [P, 2 * KC], F32)
    c.wcT = pg.sb("wcT", [P, 8, 128], BF16)
    c.sgug = pg.sb("sgug", [P, 1024], F32); c.sgub = pg.sb("sgub", [P, 8, 128], F32)
    c.wgt = pg.sb("wgt", [P, KC, 24], BF16)
    c.kcmpT = pg.sb("kcmpT", [P, 2, 1024], BF16)
    c.vcmp = pg.sb("vcmp", [P, 2, 8, 128], BF16)
    c.rstd = pg.sb("rstd", [P, 512], F32)
    c.silu = [pg.sb(f"silu{i}", [P, 512], F32) for i in range(2)]
    h = pg.sb("h", [P, KC, 512], F32)
    hn = pg.sb("hn", [P, KC, 512], BF16)
    ZONE = (pg.sb_off + 31) // 32 * 32
    ws = WStream(pg, 4, 6144)
    PL = (pg.sb_off + 31) // 32 * 32
    print("persistent bytes", ZONE - SB_LO, "PL start", PL - SB_LO, "PL size", SB_HI - PL)

    def ld(q, dst, src, key):
        pg.dma(q, dst.t[:] if isinstance(dst, T) else dst, src, key, reads=[], writes=[dst.all()])
    ld("pool", c.cst, cst_d[:, :], "c0")
    ld("sp", c.cstf, cstf_d[:, :], "c1")
    ld("sp", c.tabs, tabs_d[:, :], "c2")
    pg.dma("sp", c.alibi.t[:].rearrange("p a b -> p (a b)"), alibi_d[:, :], "c3", reads=[], writes=[c.alibi.all()])
    ld("sp", c.gains, gains_d[:, :], "c4")
    pg.dma("pool", c.wcT.t[:].rearrange("p a b -> p (a b)"), wcT_d[:, :], "c5", reads=[], writes=[c.wcT.all()])
    ld("sp", c.sgug, sgug_d[:, :], "c6")
    pg.dma("sp", c.sgub.t[:].rearrange("p a b -> p (a b)"), sgub_d[:, :], "c7", reads=[], writes=[c.sgub.all()])
    pg.dma("pool", c.wgt.t[:].rearrange("p a b -> p (a b)"), wgt_d[:, :], "c8", reads=[], writes=[c.wgt.all()])
    pg.op("pool", lambda: nc.gpsimd.affine_select(out=c.wcT[:, :, :], in_=c.wcT[:, :, :], pattern=[[0, 8], [1, 128]],
                                                   compare_op=ALU.is_ge, fill=0.0, base=0, channel_multiplier=-1),
          reads=[c.wcT.all()], writes=[c.wcT.all()])
    pg.op("dve", lambda: nc.vector.memset(c.kcmpT.t[:], 0.0), writes=[c.kcmpT.all()])
    pg.op("dve", lambda: nc.vector.memset(c.vcmp.t[:], 0.0), writes=[c.vcmp.all()])
    g_mix = V(c.gains, 0, KC); g_ffn = V(c.gains, KC, 2 * KC)

    o = ZONE
    wkvf = pg.sb("wkvf", [P, KC, 1024], BF16, at=o); o += KC * 1024 * 2
    wkvt = pg.sb("wkvt", [P, KC, 512], BF16, at=o); o += KC * 512 * 2
    sq = pg.sb("sq0", [P, KC, 512], BF16, at=o); o += KC * 512 * 2
    stf = pg.sb("stf", [P, 8, 512], BF16, at=o); o += 8 * 512 * 2
    stt = pg.sb("stt", [P, 4, 512], BF16, at=o); o += 4 * 512 * 2
    pg.dma("pool", wkvf.t[:].rearrange("p a b -> p (a b)"), wkvf_d[:, :], "c9", reads=[], writes=[wkvf.all()])
    pg.dma("pool", wkvt.t[:].rearrange("p a b -> p (a b)"), wkvt_d[:, :], "c10", reads=[], writes=[wkvt.all()])
    went = None
    conv = []
    if with_l1:
        went = {}
        for nm, shp in (("wuq", [8, P, KC * 256]), ("wv", [4, P, KC * 256]), ("wout", [6, P, 6144]), ("wg", [15, P, 6144]),
                        ("wu", [15, P, 6144]), ("wd", [16, P, FC * 128]), ("wc", [16, P, 6144]), ("woc", [6, P, 6144]),
                        ("wg1", [15, P, 6144]), ("wu1", [15, P, 6144]), ("wd1", [16, P, FC * 128])):
            src = dr(nm, shp)
            dst = dr(nm + "_b", shp, BF16, "Internal")
            went[nm] = (dst, ("wconv", 0, 1))
            for sidx in range(shp[0]):
                conv.append((dst, src, sidx))
    n_t0 = S // 512
    per_tile = (len(conv) + n_t0 - 1) // n_t0
    for j in range(S // 512):
        pg.dma("sp", h.t[:], xT[:, :, 512 * j:512 * j + 512], "x", reads=[], writes=[h.all()])
        emit_norm(pg, c, h, hn, sq, g_mix, 512, 7)
        for fcn in range(8):
            b = fcn % 2
            for kc in range(KC):
                pg.mm(pg.ps_banks[b][:, :], wkvf[:, kc, fcn * 128:(fcn + 1) * 128], hn[:, kc, :], kc == 0, kc == KC - 1,
                      reads=[wkvf.all(), hn.all()], writes=[pg.psr(b)])
            if fcn % 2 == 0:
                pg.op("act", lambda b=b, fcn=fcn: nc.scalar.copy(out=stf[:, fcn, :], in_=pg.ps_banks[b][:, :]),
                      reads=[pg.psr(b)], writes=[stf.r(fcn * 512, fcn * 512 + 512)])
            else:
                pg.op("dve", lambda b=b, fcn=fcn: nc.vector.tensor_copy(out=stf[:, fcn, :], in_=pg.ps_banks[b][:, :]),
                      reads=[pg.psr(b)], writes=[stf.r(fcn * 512, fcn * 512 + 512)])
        pg.dma("pool", scr_f.rearrange("f p t -> p f t")[:, :, 512 * j:512 * j + 512], stf.t[:], "sf",
               reads=[stf.all()], writes=[("scr_f", 512 * j, 512 * j + 512)])
        for ts in range(4):
            b = 2 + ts % 2
            for kc in range(KC):
                pg.mm(pg.ps_banks[b][:, :], hn[:, kc, ts * 128:(ts + 1) * 128], wkvt[:, kc, :], kc == 0, kc == KC - 1,
                      reads=[wkvt.all(), hn.all()], writes=[pg.psr(b)])
            if ts % 2 == 0:
                pg.op("act", lambda b=b, ts=ts: nc.scalar.copy(out=stt[:, ts, :], in_=pg.ps_banks[b][:, :]),
                      reads=[pg.psr(b)], writes=[stt.r(ts * 512, ts * 512 + 512)])
            else:
                pg.op("dve", lambda b=b, ts=ts: nc.vector.tensor_copy(out=stt[:, ts, :], in_=pg.ps_banks[b][:, :]),
                      reads=[pg.psr(b)], writes=[stt.r(ts * 512, ts * 512 + 512)])
        pg.dma("pool", scr_t[512 * j:512 * j + 512, :, :].rearrange("(ts p) i d -> p ts (i d)", p=128), stt.t[:], "st",
               reads=[stt.all()], writes=[("scr_t", 512 * j, 512 * j + 512)])
        for (dst, src, sidx) in conv[j * per_tile:(j + 1) * per_tile]:
            pg.dma("pool", dst[sidx, :, :], src[sidx, :, :], "cv", reads=[], writes=[("wconv", 0, 1)])

    o = ZONE
    w1 = [pg.sb("w1k", [P, 32, 128], BF16, at=o), pg.sb("w1v", [P, 32, 128], BF16, at=o + 8192)]; o += 16384
    w2 = [pg.sb("w2k", [P, 128], BF16, at=o), pg.sb("w2v", [P, 128], BF16, at=o + 256)]; o += 512
    pe = [pg.sb("pek", [P, 32], BF16, at=o), pg.sb("pev", [P, 32], BF16, at=o + 64)]; o += 128
    peb = [pg.sb("pebk", [P, 1], F32, at=o), pg.sb("pebv", [P, 1], F32, at=o + 32)]; o += 64
    hid = pg.sb("hid", [P, 256], BF16, at=o); o += 512
    cwin = pg.sb("cwin", [P, 4, 2064], BF16, at=o); o += 4 * 2064 * 2
    for i, (a, b_, cc, d_) in enumerate([(w1[0], w1k_d, "d0", None), (w1[1], w1v_d, "d1", None)]):
        pg.dma("pool", a.t[:].rearrange("p a b -> p (a b)"), b_[:, :], cc, reads=[], writes=[a.all()])
    pg.dma("pool", w2[0].t[:], w2k_d[:, :], "d2", reads=[], writes=[w2[0].all()])
    pg.dma("pool", w2[1].t[:], w2v_d[:, :], "d3", reads=[], writes=[w2[1].all()])
    pg.dma("pool", pe[0].t[:], pek_d[:, :], "d4", reads=[], writes=[pe[0].all()])
    pg.dma("pool", pe[1].t[:], pev_d[:, :], "d5", reads=[], writes=[pe[1].all()])
    for wh in range(2):
        for l in range(32):
            pg.mm(pg.ps_banks[6][:, 0:1], w1[wh][:, l, :], pe[wh][:, l:l + 1], l == 0, l == 31,
                  reads=[w1[wh].all(), pe[wh].all()], writes=[pg.psr(6, 0, 1)])
        pg.op("dve", lambda wh=wh: nc.vector.tensor_copy(out=peb[wh][:, :], in_=pg.ps_banks[6][:, 0:1]),
              reads=[pg.psr(6, 0, 1)], writes=[peb[wh].all()])
    for m in range(S // 2048):
        last = (m == S // 2048 - 1)
        ncol = 2048 if last else 2064
        nblk = 127 if last else 128
        pg.dma("sp", cwin[:, :, 0:ncol], scr_f[0:4, :, 2048 * m:2048 * m + ncol].rearrange("f p t -> p f t"), "cw",
               reads=[("scr_f", 0, S)], writes=[cwin.all()])
        for wh in range(2):
            for g in range(2):
                idx = wh * 2 + g
                for l in range(32):
                    pg.mm(pg.ps_banks[g][:, 0:nblk], w1[wh][:, l, :], cwin[:, idx, l:l + 16 * (nblk - 1) + 1:16],
                          l == 0, l == 31, reads=[w1[wh].all(), cwin.all()], writes=[pg.psr(g, 0, nblk)])
                pg.op("act", lambda g=g, wh=wh: nc.scalar.activation(out=hid[:, g * 128:g * 128 + nblk], in_=pg.ps_banks[g][:, 0:nblk],
                                                                    func=AF.Gelu_apprx_tanh, bias=peb[wh][:, 0:1], scale=1.0),
                      reads=[pg.psr(g, 0, nblk), peb[wh].all()], writes=[hid.r(g * 128, g * 128 + nblk)])
                if wh == 0:
                    pg.mm(pg.ps_banks[2 + g][:, 0:nblk], w2[0][:, :], hid[:, g * 128:g * 128 + nblk], True, True,
                          reads=[w2[0].all(), hid.r(g * 128, g * 128 + nblk)], writes=[pg.psr(2 + g, 0, nblk)])
                    pg.op("dve", lambda g=g, m=m: nc.vector.tensor_copy(out=c.kcmpT[:, g, 128 * m:128 * m + nblk], in_=pg.ps_banks[2 + g][:, 0:nblk]),
                          reads=[pg.psr(2 + g, 0, nblk)], writes=[c.kcmpT.all()])
                else:
                    pg.mm(pg.ps_banks[2 + g][0:nblk, 0:128], hid[:, g * 128:g * 128 + nblk], w2[1][:, :], True, True,
                          reads=[w2[1].all(), hid.r(g * 128, g * 128 + nblk)], writes=[pg.psr(2 + g, 0, 128)])
                    pg.op("dve", lambda g=g, m=m: nc.vector.tensor_copy(out=c.vcmp[0:nblk, g, m, :], in_=pg.ps_banks[2 + g][0:nblk, 0:128]),
                          reads=[pg.psr(2 + g, 0, 128)], writes=[c.vcmp.all()])
    if debug == "p0":
        dbg_k = dr("dbg_k", [P, 2048], BF16, "ExternalOutput")
        dbg_v = dr("dbg_v", [P, 2048], BF16, "ExternalOutput")
        pg.dma("sp", dbg_k[:, :], c.kcmpT.t[:].rearrange("p a b -> p (a b)"), "dbg", reads=[c.kcmpT.all()], writes=[])
        pg.dma("sp", dbg_v[:, :], c.vcmp.t[:].rearrange("p a b c -> p (a b c)"), "dbg", reads=[c.vcmp.all()], writes=[])
        pg.finish("sp")
        return pg
    emit_l0_main(pg, c, dr, ws, h, hn, g_mix, g_ffn, xT, scr_f, scr_t, S, PL, debug, fused=with_l1, went=went)
    pg.finish("sp")
    return pg


def emit_l0_main(pg, c, dr, ws, h, hn, g_mix, g_ffn, xT, scr_f, scr_t, S, PL, debug, fused=False, went=None):
    nc = pg.nc
    NS = S // 2048
    NKT = S // 128
    SL = slopes()
    SCALE = 128.0 ** -0.5
    full = debug in (None, "h1", "h2")
    if went is not None:
        wuq_d, wv_d, wout_d, wg_d, wu_d, wd_d = (went[k] for k in ("wuq", "wv", "wout", "wg", "wu", "wd"))
    else:
        wuq_d = dr("wuq", [8, P, KC * 256])
        wv_d = dr("wv", [4, P, KC * 256])
        if full:
            wout_d = dr("wout", [6, P, 3 * 2048])
        if debug in (None, "h2"):
            wg_d = dr("wg", [15, P, 3 * 2048]); wu_d = dr("wu", [15, P, 3 * 2048]); wd_d = dr("wd", [16, P, FC * 128])
    if debug in (None, "h1", "h2") and not fused:
        out_d = dr("outT", [P, KC, NS * 512], F32, "ExternalOutput")
    if fused:
        gains1_d = dr("gains1", [P, 3 * KC]); cw_d = dr("cw", [P, KC * 3])
        wc_d, woc_d, wg1_d, wu1_d, wd1_d = (went[k] for k in ("wc", "woc", "wg1", "wu1", "wd1"))
        outF_d = dr("outF", [P, KC, NS * 512], F32, "ExternalOutput")
    if debug == "y":
        dbg_y = dr("dbg_y", [P, 16, NS * 512], BF16, "ExternalOutput")
        dbg_g = dr("dbg_g", [P, NS * 4 * 24], F32, "ExternalOutput")
    o = [PL]

    def pl(name, shape, dt, at=None):
        if at is None:
            off = (o[0] + 31) // 32 * 32
            t = pg.sb(name, shape, dt, at=off)
            o[0] = off + t.nbytes
        else:
            t = pg.sb(name, shape, dt, at=at)
        return t
    uT = pl("uT", [P, 8, 512], BF16)
    qT = pl("qT", [P, 8, 512], BF16)
    ybT = pl("ybT", [P, 8, 512], BF16)
    X0 = (o[0] + 31) // 32 * 32
    sq = pl("sq", [P, KC, 512], BF16)
    gv = pl("gv", [P, 4, 1024], F32, at=ybT.off)
    tmpv = pl("tmpv", [P, 1024], F32, at=X0 + 8192)
    vn = pl("vn", [P, 1024], BF16, at=X0 + 12288)
    s_sb = pl("s_sb", [P, 1024], F32, at=X0)
    p_sb = pl("p_sb", [P, 1024], F32, at=X0 + 4096)
    paccb = pl("paccb", [P, 1040], F32, at=X0 + 8192)
    pbf = pl("pbf", [P, 1024], BF16, at=X0 + 8192 + 4160)
    ptc = [pl(f"ptc{i}", [P, 128], BF16, at=X0 + 8192 + 4160 + 2048 + 256 * i) for i in range(2)]
    aT = pl("aT", [P, FC, 512], BF16, at=PL)
    gates = pl("gates", [P, 4, 24], F32)
    ksT = [pl(f"ksT{i}", [P, 1024], BF16) for i in range(2)]
    vsb = [pl(f"vsb{i}", [P, 8, 129], BF16) for i in range(2)]
    kwT = pl("kwT", [P, 2, 1024], BF16)
    vwb = pl("vwb", [P, 2, 8, 129], BF16)
    PT = [pl(f"PT{i}", [P, 4, 128], BF16) for i in range(3)]
    sc = pl("sc", [P, 264], F32)
    sc2 = pl("sc2", [P, 264], F32)
    t1 = pl("t1", [P, 264], F32)
    sm = pl("sm", [P, 64], F32)
    selbf = pl("selbf", [P, 256], BF16)
    selT = pl("selT", [P, 8, 128], BF16)
    ybt = pl("ybt", [P, 4, 128], F32)
    ybtb = pl("ybtb", [P, 4, 128], BF16)
    if fused:
        gains1 = pl("gains1", [P, 3 * KC], F32)
        cw1 = pl("cw1", [P, KC, 3], F32)
        hhalo = pl("hh", [P, KC, 2], F32)
        hnh = pl("hnh", [P, KC, 2], BF16)
        yT1 = pl("yT1", [P, KC, 512], BF16, at=PL)
        zb1 = pl("zb1", [P, 514], F32, at=ybT.off)
        tmpz1 = pl("tmpz1", [P, 514], F32, at=ybT.off + 2080)
        acc1 = pl("acc1", [P, 512], F32, at=ybT.off + 4160)
        outn1 = pl("outn1", [P, KC, 512], F32, at=PL)
        pg.dma("sp", gains1.t[:], gains1_d[:, :], "c11", reads=[], writes=[gains1.all()])
        pg.dma("sp", cw1.t[:].rearrange("p a b -> p (a b)"), cw_d[:, :], "c12", reads=[], writes=[cw1.all()])
        g3 = (V(gains1, 0, KC), V(gains1, KC, 2 * KC), V(gains1, 2 * KC, 3 * KC))
        l1bufs = (sq, yT1, zb1, tmpz1, acc1, aT, hhalo, hnh, outn1)
    print("PL used", o[0] - PL, "of", SB_HI - PL)
    assert aT.off + aT.nbytes <= SB_HI
    SM = lambda a, b: (sm[:, a:b], sm.r(a, b))
    bank = pg.ps_banks
    tps_bf = bank[4][:, 256:512].bitcast(BF16)
    tps_n = [0]

    def tps_slot():
        k = tps_n[0] % 4
        tps_n[0] += 1
        return tps_bf[:, k * 128:(k + 1) * 128], pg.psr(4)
    msk_n = [0]

    def msk_slot():
        k = msk_n[0] % 2
        msk_n[0] += 1
        return bank[6 + k][:, 0:128], pg.psr(6 + k)
    pt_n = [0]
    cp_n = [0]

    def evac_copy(out, in_, reads, writes):
        cp_n[0] += 1
        if cp_n[0] % 2:
            pg.op("act", lambda: nc.scalar.copy(out=out, in_=in_), reads=reads, writes=writes)
        else:
            pg.op("dve", lambda: nc.vector.tensor_copy(out=out, in_=in_), reads=reads, writes=writes)

    for vb in vsb:
        pg.op("dve", lambda vb=vb: nc.vector.memset(vb[:, :, 128:129], 1.0), writes=[vb.all()])
    pg.op("dve", lambda: nc.vector.memset(vwb[:, :, :, 128:129], 1.0), writes=[vwb.all()])

    import os as _os
    _slots = [int(v) for v in _os.environ.get('K_SLOTS', '').split(',') if v] or list(range(NS))
    tiles = []
    for i in _slots:
        if fused:
            tiles.append((i, 512 * (4 * i + 3) - 128, 16 * i + 11, 1, True))
        tiles.append((i, 512 * (4 * i + 3), 16 * i + 12, 4, False))
    for (i, T0, qb0, NQ, is_halo) in tiles:
        NT = 128 * NQ
        pg.dma("sp", h[:, :, 0:NT], xT[:, :, T0:T0 + NT], "x", reads=[], writes=[h.all()])
        emit_norm(pg, c, h, hn, sq, g_mix, NT, 7)
        for sidx in range(8):
            sl = wload(ws, wuq_d, sidx, 4096)
            for j in range(2):
                ci = 2 * sidx + j
                b = ci % 2
                for kc in range(KC):
                    pg.mm(bank[b][:, 0:NT], sl[:, kc * 256 + j * 128: kc * 256 + (j + 1) * 128], hn[:, kc, 0:NT], kc == 0, kc == KC - 1,
                          reads=[sl.r(0, 4096), hn.all()], writes=[pg.psr(b)])
                if ci < 8:
                    pg.op("act", lambda b=b, ci=ci: nc.scalar.activation(out=uT[:, ci, 0:NT], in_=bank[b][:, 0:NT], func=AF.Gelu_apprx_tanh),
                          reads=[pg.psr(b)], writes=[uT.r(ci * 512, ci * 512 + 512)])
                else:
                    pg.op("act", lambda b=b, ci=ci: nc.scalar.activation(out=qT[:, ci - 8, 0:NT], in_=bank[b][:, 0:NT], func=AF.Copy, scale=SCALE),
                          reads=[pg.psr(b)], writes=[qT.r((ci - 8) * 512, (ci - 8) * 512 + 512)])
        for qv in range(4):
            sl = wload(ws, wv_d, qv, 4096)
            for ts in range(NQ):
                b = 2 + ts % 2
                for kc in range(KC):
                    pg.mm(bank[b][:, 0:256], hn[:, kc, ts * 128:(ts + 1) * 128], sl[:, kc * 256:(kc + 1) * 256], kc == 0, kc == KC - 1,
                          reads=[sl.r(0, 4096), hn.all()], writes=[pg.psr(b, 0, 256)])
                pg.op("act", lambda b=b, ts=ts, qv=qv: nc.scalar.activation(out=gv[:, ts, qv * 256:(qv + 1) * 256], in_=bank[b][:, 0:256],
                                                                           func=AF.Gelu_apprx_tanh),
                      reads=[pg.psr(b, 0, 256)], writes=[gv.r(ts * 1024 + qv * 256, ts * 1024 + (qv + 1) * 256)])
        for ts in range(NQ):
            for kc in range(KC):
                pg.mm(bank[6][:, ts * 32:ts * 32 + 24], hn[:, kc, ts * 128:(ts + 1) * 128], c.wgt[:, kc, :], kc == 0, kc == KC - 1,
                      reads=[c.wgt.all(), hn.all()], writes=[pg.psr(6, ts * 32, ts * 32 + 24)])
            pg.op("act", lambda ts=ts: nc.scalar.activation(out=gates[:, ts, :], in_=bank[6][:, ts * 32:ts * 32 + 24], func=AF.Sigmoid),
                  reads=[pg.psr(6, ts * 32, ts * 32 + 24)], writes=[gates.r(ts * 24, ts * 24 + 24)])
        for ts in range(NQ):
            gvr = gv.r(ts * 1024, ts * 1024 + 1024)
            pg.op("dve", lambda ts=ts: nc.vector.tensor_tensor(out=tmpv[:, :], in0=gv[:, ts, :], in1=gv[:, ts, :], op=ALU.mult),
                  reads=[gvr], writes=[tmpv.all()])
            ss, ssr = SM(0, 8)
            pg.op("dve", lambda ss=ss: nc.vector.tensor_reduce(out=ss, in_=tmpv[:, :].rearrange("p (g c) -> p g c", g=8), axis=AX.X, op=ALU.add),
                  reads=[tmpv.all()], writes=[ssr])
            pg.op("dve", lambda ss=ss: nc.vector.tensor_scalar(out=ss, in0=ss, scalar1=1.0 / 128, scalar2=EPS, op0=ALU.mult, op1=ALU.add),
                  reads=[ssr], writes=[ssr])
            pg.op("act", lambda ss=ss: nc.scalar.activation(out=ss, in_=ss, func=AF.Sqrt), reads=[ssr], writes=[ssr])
            pg.op("dve", lambda ss=ss: nc.vector.reciprocal(out=ss, in_=ss), reads=[ssr], writes=[ssr])
            for g in range(8):
                pg.op("dve", lambda g=g, ts=ts: nc.vector.scalar_tensor_tensor(out=vn[:, g * 128:(g + 1) * 128], in0=gv[:, ts, g * 128:(g + 1) * 128],
                                                                               scalar=sm[:, g:g + 1], in1=c.sgug[:, g * 128:(g + 1) * 128],
                                                                               op0=ALU.mult, op1=ALU.mult),
                      reads=[gvr, ssr, c.sgug.all()], writes=[vn.r(g * 128, (g + 1) * 128)])
            for g in range(8):
                b = 2 + g // 4
                pg.mm(bank[b][:, (g % 4) * 128:(g % 4 + 1) * 128], vn[:, g * 128:(g + 1) * 128], c.wcT[:, g, :], True, True,
                      reads=[vn.r(g * 128, (g + 1) * 128), c.wcT.all()], writes=[pg.psr(b, (g % 4) * 128, (g % 4 + 1) * 128)])
            for hf in range(2):
                b = 2 + hf
                pg.op("dve", lambda b=b, hf=hf: nc.vector.tensor_tensor(out=tmpv[:, hf * 512:(hf + 1) * 512], in0=bank[b][:, :],
                                                                        in1=c.sgub[:, 4 * hf:4 * hf + 4, :].rearrange("p a b -> p (a b)"), op=ALU.add),
                      reads=[pg.psr(b), c.sgub.all()], writes=[tmpv.r(hf * 512, (hf + 1) * 512)])
                pg.op("dve", lambda hf=hf, ts=ts: nc.vector.tensor_tensor(out=uT[:, 4 * hf:4 * hf + 4, ts * 128:(ts + 1) * 128],
                                                                          in0=tmpv[:, hf * 512:(hf + 1) * 512].rearrange("p (a b) -> p a b", a=4),
                                                                          in1=uT[:, 4 * hf:4 * hf + 4, ts * 128:(ts + 1) * 128], op=ALU.mult),
                      reads=[tmpv.r(hf * 512, (hf + 1) * 512), uT.r(4 * hf * 512, (4 * hf + 4) * 512)], writes=[uT.r(4 * hf * 512, (4 * hf + 4) * 512)])
        w0 = 128 * (qb0 - 4)
        pg.dma("sp", kwT[:, :, :], scr_f[6:8, :, w0:w0 + 1024].rearrange("g p t -> p g t"), "kw",
               reads=[("scr_f", 0, S)], writes=[kwT.all()])
        for g in range(2):
            pg.dma("sp", vwb[:, g, :, 0:128], scr_t[w0:w0 + 1024, 2 + g, :].rearrange("(kt p) d -> p kt d", p=128), "vw",
                   reads=[("scr_t", 0, S)], writes=[vwb.all()])
        kvn = [0]
        _jqs = [int(v) for v in _os.environ.get('K_JQ', '').split(',') if v] or list(range(NQ))
        _stage = int(_os.environ.get('K_STAGE', '9'))
        for jq in _jqs:
            qb = qb0 + jq
            qc = slice(jq * 128, (jq + 1) * 128)
            for g in range(2):
                ncols = 8 * qb + 7
                npc = (ncols + 511) // 512
                pg.op("dve", lambda: nc.vector.memset(paccb[:, :], 0.0), writes=[paccb.all()])
                for hl in range(4):
                    hh = 4 * g + hl
                    for pc in range(npc):
                        w = min(512, ncols - 512 * pc)
                        pg.mm(bank[pc][:, 0:w], qT[:, hh, qc], c.kcmpT[:, g, 512 * pc:512 * pc + w], True, True,
                              reads=[qT.r(hh * 512, hh * 512 + 512), c.kcmpT.all()], writes=[pg.psr(pc, 0, w)])
                        pg.op("dve", lambda pc=pc, w=w, hh=hh: nc.vector.scalar_tensor_tensor(
                            out=s_sb[:, 512 * pc:512 * pc + w], in0=c.mid[:, 512 * pc:512 * pc + w], scalar=float(SL[hh]),
                            in1=bank[pc][:, 0:w], op0=ALU.mult, op1=ALU.add),
                            reads=[pg.psr(pc, 0, w), c.mid.all()], writes=[s_sb.r(512 * pc, 512 * pc + w)])
                    cw_ = min(96, ncols)
                    pg.op("dve", lambda cw_=cw_: nc.vector.tensor_tensor(out=s_sb[:, 0:cw_], in0=s_sb[:, 0:cw_], in1=c.cex[:, 0:cw_], op=ALU.add),
                          reads=[s_sb.r(0, cw_), c.cex.all()], writes=[s_sb.r(0, cw_)])
                    pg.op("dve", lambda ncols=ncols: nc.vector.tensor_tensor(out=s_sb[:, ncols - 8:ncols], in0=s_sb[:, ncols - 8:ncols], in1=c.m8[:, :], op=ALU.add),
                          reads=[s_sb.r(ncols - 8, ncols), c.m8.all()], writes=[s_sb.r(ncols - 8, ncols)])
                    mx, mxr = SM(8, 9)
                    pg.op("dve", lambda ncols=ncols, mx=mx: nc.vector.reduce_max(out=mx, in_=s_sb[:, 0:ncols], axis=AX.X),
                          reads=[s_sb.r(0, ncols)], writes=[mxr])
                    pg.op("dve", lambda mx=mx: nc.vector.tensor_scalar(out=mx, in0=mx, scalar1=-1.0e20, scalar2=-1.0, op0=ALU.max, op1=ALU.mult),
                          reads=[mxr], writes=[mxr])
                    ls, lsr = SM(9, 10)
                    pg.op("dve", lambda ls=ls: nc.vector.memset(ls, 0.0), writes=[lsr])
                    pg.op("act", lambda ncols=ncols, mx=mx, ls=ls: nc.scalar.activation(out=p_sb[:, 0:ncols], in_=s_sb[:, 0:ncols], func=AF.Exp,
                                                                                       bias=mx, scale=1.0, accum_out=ls),
                          reads=[s_sb.r(0, ncols), mxr, lsr], writes=[p_sb.r(0, ncols), lsr])
                    pg.op("dve", lambda ls=ls: nc.vector.tensor_scalar(out=ls, in0=ls, scalar1=1.0e-30, scalar2=None, op0=ALU.max),
                          reads=[lsr], writes=[lsr])
                    pg.op("dve", lambda ls=ls: nc.vector.reciprocal(out=ls, in_=ls), reads=[lsr], writes=[lsr])
                    if hl == 0:
                        pg.op("dve", lambda ncols=ncols, ls=ls: nc.vector.tensor_scalar(out=paccb[:, 1:1 + ncols], in0=p_sb[:, 0:ncols], scalar1=ls, scalar2=None, op0=ALU.mult),
                              reads=[p_sb.r(0, ncols), lsr], writes=[paccb.r(1, 1 + ncols)])
                    else:
                        pg.op("dve", lambda ncols=ncols, ls=ls: nc.vector.scalar_tensor_tensor(out=paccb[:, 1:1 + ncols], in0=p_sb[:, 0:ncols], scalar=ls,
                                                                                              in1=paccb[:, 1:1 + ncols], op0=ALU.mult, op1=ALU.add),
                              reads=[p_sb.r(0, ncols), lsr, paccb.r(1, 1 + ncols)], writes=[paccb.r(1, 1 + ncols)])
                    pg.op("act", lambda ncols=ncols, ls=ls: nc.scalar.activation(out=pbf[:, 0:ncols], in_=p_sb[:, 0:ncols], func=AF.Copy, scale=ls),
                          reads=[p_sb.r(0, ncols), lsr], writes=[pbf.r(0, ncols)])
                    ntt = (ncols + 127) // 128
                    for tt in range(ntt):
                        w = min(128, ncols - 128 * tt)
                        tp, tpr = tps_slot()
                        pg.op("pe", lambda tp=tp, tt=tt, w=w: nc.tensor.transpose(tp[0:w, :], pbf[:, 128 * tt:128 * tt + w], c.ident[:, :]),
                              reads=[pbf.r(128 * tt, 128 * tt + w), c.ident.all()], writes=[tpr])
                        pk = ptc[pt_n[0] % 2]
                        pt_n[0] += 1
                        evac_copy(pk[0:w, :], tp[0:w, :], [tpr], [pk.all()])
                        pg.mm(bank[5][:, hl * 128:(hl + 1) * 128], pk[0:w, :], c.vcmp[0:w, g, tt, :], tt == 0, tt == ntt - 1,
                              reads=[pk.all(), c.vcmp.all()], writes=[pg.psr(5, hl * 128, (hl + 1) * 128)])
                if _stage < 2:
                    continue
                W = 2 * qb + 2
                Wc = max(W, 32)
                A = paccb[:, 0:4 * W].rearrange("p (j f) -> p j f", f=4)
                pg.op("dve", lambda A=A, W=W: nc.vector.tensor_reduce(out=t1[:, 0:W], in_=A[:, :, 1:4], axis=AX.X, op=ALU.add),
                      reads=[paccb.all()], writes=[t1.r(0, W)])
                pg.op("dve", lambda A=A, W=W: nc.vector.scalar_tensor_tensor(out=t1[:, 0:W], in0=t1[:, 0:W], scalar=2.0, in1=A[:, :, 0],
                                                                             op0=ALU.mult, op1=ALU.add),
                      reads=[paccb.all(), t1.r(0, W)], writes=[t1.r(0, W)])
                if W < Wc:
                    pg.op("dve", lambda W=W, Wc=Wc: nc.vector.memset(sc[:, 1 + W:1 + Wc], -1.0), writes=[sc.r(1 + W, 1 + Wc)])
                pg.op("dve", lambda W=W: nc.vector.tensor_tensor(out=sc[:, 1:1 + W], in0=t1[:, 0:W], in1=paccb[:, 4:4 + 4 * W:4], op=ALU.add),
                      reads=[paccb.all(), t1.r(0, W)], writes=[sc.r(1, 1 + W)])
                pg.op("dve", lambda qb=qb: nc.vector.tensor_tensor(out=sc[:, 2 * qb:2 * qb + 3], in0=sc[:, 2 * qb:2 * qb + 3], in1=c.fpat[:, :], op=ALU.add),
                      reads=[sc.r(2 * qb, 2 * qb + 3), c.fpat.all()], writes=[sc.r(2 * qb, 2 * qb + 3)])
                pg.op("dve", lambda: nc.vector.tensor_tensor(out=sc[:, 1:33], in0=sc[:, 1:33], in1=c.exm[:, :], op=ALU.mult),
                      reads=[sc.r(1, 33), c.exm.all()], writes=[sc.r(1, 33)])
                pg.op("dve", lambda: nc.vector.tensor_tensor(out=sc[:, 1:33], in0=sc[:, 1:33], in1=c.fx[:, :], op=ALU.add),
                      reads=[sc.r(1, 33), c.fx.all()], writes=[sc.r(1, 33)])
                m8a, m8ar = SM(16, 24)
                m8b, m8br = SM(24, 32)
                pg.op("dve", lambda Wc=Wc, m8a=m8a: nc.vector.max(out=m8a, in_=sc[:, 1:1 + Wc]), reads=[sc.r(1, 1 + Wc)], writes=[m8ar])
                pg.op("dve", lambda Wc=Wc, m8a=m8a: nc.vector.match_replace(out=sc2[:, 1:1 + Wc], in_to_replace=m8a, in_values=sc[:, 1:1 + Wc], imm_value=-2.0),
                      reads=[sc.r(1, 1 + Wc), m8ar], writes=[sc2.r(1, 1 + Wc)])
                pg.op("dve", lambda Wc=Wc, m8b=m8b: nc.vector.max(out=m8b, in_=sc2[:, 1:1 + Wc]), reads=[sc2.r(1, 1 + Wc)], writes=[m8br])
                pg.op("dve", lambda Wc=Wc: nc.vector.tensor_scalar(out=selbf[:, 0:Wc], in0=sc[:, 1:1 + Wc], scalar1=sm[:, 31:32], scalar2=None, op0=ALU.is_ge),
                      reads=[sc.r(1, 1 + Wc), m8br], writes=[selbf.r(0, Wc)])
                nch = (Wc + 31) // 32
                for ch in range(nch):
                    w = min(32, Wc - 32 * ch)
                    tp, tpr = tps_slot()
                    pg.op("pe", lambda tp=tp, ch=ch, w=w: nc.tensor.transpose(tp[0:w, :], selbf[:, 32 * ch:32 * ch + w], c.ident[:, :]),
                          reads=[selbf.r(32 * ch, 32 * ch + w), c.ident.all()], writes=[tpr])
                    evac_copy(selT[0:w, ch, :], tp[0:w, :], [tpr], [selT.r(ch * 128, ch * 128 + 128)])

                def key_step(kmat, kreg, vmat, vreg, delta, obanks, first, last, maskmode, kt, wexcol=None):
                    b = kt % 2
                    pg.mm(bank[b][:, :], kmat, qT[:, 4 * g:4 * g + 4, qc], True, True,
                          reads=[kreg, qT.r(4 * g * 512, (4 * g + 4) * 512)], writes=[pg.psr(b)])
                    pt = PT[pt_n[0] % 3]
                    pt_n[0] += 1
                    for hl in range(4):
                        hh = 4 * g + hl
                        pg.op("act", lambda hl=hl, hh=hh, pt=pt, b=b: nc.scalar.activation(out=pt[:, hl, :], in_=bank[b][:, hl * 128:(hl + 1) * 128],
                                                                                          func=AF.Exp, bias=c.alibi[:, delta, hh:hh + 1], scale=1.0),
                              reads=[pg.psr(b, hl * 128, (hl + 1) * 128), c.alibi.all()], writes=[pt.r(hl * 128, (hl + 1) * 128)])
                    if maskmode in ("tril", "wfirst"):
                        cm = c.tril if maskmode == "tril" else c.wfirst
                        if wexcol is None:
                            pg.op("dve", lambda pt=pt, cm=cm: nc.vector.tensor_tensor(out=pt[:, :, :], in0=pt[:, :, :],
                                                                                      in1=cm[:, :].unsqueeze(1).to_broadcast([P, 4, 128]), op=ALU.mult),
                                  reads=[pt.all(), cm.all()], writes=[pt.all()])
                        else:
                            for hl in range(4):
                                pg.op("dve", lambda pt=pt, cm=cm, hl=hl: nc.vector.scalar_tensor_tensor(out=pt[:, hl, :], in0=pt[:, hl, :], scalar=c.wex[:, wexcol:wexcol + 1],
                                                                                                        in1=cm[:, :], op0=ALU.mult, op1=ALU.mult),
                                      reads=[pt.r(hl * 128, hl * 128 + 128), cm.all(), c.wex.all()], writes=[pt.r(hl * 128, hl * 128 + 128)])
                    elif maskmode == "sel":
                        ch = kt // 16
                        w = min(32, Wc - 32 * ch)
                        mk, mkr = msk_slot()
                        pg.mm(mk, c.wide[0:w, 128 * (kt % 16):128 * (kt % 16) + 128], selT[0:w, ch, :], True, True,
                              reads=[c.wide.all(), selT.r(ch * 128, ch * 128 + 128)], writes=[mkr])
                        pg.op("dve", lambda pt=pt, mk=mk: nc.vector.tensor_tensor(out=pt[:, :, :], in0=pt[:, :, :],
                                                                                  in1=mk.unsqueeze(1).to_broadcast([P, 4, 128]), op=ALU.mult),
                              reads=[pt.all(), mkr], writes=[pt.all()])
                    elif maskmode == "wex":
                        pg.op("dve", lambda pt=pt: nc.vector.tensor_scalar(out=pt[:, :, :], in0=pt[:, :, :], scalar1=c.wex[:, wexcol:wexcol + 1], scalar2=None, op0=ALU.mult),
                              reads=[pt.all(), c.wex.all()], writes=[pt.all()])
                    def _pv(pt=pt, vmat=vmat, vreg=vreg, obanks=obanks, first=first, last=last):
                        for hl in range(4):
                            ob = obanks[hl // 2]
                            col = (hl % 2) * 129
                            pg.mm(bank[ob][:, col:col + 129], pt[:, hl, :], vmat, first, last,
                                  reads=[pt.r(hl * 128, hl * 128 + 128), vreg], writes=[pg.psr(ob, col, col + 129)])
                    if pend[0] is not None:
                        pend[0]()
                    pend[0] = _pv

                if _stage < 3:
                    continue
                pend = [None]
                for kt in range(qb + 1):
                    if kt % 8 == 0:
                        cch = kt // 8
                        kb = kvn[0] % 2
                        kvn[0] += 1
                        ks_, vs_ = ksT[kb], vsb[kb]
                        pg.dma("sp", ks_[:, :], scr_f[4 + g, :, 1024 * cch:1024 * cch + 1024], f"ks{kb}",
                               reads=[("scr_f", 0, S)], writes=[ks_.all()])
                        pg.dma("sp", vs_[:, :, 0:128], scr_t[1024 * cch:1024 * cch + 1024, g, :].rearrange("(kt p) d -> p kt d", p=128), f"vs{kb}",
                               reads=[("scr_t", 0, S)], writes=[vs_.all()])
                    key_step(ks_[:, (kt % 8) * 128:(kt % 8 + 1) * 128], ks_.all(), vs_[:, kt % 8, :], vs_.all(), qb - kt, (2, 3),
                             kt == 0, kt == qb, "tril" if kt == qb else "sel", kt)
                if _stage < 4:
                    continue
                if pend[0] is not None:
                    pend[0]()
                    pend[0] = None
                for kt in range(qb - 4, qb + 1):
                    lw = kt - (qb0 - 4)
                    mode = "wfirst" if kt == qb - 4 else ("tril" if kt == qb else None)
                    wexcol = kt if kt < 12 else None
                    if mode is None and wexcol is not None:
                        mode = "wex"
                    key_step(kwT[:, g, lw * 128:(lw + 1) * 128], kwT.all(), vwb[:, g, lw, :], vwb.all(), qb - kt, (6, 7),
                             kt == qb - 4, kt == qb, mode, kt, wexcol)
                if _stage < 5:
                    continue
                if pend[0] is not None:
                    pend[0]()
                    pend[0] = None
                lsw, lswr = SM(32, 40)
                for k2, ob in enumerate((2, 3, 6, 7)):
                    pg.op("dve", lambda k2=k2, ob=ob: nc.vector.tensor_copy(out=sm[:, 32 + 2 * k2:34 + 2 * k2], in_=bank[ob][:, 128:258:129]),
                          reads=[pg.psr(ob, 0, 258)], writes=[lswr])
                pg.op("dve", lambda: nc.vector.tensor_scalar(out=sm[:, 32:40], in0=sm[:, 32:40], scalar1=1.0e-30, scalar2=None, op0=ALU.max),
                      reads=[lswr], writes=[lswr])
                pg.op("dve", lambda: nc.vector.reciprocal(out=sm[:, 32:40], in_=sm[:, 32:40]), reads=[lswr], writes=[lswr])
                gview = gates[:, jq, 12 * g:12 * g + 12].rearrange("p (h b) -> p h b", b=3)
                cf, cfr = SM(40, 48)
                pg.op("dve", lambda gview=gview: nc.vector.tensor_tensor(out=sm[:, 40:44], in0=sm[:, 32:36], in1=gview[:, :, 1], op=ALU.mult),
                      reads=[lswr, gates.all()], writes=[cfr])
                pg.op("dve", lambda gview=gview: nc.vector.tensor_tensor(out=sm[:, 44:48], in0=sm[:, 36:40], in1=gview[:, :, 2], op=ALU.mult),
                      reads=[lswr, gates.all()], writes=[cfr])
                for hl in range(4):
                    yr = ybt.r(hl * 128, hl * 128 + 128)
                    pg.op("dve", lambda hl=hl, gview=gview: nc.vector.tensor_scalar(out=ybt[:, hl, :], in0=bank[5][:, hl * 128:(hl + 1) * 128],
                                                                                    scalar1=gview[:, hl, 0:1], scalar2=None, op0=ALU.mult),
                          reads=[pg.psr(5, hl * 128, hl * 128 + 128), gates.all()], writes=[yr])
                    ob = 2 + hl // 2
                    col = (hl % 2) * 129
                    pg.op("dve", lambda hl=hl, ob=ob, col=col: nc.vector.scalar_tensor_tensor(out=ybt[:, hl, :], in0=bank[ob][:, col:col + 128], scalar=sm[:, 40 + hl:41 + hl],
                                                                                              in1=ybt[:, hl, :], op0=ALU.mult, op1=ALU.add),
                          reads=[pg.psr(ob, col, col + 128), cfr, yr], writes=[yr])
                    ob = 6 + hl // 2
                    pg.op("dve", lambda hl=hl, ob=ob, col=col: nc.vector.scalar_tensor_tensor(out=ybtb[:, hl, :], in0=bank[ob][:, col:col + 128], scalar=sm[:, 44 + hl:45 + hl],
                                                                                              in1=ybt[:, hl, :], op0=ALU.mult, op1=ALU.add),
                          reads=[pg.psr(ob, col, col + 128), cfr, yr], writes=[ybtb.r(hl * 128, hl * 128 + 128)])
                    tp, tpr = tps_slot()
                    pg.op("pe", lambda tp=tp, hl=hl: nc.tensor.transpose(tp[:, :], ybtb[:, hl, :], c.ident[:, :]),
                          reads=[ybtb.r(hl * 128, hl * 128 + 128), c.ident.all()], writes=[tpr])
                    hh = 4 * g + hl
                    evac_copy(ybT[:, hh, qc], tp[:, :], [tpr], [ybT.r(hh * 512 + jq * 128, hh * 512 + jq * 128 + 128)])
        if debug == "y":
            if is_halo:
                continue
            pg.dma("sp", dbg_y[:, 0:8, i * 512:(i + 1) * 512], uT.t[:], "dbg", reads=[uT.all()], writes=[])
            pg.dma("sp", dbg_y[:, 8:16, i * 512:(i + 1) * 512], ybT.t[:], "dbg", reads=[ybT.all()], writes=[])
            pg.dma("sp", dbg_g[:, i * 96:(i + 1) * 96], gates.t[:].rearrange("p a b -> p (a b)"), "dbg", reads=[gates.all()], writes=[])
            continue
        for gi in range(6):
            sl = wload(ws, wout_d, gi, 6144)
            for j in range(3):
                fcn = 3 * gi + j
                if fcn >= 16:
                    break
                b = fcn % 2
                for kc in range(KC):
                    rhs = uT[:, kc, 0:NT] if kc < 8 else ybT[:, kc - 8, 0:NT]
                    pg.mm(bank[b][:, 0:NT], sl[:, j * 2048 + kc * 128:j * 2048 + (kc + 1) * 128], rhs, kc == 0, kc == KC - 1,
                          reads=[sl.r(j * 2048, (j + 1) * 2048), uT.all(), ybT.all()], writes=[pg.psr(b)])
                pg.op("dve", lambda b=b, fcn=fcn: nc.vector.tensor_tensor(out=h[:, fcn, 0:NT], in0=bank[b][:, 0:NT], in1=h[:, fcn, 0:NT], op=ALU.add),
                      reads=[pg.psr(b), h.r(fcn * 512, fcn * 512 + 512)], writes=[h.r(fcn * 512, fcn * 512 + 512)])
        if debug != "h1":
            emit_norm(pg, c, h, hn, sq, g_ffn, NT, 7)
            emit_ffn(pg, c, ws, h, hn, aT, wg_d, wu_d, wd_d, NT)
        if not fused:
            pg.dma("sp", out_d[:, :, i * 512:(i + 1) * 512], h.t[:], "out", reads=[h.all()], writes=[])
        elif is_halo:
            if i == 0:
                pg.op("dve", lambda: nc.vector.tensor_scalar(out=hhalo[:, :, :], in0=h[:, :, NT - 2:NT], scalar1=c.hex[:, 0:1], scalar2=None, op0=ALU.mult),
                      reads=[h.all(), c.tabs.all()], writes=[hhalo.all()])
            else:
                pg.op("dve", lambda: nc.vector.tensor_copy(out=hhalo[:, :, :], in_=h[:, :, NT - 2:NT]), reads=[h.all()], writes=[hhalo.all()])
        else:
            emit_l1_slot(pg, c, ws, h, hn, l1bufs, g3, cw1, wc_d, woc_d, wg1_d, wu1_d, wd1_d, None, outF_d[:, :, i * 512:(i + 1) * 512])


def emit_l1_slot(pg, c, ws, h, hn, bufs, gains3, cw, wc_d, woc_d, wg_d, wu_d, wd_d, halo_src, out_dst):
    nc = pg.nc
    bank = pg.ps_banks
    sq, yT, zb, tmpz, acc, aT, hh, hnh, outn = bufs
    g_mix, g_ffn, g_f = gains3
    if halo_src is not None:
        pg.dma("sp", hh.t[:], halo_src, "halo", reads=[], writes=[hh.all()])
    emit_norm(pg, c, h, hn, sq, g_mix, 512, 7)
    emit_norm(pg, c, hh, hnh, sq, g_mix, 2, 7, stride=2)
    for j in range(KC):
        sl = wload(ws, wc_d, j, 6144)
        bb = 0 if j % 2 == 0 else 3
        for part in range(3):
            for kc in range(KC):
                pg.mm(bank[bb + part][:, :], sl[:, part * 2048 + kc * 128:part * 2048 + (kc + 1) * 128], hn[:, kc, :], kc == 0, kc == KC - 1,
                      reads=[sl.r(part * 2048, (part + 1) * 2048), hn.all()], writes=[pg.psr(bb + part)])
        for part in (1, 2):
            for kc in range(KC):
                pg.mm(bank[6][:, 2 * (part - 1):2 * part], sl[:, part * 2048 + kc * 128:part * 2048 + (kc + 1) * 128], hnh[:, kc, :], kc == 0, kc == KC - 1,
                      reads=[sl.r(part * 2048, (part + 1) * 2048), hnh.all()], writes=[pg.psr(6, 2 * (part - 1), 2 * part)])
        pg.op("act", lambda bb=bb: nc.scalar.copy(out=tmpz[:, 2:514], in_=bank[bb + 2][:, :]), reads=[pg.psr(bb + 2)], writes=[tmpz.r(2, 514)])
        pg.op("act", lambda: nc.scalar.copy(out=tmpz[:, 0:2], in_=bank[6][:, 2:4]), reads=[pg.psr(6, 2, 4)], writes=[tmpz.r(0, 2)])
        pg.op("dve", lambda bb=bb: nc.vector.tensor_tensor(out=zb[:, 2:514], in0=bank[bb + 1][:, :], in1=tmpz[:, 2:514], op=ALU.mult),
              reads=[pg.psr(bb + 1), tmpz.r(2, 514)], writes=[zb.r(2, 514)])
        pg.op("dve", lambda: nc.vector.tensor_tensor(out=zb[:, 0:2], in0=bank[6][:, 0:2], in1=tmpz[:, 0:2], op=ALU.mult),
              reads=[pg.psr(6, 0, 2), tmpz.r(0, 2)], writes=[zb.r(0, 2)])
        pg.op("dve", lambda j=j: nc.vector.tensor_scalar(out=acc[:, :], in0=zb[:, 2:514], scalar1=cw[:, j, 2:3], scalar2=None, op0=ALU.mult),
              reads=[zb.all(), cw.all()], writes=[acc.all()])
        pg.op("dve", lambda j=j: nc.vector.scalar_tensor_tensor(out=acc[:, :], in0=zb[:, 1:513], scalar=cw[:, j, 1:2], in1=acc[:, :], op0=ALU.mult, op1=ALU.add),
              reads=[zb.all(), cw.all(), acc.all()], writes=[acc.all()])
        pg.op("dve", lambda j=j: nc.vector.scalar_tensor_tensor(out=acc[:, :], in0=zb[:, 0:512], scalar=cw[:, j, 0:1], in1=acc[:, :], op0=ALU.mult, op1=ALU.add),
              reads=[zb.all(), cw.all(), acc.all()], writes=[acc.all()])
        pg.op("dve", lambda j=j, bb=bb: nc.vector.tensor_tensor(out=yT[:, j, :], in0=bank[bb][:, :], in1=acc[:, :], op=ALU.mult),
              reads=[pg.psr(bb), acc.all()], writes=[yT.r(j * 512, j * 512 + 512)])
    for gi in range(6):
        sl = wload(ws, woc_d, gi, 6144)
        for j in range(3):
            fcn = 3 * gi + j
            if fcn >= 16:
                break
            b = fcn % 2
            for kc in range(KC):
                pg.mm(bank[b][:, :], sl[:, j * 2048 + kc * 128:j * 2048 + (kc + 1) * 128], yT[:, kc, :], kc == 0, kc == KC - 1,
                      reads=[sl.r(j * 2048, (j + 1) * 2048), yT.all()], writes=[pg.psr(b)])
            pg.op("dve", lambda b=b, fcn=fcn: nc.vector.tensor_tensor(out=h[:, fcn, :], in0=bank[b][:, :], in1=h[:, fcn, :], op=ALU.add),
                  reads=[pg.psr(b), h.r(fcn * 512, fcn * 512 + 512)], writes=[h.r(fcn * 512, fcn * 512 + 512)])
    emit_norm(pg, c, h, hn, sq, g_ffn, 512, 7)
    emit_ffn(pg, c, ws, h, hn, aT, wg_d, wu_d, wd_d, 512)
    emit_norm(pg, c, h, outn, sq, g_f, 512, 7)
    pg.dma("sp", out_dst, outn.t[:], "out", reads=[outn.all()], writes=[])


class V:
    def __init__(s, base, lo, hi):
        s.base = base; s.lo = lo; s.hi = hi

    def all(s):
        return s.base.r(s.lo, s.hi)

    def __getitem__(s, k):
        return s.base.t[:, s.lo:s.hi][k]


def build_l1(S):
    NS = S // 2048
    pg = Prog()
    nc = pg.nc
    c = Ctx()
    dr = lambda name, shape, dt=F32, kind="ExternalInput": pg.dram(name, shape, dt, kind)
    hin = dr("hin", [P, KC, NS * 512])
    halo = dr("halo", [P, KC, NS * 2])
    gains_d = dr("gains1", [P, 3 * KC])
    cw_d = dr("cw", [P, KC * 3])
    ones_d = dr("ones", [P, 128])
    wc_d = dr("wc", [16, P, 6144]); woc_d = dr("woc", [6, P, 6144])
    wg_d = dr("wg1", [15, P, 6144]); wu_d = dr("wu1", [15, P, 6144]); wd_d = dr("wd1", [16, P, FC * 128])
    out_d = dr("outF", [P, KC, NS * 512], F32, "ExternalOutput")
    c.ones_bf = pg.sb("ones", [P, 128], BF16)
    c.rstd = pg.sb("rstd", [P, 512], F32)
    c.silu = [pg.sb(f"silu{i}", [P, 512], F32) for i in range(2)]
    gains = pg.sb("gains1", [P, 3 * KC], F32)
    cw = pg.sb("cw", [P, KC, 3], F32)
    h = pg.sb("h", [P, KC, 512], F32)
    hn = pg.sb("hn", [P, KC, 512], BF16)
    ws = WStream(pg, 4, 6144)
    sq = pg.sb("sq", [P, KC, 512], BF16)
    yT = pg.sb("yT", [P, KC, 512], BF16)
    zb = pg.sb("zb", [P, 514], F32); tmpz = pg.sb("tmpz", [P, 514], F32); acc = pg.sb("acc", [P, 512], F32)
    aT = pg.sb("aT", [P, FC, 512], BF16)
    hh = pg.sb("hh", [P, KC, 2], F32); hnh = pg.sb("hnh", [P, KC, 2], BF16)
    outn = pg.sb("outn", [P, KC, 512], F32)
    pg.dma("pool", c.ones_bf.t[:], ones_d[:, :], "c0", reads=[], writes=[c.ones_bf.all()])
    pg.dma("sp", gains.t[:], gains_d[:, :], "c1", reads=[], writes=[gains.all()])
    pg.dma("sp", cw.t[:].rearrange("p a b -> p (a b)"), cw_d[:, :], "c2", reads=[], writes=[cw.all()])
    g3 = (V(gains, 0, KC), V(gains, KC, 2 * KC), V(gains, 2 * KC, 3 * KC))
    bufs = (sq, yT, zb, tmpz, acc, aT, hh, hnh, outn)
    for i in range(NS):
        pg.dma("sp", h.t[:], hin[:, :, i * 512:(i + 1) * 512], "x", reads=[], writes=[h.all()])
        emit_l1_slot(pg, c, ws, h, hn, bufs, g3, cw, wc_d, woc_d, wg_d, wu_d, wd_d, halo[:, :, 2 * i:2 * i + 2], out_d[:, :, i * 512:(i + 1) * 512])
    pg.finish("sp")
    return pg


def _kcl(wm):
    n = wm.shape[1]
    return np.ascontiguousarray(wm.reshape(KC, P, n).transpose(1, 0, 2)).reshape(P, KC * n)


def _grp3(wm, ng):
    nf = wm.shape[1] // 128
    out = np.zeros((ng, P, 3, KC, 128), np.float32)
    w4 = wm.reshape(KC, P, nf, 128)
    for fc in range(nf):
        out[fc // 3, :, fc % 3] = w4[:, :, fc, :].transpose(1, 0, 2)
    return out.reshape(ng, P, 3 * 2048)


def host_consts(S):
    NKT = S // 128
    p = np.arange(P)
    cst = np.zeros((P, 512 + 2048), np.float32)
    cst[:, 0:128] = 1.0
    cst[:, 128:256] = np.eye(P)
    cst[:, 256:384] = (p[:, None] <= p[None, :])
    cst[:, 384:512] = (p[:, None] > p[None, :])
    xw = np.arange(2048)
    cst[:32, 512:] = (xw[None, :] // 64 == np.arange(32)[:, None])
    cstf = np.zeros((P, 1035), np.float32)
    cstf[:, 0:1024] = (16.0 * np.arange(1024) + 15.5)[None, :]
    cc = np.arange(8)
    cstf[:, 1024:1032] = np.where(p[:, None] >= 16 * cc[None, :] + 15, 0.0, NEG)
    fp = np.zeros((P, 3), np.float32)
    fp[:64] = [1e9, 1e9, -1.0]
    fp[64:] = [0.0, 1e9, 1e9]
    cstf[:, 1032:1035] = fp
    sl = np.array(slopes(), np.float32)
    dl = np.arange(NKT)
    alibi = -(sl[None, None, :] * (127.0 - p[:, None, None] + 128.0 * dl[None, :, None]))
    return cst, cstf, np.ascontiguousarray(alibi.reshape(P, NKT * 8).astype(np.float32))


def host_tabs(r):
    tabs = np.zeros((P, 173), np.float32)
    tabs[:, 172] = 0.0 if r == 0 else 1.0
    n = np.arange(96)
    tabs[:, 0:96] = np.where(n < 32 * (3 - r), NEG, 0.0)[None, :]
    j0 = 8 * (3 - r)
    j = np.arange(32)
    fx = np.where(j < j0, -3.0, np.where(j == j0, 1e9, 0.0))
    tabs[:, 96:128] = fx[None, :]
    kt = np.arange(12)
    tabs[:, 128:140] = (kt >= 4 * (3 - r)).astype(np.float32)[None, :]
    tabs[:, 140:172] = (j >= j0).astype(np.float32)[None, :]
    return tabs


def host_prep_l0(inp, S):
    f = lambda a: np.asarray(a, np.float32)
    w = f(inp["w_in_ab"])[0]
    sh = {}
    sh["wkvf"] = _kcl(np.concatenate([w[:, 3072:3328], w[:, 3328:3584], w[:, 3584:3840], w[:, 4096:4352]], 1))
    sh["wkvt"] = _kcl(np.concatenate([w[:, 3840:4096], w[:, 4352:4608]], 1))
    sh["w1k"] = np.ascontiguousarray(f(inp["cmp_w1_k"])[0].transpose(1, 0, 2)).reshape(P, 32 * 128)
    sh["w1v"] = np.ascontiguousarray(f(inp["cmp_w1_v"])[0].transpose(1, 0, 2)).reshape(P, 32 * 128)
    sh["w2k"] = np.ascontiguousarray(f(inp["cmp_w2_k"])[0])
    sh["w2v"] = np.ascontiguousarray(f(inp["cmp_w2_v"])[0])
    sh["pek"] = np.ascontiguousarray(f(inp["cmp_pe_k"])[0].T)
    sh["pev"] = np.ascontiguousarray(f(inp["cmp_pe_v"])[0].T)
    sh["gains"] = np.ascontiguousarray(np.concatenate([f(inp["norm_mix"])[0].reshape(KC, P).T, f(inp["norm_ffn"])[0].reshape(KC, P).T], 1))
    uq = [w[:, 256 * i:256 * (i + 1)] for i in range(4)] + [w[:, 2048 + 256 * i:2048 + 256 * (i + 1)] for i in range(4)]
    sh["wuq"] = np.stack([_kcl(a) for a in uq])
    sh["wv"] = np.stack([_kcl(w[:, 1024 + 256 * i:1024 + 256 * (i + 1)]) for i in range(4)])
    sh["wgt"] = _kcl(w[:, 4608:4632])
    sh["wout"] = _grp3(f(inp["w_out_ab"])[0], 6)
    sh["wg"] = _grp3(f(inp["w_gate"])[0], 15)
    sh["wu"] = _grp3(f(inp["w_up"])[0], 15)
    wd = f(inp["w_down"])[0]
    sh["wd"] = np.ascontiguousarray(wd.reshape(FC, P, KC, 128).transpose(2, 1, 0, 3)).reshape(KC, P, FC * 128)
    sh["wcT"] = np.ascontiguousarray(f(inp["sgu_w"])[0].transpose(2, 0, 1)).reshape(P, 1024)
    sh["sgug"] = np.ascontiguousarray(np.broadcast_to(f(inp["sgu_g"])[0].reshape(1, 1024), (P, 1024)))
    sh["sgub"] = np.ascontiguousarray(np.broadcast_to(f(inp["sgu_b"])[0].reshape(1, 1024), (P, 1024)))
    cst, cstf, alibi = host_consts(S)
    sh["cst"] = cst; sh["cstf"] = cstf; sh["alibi"] = alibi
    x = f(inp["x"])
    maps = []
    for c in range(8):
        b, r = c // 4, c % 4
        pad = 512 * (3 - r)
        xs = np.zeros((S, D), np.float32)
        xs[pad:] = x[b, :S - pad]
        m = dict(sh)
        m["xT"] = np.ascontiguousarray(xs.reshape(S, KC, P).transpose(2, 1, 0))
        m["tabs"] = host_tabs(r)
        maps.append(m)
    return maps


def host_prep_l1(inp, S, l0_outs):
    f = lambda a: np.asarray(a, np.float32)
    NS = S // 2048
    sh = {}
    sh["gains1"] = np.ascontiguousarray(np.concatenate([f(inp["norm_mix"])[1].reshape(KC, P).T, f(inp["norm_ffn"])[1].reshape(KC, P).T,
                                                        f(inp["norm_f"]).reshape(KC, P).T], 1))
    sh["cw"] = np.ascontiguousarray(f(inp["conv_w"])[0].reshape(3, KC, P).transpose(2, 1, 0)).reshape(P, KC * 3)
    sh["ones"] = np.ones((P, 128), np.float32)
    wc = f(inp["w_in_c"])[0]
    w4 = wc.reshape(KC, P, 3, KC, 128)
    sh["wc"] = np.ascontiguousarray(w4.transpose(3, 1, 2, 0, 4)).reshape(KC, P, 6144)
    sh["woc"] = _grp3(f(inp["w_out_c"])[0], 6)
    sh["wg1"] = _grp3(f(inp["w_gate"])[1], 15)
    sh["wu1"] = _grp3(f(inp["w_up"])[1], 15)
    wd = f(inp["w_down"])[1]
    sh["wd1"] = np.ascontiguousarray(wd.reshape(FC, P, KC, 128).transpose(2, 1, 0, 3)).reshape(KC, P, FC * 128)
    maps = []
    if l0_outs is None:
        return [dict(sh) for _ in range(8)]
    for c in range(8):
        b, r = c // 4, c % 4
        m = dict(sh)
        m["hin"] = np.ascontiguousarray(l0_outs[c])
        halo = np.zeros((P, KC, NS * 2), np.float32)
        for i in range(NS):
            if r > 0:
                src, si = l0_outs[b * 4 + r - 1], i
            elif i > 0:
                src, si = l0_outs[b * 4 + 3], i - 1
            else:
                continue
            halo[:, :, 2 * i:2 * i + 2] = src[:, :, si * 512 + 510:si * 512 + 512]
        m["halo"] = halo
        maps.append(m)
    return maps


_CACHE = {}


def run_fused(inp, S):
    NS = S // 2048
    if ("f", S) not in _CACHE:
        _CACHE[("f", S)] = build_l0(S, with_l1=True)
    pg = _CACHE[("f", S)]
    maps0 = host_prep_l0(inp, S)
    maps1 = host_prep_l1(inp, S, None)
    maps = []
    for a, b in zip(maps0, maps1):
        m = dict(a); m.update(b); maps.append(m)
    maps = pg.filter_inputs(maps)
    res = run_bass_kernel_spmd(pg.nc, maps, core_ids=list(range(8)))
    out = np.zeros((2, S, D), np.float32)
    for c in range(8):
        b, r = c // 4, c % 4
        y = np.asarray(res.results[c]["outF"])
        for i in range(NS):
            t0 = 512 * (4 * i + r)
            out[b, t0:t0 + 512] = y[:, :, i * 512:(i + 1) * 512].transpose(2, 1, 0).reshape(512, D)
    return out


def run_model(inp, S):
    NS = S // 2048
    if ("l0", S) not in _CACHE:
        _CACHE[("l0", S)] = build_l0(S)
        _CACHE[("l1", S)] = build_l1(S)
    pg0 = _CACHE[("l0", S)]
    pg1 = _CACHE[("l1", S)]
    maps0 = pg0.filter_inputs(host_prep_l0(inp, S))
    res0 = run_bass_kernel_spmd(pg0.nc, maps0, core_ids=list(range(8)))
    l0_outs = [np.asarray(res0.results[c]["outT"]) for c in range(8)]
    del maps0
    maps1 = pg1.filter_inputs(host_prep_l1(inp, S, l0_outs))
    res1 = run_bass_kernel_spmd(pg1.nc, maps1, core_ids=list(range(8)))
    out = np.zeros((2, S, D), np.float32)
    for c in range(8):
        b, r = c // 4, c % 4
        y = np.asarray(res1.results[c]["outF"])
        for i in range(NS):
            t0 = 512 * (4 * i + r)
            out[b, t0:t0 + 512] = y[:, :, i * 512:(i + 1) * 512].transpose(2, 1, 0).reshape(512, D)
    return out


def kernel(**inputs):
    S = int(np.asarray(inputs["x"]).shape[1])
    return run_fused(inputs, S)
```
